# Optimizing a Trainium2 kernel written in Bass

```python
import math
import jax, jax.numpy as jnp
from jax import lax
import numpy as np

D_MODEL = 2048
BATCH = 16
SEQ = 256
DEPTH = 4
DEC_BATCH = 8
DEC_SEQ = 4096
PAST_LEN = 256

GRID_W = 64
N_MIXERS = 4
N_A = (DEPTH + 3) // 4
N_B = (DEPTH + 2) // 4
N_C = (DEPTH + 1) // 4
N_D = DEPTH // 4

NA_HEADS = 16
NA_HEAD_DIM = 128
NA_WIN_ROWS = 8
NA_WIN_COLS = 16
NA_KEY_COLS = 32

LRU_WIDTH = 2688
LRU_BLOCKS = 16
LRU_BLOCK = LRU_WIDTH // LRU_BLOCKS
LRU_CONV = 4
LRU_C = 8.0

MLA_HEADS = 16
MLA_Q_RANK = 768
MLA_KV_RANK = 512
MLA_NOPE = 128
MLA_ROPE = 64
MLA_QK = MLA_NOPE + MLA_ROPE
MLA_V = 128

SWA_HEADS = 32
SWA_KV_HEADS = 4
SWA_HEAD_DIM = 64
SWA_WINDOW = 128
SWA_BLOCK = 128

D_FF = 5632
FFN_CONV = 3

Q_BLOCK = 128
ROPE_BASE = 10000.0
EPS = 1e-6
NEG = -1e30

kernel_name = 'hybrid_diffusion_trunk_step'


def rms_norm(x, g):
    xf = x.astype(jnp.float32)
    y = xf * lax.rsqrt(jnp.mean(xf * xf, axis=-1, keepdims=True) + EPS)
    return (y * g.astype(jnp.float32)).astype(x.dtype)


def modulation(cond, w_mod, b_mod):
    m = jax.nn.silu(cond) @ w_mod + b_mod
    return jnp.split(m[:, None, :], 6, axis=-1)


def modulate(x, shift, scale):
    return x * (1 + scale) + shift


def dwconv_centred(x, w, b):
    K = w.shape[0]
    S = x.shape[1]
    left = K // 2
    xp = jnp.pad(x, ((0, 0), (left, K - 1 - left), (0, 0)))
    out = b
    for k in range(K):
        out = out + xp[:, k:k + S] * w[k]
    return out


def axial_rope_tables(n_tokens, rot_dim):
    t = jnp.arange(n_tokens)
    row = (t // GRID_W).astype(jnp.float32)
    col = (t % GRID_W).astype(jnp.float32)
    half = rot_dim // 2
    inv = ROPE_BASE ** (-jnp.arange(0, half, 2, dtype=jnp.float32) / half)
    ar = row[:, None] * inv
    ac = col[:, None] * inv
    ang = jnp.concatenate([ar, ar, ac, ac], axis=-1)
    return jnp.cos(ang), jnp.sin(ang)


def rotate_half(z):
    z1, z2 = jnp.split(z, 2, axis=-1)
    return jnp.concatenate([-z2, z1], axis=-1)


def apply_axial_rope(x, cos, sin):
    xf = x.astype(jnp.float32)
    xr, xc = jnp.split(xf, 2, axis=-1)
    rot = jnp.concatenate([rotate_half(xr), rotate_half(xc)], axis=-1)
    return (xf * cos + rot * sin).astype(x.dtype)


def rope_tail(x, cos, sin):
    n = cos.shape[-1]
    return jnp.concatenate([x[..., :-n], apply_axial_rope(x[..., -n:], cos, sin)], axis=-1)


def attn_probs(s, sink=None):
    s = s.astype(jnp.float32)
    m = jnp.max(s, axis=-1, keepdims=True)
    if sink is not None:
        m = jnp.maximum(m, sink)
    e = jnp.exp(s - m)
    den = jnp.sum(e, axis=-1, keepdims=True)
    if sink is not None:
        den = den + jnp.exp(sink - m)
    return e / den


def joint_probs(s_a, s_b, sink=None):
    p = attn_probs(jnp.concatenate([s_a.astype(jnp.float32), s_b.astype(jnp.float32)], axis=-1), sink)
    n = s_a.shape[-1]
    return p[..., :n], p[..., n:]


def dense_attention(q, k, v):
    s = jnp.einsum('bhqd,bhkd->bhqk', q, k).astype(jnp.float32) * (q.shape[-1] ** -0.5)
    p = attn_probs(s)
    return jnp.einsum('bhqk,bhkd->bhqd', p.astype(v.dtype), v)


def merge_heads(o, w_o):
    B, H, S, Dh = o.shape
    return o.transpose(0, 2, 1, 3).reshape(B, S, H * Dh) @ w_o


def natten_qkv(h, w_qkv, g_q, g_k):
    B, S, _ = h.shape
    qkv = (h @ w_qkv).reshape(B, S, 3, NA_HEADS, NA_HEAD_DIM)
    q = rms_norm(qkv[:, :, 0], g_q).transpose(0, 2, 1, 3)
    k = rms_norm(qkv[:, :, 1], g_k).transpose(0, 2, 1, 3)
    v = qkv[:, :, 2].transpose(0, 2, 1, 3)
    return q, k, v


def natten_context(h, w_qkv, g_q, g_k, w_o):
    q, k, v = natten_qkv(h, w_qkv, g_q, g_k)
    return merge_heads(dense_attention(q, k, v), w_o), k, v


def natten_latent(h, k_ctx, v_ctx, w_qkv, g_q, g_k, rpb, w_o):
    B, N, _ = h.shape
    H, Dh = NA_HEADS, NA_HEAD_DIM
    rows = N // GRID_W
    wr = min(NA_WIN_ROWS, rows)
    ncb = GRID_W // NA_WIN_COLS
    scale = Dh ** -0.5
    q, k, v = natten_qkv(h, w_qkv, g_q, g_k)
    qg = q.reshape(B, H, rows, GRID_W, Dh)
    kg = k.reshape(B, H, rows, GRID_W, Dh)
    vg = v.reshape(B, H, rows, GRID_W, Dh)
    qcol = np.arange(GRID_W).reshape(ncb, NA_WIN_COLS)
    kcol = np.clip(qcol[:, :1] - NA_WIN_COLS // 2, 0, GRID_W - NA_KEY_COLS) + np.arange(NA_KEY_COLS)
    cstart = np.clip(qcol - NA_WIN_COLS // 2, 0, GRID_W - NA_WIN_COLS)
    kc = kcol[:, None, :]
    col_mask = (kc >= cstart[:, :, None]) & (kc < cstart[:, :, None] + NA_WIN_COLS)
    dc_idx = np.clip(kc - qcol[:, :, None], 1 - NA_WIN_COLS, NA_WIN_COLS - 1) + NA_WIN_COLS - 1
    mask = jnp.asarray(np.broadcast_to(col_mask[:, :, None, :], (ncb, NA_WIN_COLS, wr, NA_KEY_COLS)).reshape(ncb, NA_WIN_COLS, wr * NA_KEY_COLS))

    def row_block(r):
        rs = jnp.clip(r - wr // 2, 0, rows - wr)

        def gather(z):
            zr = lax.dynamic_slice_in_dim(z, rs, wr, axis=2)[:, :, :, kcol]
            return zr.transpose(0, 1, 3, 2, 4, 5).reshape(B, H, ncb, wr * NA_KEY_COLS, Dh)

        kb, vb = gather(kg), gather(vg)
        qr = lax.dynamic_index_in_dim(qg, r, axis=2, keepdims=False).reshape(B, H, ncb, NA_WIN_COLS, Dh)
        dr_idx = rs + jnp.arange(wr) - r + NA_WIN_ROWS - 1
        bias = rpb[:, dr_idx][:, :, dc_idx].transpose(0, 2, 3, 1, 4).reshape(H, ncb, NA_WIN_COLS, wr * NA_KEY_COLS)
        s_loc = jnp.einsum('bhnqd,bhnkd->bhnqk', qr, kb).astype(jnp.float32) * scale + bias.astype(jnp.float32)
        s_loc = jnp.where(mask, s_loc, NEG)
        s_ctx = jnp.einsum('bhnqd,bhkd->bhnqk', qr, k_ctx).astype(jnp.float32) * scale
        p_loc, p_ctx = joint_probs(s_loc, s_ctx)
        o = (jnp.einsum('bhnqk,bhnkd->bhnqd', p_loc.astype(v.dtype), vb)
             + jnp.einsum('bhnqk,bhkd->bhnqd', p_ctx.astype(v.dtype), v_ctx))
        return o.reshape(B, H, GRID_W, Dh)

    o = lax.map(row_block, jnp.arange(rows))
    o = o.transpose(1, 2, 0, 3, 4).reshape(B, H, N, Dh)
    return merge_heads(o, w_o)


def lru_gates(xc, w_a, b_a, w_i, b_i, lam):
    B, S, C = xc.shape
    xb = xc.reshape(B, S, LRU_BLOCKS, LRU_BLOCK)
    r = jax.nn.sigmoid((jnp.einsum('bsnk,nkj->bsnj', xb, w_a).reshape(B, S, C) + b_a).astype(jnp.float32))
    i = jax.nn.sigmoid((jnp.einsum('bsnk,nkj->bsnj', xb, w_i).reshape(B, S, C) + b_i).astype(jnp.float32))
    log_a = -LRU_C * r * jax.nn.softplus(-lam.astype(jnp.float32))
    a = jnp.exp(log_a)
    bx = jnp.sqrt(-jnp.expm1(2.0 * log_a)) * (i * xc.astype(jnp.float32))
    return a, bx


def lru_scan(a, bx, h0, reverse):
    def step(hc, ab):
        hc = ab[0] * hc + ab[1]
        return hc, hc
    h_last, hs = lax.scan(step, h0, (jnp.swapaxes(a, 0, 1), jnp.swapaxes(bx, 0, 1)), reverse=reverse)
    return jnp.swapaxes(hs, 0, 1), h_last


def rglru_mixer(h, h0, w_in, conv_w, conv_b, w_a, b_a, w_i, b_i, lam, w_out):
    xb, gate = jnp.split(h @ w_in, 2, axis=-1)
    xc = dwconv_centred(xb, conv_w, conv_b)
    h0 = h0.astype(jnp.float32)
    a_f, b_f = lru_gates(xc, w_a[0], b_a[0], w_i[0], b_i[0], lam[0])
    hs_f, hT_f = lru_scan(a_f, b_f, h0[:, 0], False)
    a_b, b_b = lru_gates(xc, w_a[1], b_a[1], w_i[1], b_i[1], lam[1])
    hs_b, hT_b = lru_scan(a_b, b_b, h0[:, 1], True)
    y = (jax.nn.gelu(gate) * (hs_f + hs_b).astype(h.dtype)) @ w_out
    return y, jnp.stack([hT_f, hT_b], axis=1).astype(h.dtype)


def mla_down(h, w_down, g_qa, g_kva):
    d = h @ w_down
    cq = rms_norm(d[..., :MLA_Q_RANK], g_qa)
    ckv = rms_norm(d[..., MLA_Q_RANK:MLA_Q_RANK + MLA_KV_RANK], g_kva)
    k_rope = d[..., MLA_Q_RANK + MLA_KV_RANK:]
    return cq, ckv, k_rope


def mla_queries(cq, w_uq, g_q):
    B, S, _ = cq.shape
    q = (cq @ w_uq).reshape(B, S, MLA_HEADS, MLA_QK)
    return rms_norm(q, g_q).transpose(0, 2, 1, 3)


def mla_keys_values(ckv, k_rope, w_ukv, g_k):
    B, S, _ = ckv.shape
    kv = (ckv @ w_ukv).reshape(B, S, MLA_HEADS, MLA_NOPE + MLA_V)
    kr = jnp.broadcast_to(k_rope[:, :, None, :], (B, S, MLA_HEADS, MLA_ROPE))
    k = rms_norm(jnp.concatenate([kv[..., :MLA_NOPE], kr], axis=-1), g_k).transpose(0, 2, 1, 3)
    v = kv[..., MLA_NOPE:].transpose(0, 2, 1, 3)
    return k, v


def mla_context(h, w_down, g_qa, g_kva, w_uq, w_ukv, g_q, g_k, w_o):
    cq, ckv, kr = mla_down(h, w_down, g_qa, g_kva)
    q = mla_queries(cq, w_uq, g_q)
    k, v = mla_keys_values(ckv, kr, w_ukv, g_k)
    return merge_heads(dense_attention(q, k, v), w_o), ckv, kr


def blocked_joint_attention(q, k, v, k_ctx, v_ctx):
    B, H, N, Dq = q.shape
    nb = N // Q_BLOCK
    scale = Dq ** -0.5
    qb = q.reshape(B, H, nb, Q_BLOCK, Dq).transpose(2, 0, 1, 3, 4)

    def block(qi):
        s_lat = jnp.einsum('bhqd,bhkd->bhqk', qi, k).astype(jnp.float32) * scale
        s_ctx = jnp.einsum('bhqd,bhkd->bhqk', qi, k_ctx).astype(jnp.float32) * scale
        p_lat, p_ctx = joint_probs(s_lat, s_ctx)
        return (jnp.einsum('bhqk,bhkd->bhqd', p_lat.astype(v.dtype), v)
                + jnp.einsum('bhqk,bhkd->bhqd', p_ctx.astype(v.dtype), v_ctx))

    o = lax.map(block, qb)
    return o.transpose(1, 2, 0, 3, 4).reshape(B, H, N, v.shape[-1])


def mla_latent(h, ckv_ctx, kr_ctx, w_down, g_qa, g_kva, w_uq, w_ukv, g_q, g_k, w_o):
    N = h.shape[1]
    cos, sin = axial_rope_tables(N, MLA_ROPE)
    cq, ckv, kr = mla_down(h, w_down, g_qa, g_kva)
    q = rope_tail(mla_queries(cq, w_uq, g_q), cos, sin)
    k, v = mla_keys_values(ckv, kr, w_ukv, g_k)
    k = rope_tail(k, cos, sin)
    kc, vc = mla_keys_values(ckv_ctx, kr_ctx, w_ukv, g_k)
    return merge_heads(blocked_joint_attention(q, k, v, kc, vc), w_o)


def swa_qkv(h, w_qkv, g_q, g_k, cos=None, sin=None):
    B, S, _ = h.shape
    G = SWA_HEADS // SWA_KV_HEADS
    qkv = (h @ w_qkv).reshape(B, S, SWA_HEADS + 2 * SWA_KV_HEADS, SWA_HEAD_DIM)
    q = rms_norm(qkv[:, :, :SWA_HEADS], g_q).reshape(B, S, SWA_KV_HEADS, G, SWA_HEAD_DIM).transpose(0, 2, 3, 1, 4)
    k = rms_norm(qkv[:, :, SWA_HEADS:SWA_HEADS + SWA_KV_HEADS], g_k).transpose(0, 2, 1, 3)
    v = qkv[:, :, SWA_HEADS + SWA_KV_HEADS:].transpose(0, 2, 1, 3)
    if cos is not None:
        q = apply_axial_rope(q, cos, sin)
        k = apply_axial_rope(k, cos, sin)
    return q, k, v


def swa_merge(o, w_o):
    B, KVH, G, S, Dh = o.shape
    return o.transpose(0, 3, 1, 2, 4).reshape(B, S, KVH * G * Dh) @ w_o


def swa_context(h, w_qkv, g_q, g_k, sinks, w_o):
    q, k, v = swa_qkv(h, w_qkv, g_q, g_k)
    sink = sinks.astype(jnp.float32).reshape(SWA_KV_HEADS, -1, 1, 1)
    s = jnp.einsum('bkgqd,bksd->bkgqs', q, k).astype(jnp.float32) * (SWA_HEAD_DIM ** -0.5)
    p = attn_probs(s, sink)
    o = jnp.einsum('bkgqs,bksd->bkgqd', p.astype(v.dtype), v)
    return swa_merge(o, w_o), k, v


def swa_latent(h, k_ctx, v_ctx, w_qkv, g_q, g_k, sinks, w_o):
    B, N, _ = h.shape
    G = SWA_HEADS // SWA_KV_HEADS
    nb = N // SWA_BLOCK
    scale = SWA_HEAD_DIM ** -0.5
    cos, sin = axial_rope_tables(N, SWA_HEAD_DIM)
    q, k, v = swa_qkv(h, w_qkv, g_q, g_k, cos, sin)
    sink = sinks.astype(jnp.float32).reshape(SWA_KV_HEADS, G, 1, 1)
    pad = ((0, 0), (0, 0), (SWA_BLOCK, SWA_BLOCK), (0, 0))
    kp, vp = jnp.pad(k, pad), jnp.pad(v, pad)
    qb = q.reshape(B, SWA_KV_HEADS, G, nb, SWA_BLOCK, SWA_HEAD_DIM).transpose(3, 0, 1, 2, 4, 5)

    def block(args):
        qi, b = args
        kb = lax.dynamic_slice_in_dim(kp, b * SWA_BLOCK, 3 * SWA_BLOCK, axis=2)
        vb = lax.dynamic_slice_in_dim(vp, b * SWA_BLOCK, 3 * SWA_BLOCK, axis=2)
        qpos = b * SWA_BLOCK + jnp.arange(SWA_BLOCK)
        kpos = (b - 1) * SWA_BLOCK + jnp.arange(3 * SWA_BLOCK)
        mask = (jnp.abs(qpos[:, None] - kpos[None, :]) <= SWA_WINDOW) & (kpos[None, :] >= 0) & (kpos[None, :] < N)
        s_loc = jnp.where(mask, jnp.einsum('bkgqd,bksd->bkgqs', qi, kb).astype(jnp.float32) * scale, NEG)
        s_ctx = jnp.einsum('bkgqd,bksd->bkgqs', qi, k_ctx).astype(jnp.float32) * scale
        p_loc, p_ctx = joint_probs(s_loc, s_ctx, sink)
        return (jnp.einsum('bkgqs,bksd->bkgqd', p_loc.astype(v.dtype), vb)
                + jnp.einsum('bkgqs,bksd->bkgqd', p_ctx.astype(v.dtype), v_ctx))

    o = lax.map(block, (qb, jnp.arange(nb)))
    o = o.transpose(1, 2, 3, 0, 4, 5).reshape(B, SWA_KV_HEADS, G, N, SWA_HEAD_DIM)
    return swa_merge(o, w_o)


def conv_ffn(h, w_in, conv_w, conv_b, w_out):
    a, b = jnp.split(h @ w_in, 2, axis=-1)
    a = dwconv_centred(a, conv_w, conv_b)
    return (jax.nn.silu(a) * b) @ w_out


def setup_inputs(seed: int = 0) -> dict:
    key = jax.random.key(seed)
    ks = iter(jax.random.split(key, 64))

    def nrm(shape, scale=1.0):
        return jax.random.normal(next(ks), shape, jnp.float32) * scale

    def gain(shape):
        return 1.0 + nrm(shape, 0.02)

    D = D_MODEL
    u = jax.random.uniform(next(ks), (N_B, 2, LRU_WIDTH), jnp.float32, minval=0.9, maxval=0.999)
    a0 = u ** (1.0 / LRU_C)
    lam = jnp.log(a0) - jnp.log1p(-a0)
    return {
        'x_prompt': nrm((BATCH, SEQ, D)),
        'x_sample': nrm((DEC_BATCH, DEC_SEQ, D)),
        'cache_nat_k': nrm((DEC_BATCH, N_A, NA_HEADS, PAST_LEN, NA_HEAD_DIM)),
        'cache_nat_v': nrm((DEC_BATCH, N_A, NA_HEADS, PAST_LEN, NA_HEAD_DIM)),
        'state_lru': nrm((DEC_BATCH, N_B, 2, LRU_WIDTH), 0.5),
        'cache_mla_ckv': nrm((DEC_BATCH, N_C, PAST_LEN, MLA_KV_RANK)),
        'cache_mla_krope': nrm((DEC_BATCH, N_C, PAST_LEN, MLA_ROPE)),
        'cache_swa_k': nrm((DEC_BATCH, N_D, SWA_KV_HEADS, PAST_LEN, SWA_HEAD_DIM)),
        'cache_swa_v': nrm((DEC_BATCH, N_D, SWA_KV_HEADS, PAST_LEN, SWA_HEAD_DIM)),
        'c': nrm((DEC_BATCH, D)),
        'c_ctx': nrm((D,)),
        'norm_mix': gain((DEPTH, D)),
        'norm_ffn': gain((DEPTH, D)),
        'w_mod': nrm((DEPTH, D, 6 * D), 0.5 * D ** -0.5),
        'b_mod': nrm((DEPTH, 6 * D), 0.02),
        'ffn_w_in': nrm((DEPTH, D, 2 * D_FF), D ** -0.5),
        'ffn_conv_w': nrm((DEPTH, FFN_CONV, D_FF), FFN_CONV ** -0.5),
        'ffn_conv_b': nrm((DEPTH, D_FF), 0.02),
        'ffn_w_out': nrm((DEPTH, D_FF, D), D_FF ** -0.5),
        'nat_w_qkv': nrm((N_A, D, 3 * NA_HEADS * NA_HEAD_DIM), D ** -0.5),
        'nat_q_norm': gain((N_A, NA_HEAD_DIM)),
        'nat_k_norm': gain((N_A, NA_HEAD_DIM)),
        'nat_rpb': nrm((N_A, NA_HEADS, 2 * NA_WIN_ROWS - 1, 2 * NA_WIN_COLS - 1), 0.1),
        'nat_w_o': nrm((N_A, NA_HEADS * NA_HEAD_DIM, D), (NA_HEADS * NA_HEAD_DIM) ** -0.5),
        'lru_w_in': nrm((N_B, D, 2 * LRU_WIDTH), D ** -0.5),
        'lru_conv_w': nrm((N_B, LRU_CONV, LRU_WIDTH), LRU_CONV ** -0.5),
        'lru_conv_b': nrm((N_B, LRU_WIDTH), 0.02),
        'lru_w_a': nrm((N_B, 2, LRU_BLOCKS, LRU_BLOCK, LRU_BLOCK), LRU_BLOCK ** -0.5),
        'lru_b_a': nrm((N_B, 2, LRU_WIDTH), 0.02),
        'lru_w_i': nrm((N_B, 2, LRU_BLOCKS, LRU_BLOCK, LRU_BLOCK), LRU_BLOCK ** -0.5),
        'lru_b_i': nrm((N_B, 2, LRU_WIDTH), 0.02),
        'lru_lambda': lam,
        'lru_w_out': nrm((N_B, LRU_WIDTH, D), LRU_WIDTH ** -0.5),
        'mla_w_down': nrm((N_C, D, MLA_Q_RANK + MLA_KV_RANK + MLA_ROPE), D ** -0.5),
        'mla_q_a_norm': gain((N_C, MLA_Q_RANK)),
        'mla_kv_a_norm': gain((N_C, MLA_KV_RANK)),
        'mla_w_uq': nrm((N_C, MLA_Q_RANK, MLA_HEADS * MLA_QK), MLA_Q_RANK ** -0.5),
        'mla_w_ukv': nrm((N_C, MLA_KV_RANK, MLA_HEADS * (MLA_NOPE + MLA_V)), MLA_KV_RANK ** -0.5),
        'mla_q_norm': gain((N_C, MLA_QK)),
        'mla_k_norm': gain((N_C, MLA_QK)),
        'mla_w_o': nrm((N_C, MLA_HEADS * MLA_V, D), (MLA_HEADS * MLA_V) ** -0.5),
        'swa_w_qkv': nrm((N_D, D, (SWA_HEADS + 2 * SWA_KV_HEADS) * SWA_HEAD_DIM), D ** -0.5),
        'swa_q_norm': gain((N_D, SWA_HEAD_DIM)),
        'swa_k_norm': gain((N_D, SWA_HEAD_DIM)),
        'swa_sinks': nrm((N_D, SWA_HEADS)),
        'swa_w_o': nrm((N_D, SWA_HEADS * SWA_HEAD_DIM, D), (SWA_HEADS * SWA_HEAD_DIM) ** -0.5),
    }


def reference(x_prompt, x_sample, cache_nat_k, cache_nat_v, state_lru, cache_mla_ckv, cache_mla_krope,
              cache_swa_k, cache_swa_v, c, c_ctx, norm_mix, norm_ffn, w_mod, b_mod, ffn_w_in, ffn_conv_w,
              ffn_conv_b, ffn_w_out, nat_w_qkv, nat_q_norm, nat_k_norm, nat_rpb, nat_w_o, lru_w_in, lru_conv_w,
              lru_conv_b, lru_w_a, lru_b_a, lru_w_i, lru_b_i, lru_lambda, lru_w_out, mla_w_down, mla_q_a_norm,
              mla_kv_a_norm, mla_w_uq, mla_w_ukv, mla_q_norm, mla_k_norm, mla_w_o, swa_w_qkv, swa_q_norm,
              swa_k_norm, swa_sinks, swa_w_o):
    xp, xs = x_prompt, x_sample
    nat_k_l, nat_v_l, lru_l, ckv_l, krope_l, swa_k_l, swa_v_l = [], [], [], [], [], [], []
    for l in range(DEPTH):
        kind, j = l % N_MIXERS, l // N_MIXERS
        mp = modulation(c_ctx[None, :], w_mod[l], b_mod[l])
        ms = modulation(c, w_mod[l], b_mod[l])
        hp = modulate(rms_norm(xp, norm_mix[l]), mp[0], mp[1])
        hs = modulate(rms_norm(xs, norm_mix[l]), ms[0], ms[1])
        if kind == 0:
            yp, kc, vc = natten_context(hp, nat_w_qkv[j], nat_q_norm[j], nat_k_norm[j], nat_w_o[j])
            ys = natten_latent(hs, cache_nat_k[:, j], cache_nat_v[:, j], nat_w_qkv[j], nat_q_norm[j],
                               nat_k_norm[j], nat_rpb[j], nat_w_o[j])
            nat_k_l.append(kc)
            nat_v_l.append(vc)
        elif kind == 1:
            lru_args = (lru_w_in[j], lru_conv_w[j], lru_conv_b[j], lru_w_a[j], lru_b_a[j], lru_w_i[j],
                        lru_b_i[j], lru_lambda[j], lru_w_out[j])
            h0 = jnp.zeros((hp.shape[0], 2, LRU_WIDTH), jnp.float32)
            yp, st = rglru_mixer(hp, h0, *lru_args)
            ys, _ = rglru_mixer(hs, state_lru[:, j], *lru_args)
            lru_l.append(st)
        elif kind == 2:
            yp, ckv, kr = mla_context(hp, mla_w_down[j], mla_q_a_norm[j], mla_kv_a_norm[j], mla_w_uq[j],
                                      mla_w_ukv[j], mla_q_norm[j], mla_k_norm[j], mla_w_o[j])
            ys = mla_latent(hs, cache_mla_ckv[:, j], cache_mla_krope[:, j], mla_w_down[j], mla_q_a_norm[j],
                            mla_kv_a_norm[j], mla_w_uq[j], mla_w_ukv[j], mla_q_norm[j], mla_k_norm[j], mla_w_o[j])
            ckv_l.append(ckv)
            krope_l.append(kr)
        else:
            yp, kc, vc = swa_context(hp, swa_w_qkv[j], swa_q_norm[j], swa_k_norm[j], swa_sinks[j], swa_w_o[j])
            ys = swa_latent(hs, cache_swa_k[:, j], cache_swa_v[:, j], swa_w_qkv[j], swa_q_norm[j],
                            swa_k_norm[j], swa_sinks[j], swa_w_o[j])
            swa_k_l.append(kc)
            swa_v_l.append(vc)
        xp = xp + mp[2] * yp
        xs = xs + ms[2] * ys
        hp = modulate(rms_norm(xp, norm_ffn[l]), mp[3], mp[4])
        hs = modulate(rms_norm(xs, norm_ffn[l]), ms[3], ms[4])
        xp = xp + mp[5] * conv_ffn(hp, ffn_w_in[l], ffn_conv_w[l], ffn_conv_b[l], ffn_w_out[l])
        xs = xs + ms[5] * conv_ffn(hs, ffn_w_in[l], ffn_conv_w[l], ffn_conv_b[l], ffn_w_out[l])
    new_nat_k = jnp.stack(nat_k_l, axis=1)
    new_nat_v = jnp.stack(nat_v_l, axis=1)
    new_lru = jnp.stack(lru_l, axis=1)
    new_ckv = jnp.stack(ckv_l, axis=1)
    new_krope = jnp.stack(krope_l, axis=1)
    new_swa_k = jnp.stack(swa_k_l, axis=1)
    new_swa_v = jnp.stack(swa_v_l, axis=1)
    return (xp, xs, new_nat_k, new_nat_v, new_lru, new_ckv, new_krope, new_swa_k, new_swa_v)
```

```python
import numpy as np
from contextlib import ExitStack
import concourse.bass as bass
import concourse.mybir as mybir
from concourse.bass_utils import run_bass_kernel_spmd

F32 = mybir.dt.float32
BF = mybir.dt.bfloat16
AF = mybir.ActivationFunctionType
ALU = mybir.AluOpType
EPS = 1e-6
NEGB = -30000.0
GW = 64
ENGS = ("pe", "act", "dve", "pool", "sp")
WIN = 1 << 30
DWIN = 1 << 26


class Prog:
    def __init__(self, nc, sems):
        self.nc = nc
        self.free = list(sems)
        self.q = {e: [] for e in ENGS}
        self.lastw = {}
        self.rd_c = {}
        self.rd_d = {}
        self.waited = {}
        self.semmap = {}
        self.dcnt = {}
        self.dlast = {}
        self.cc = {e: 0 for e in ENGS}

    def _sem(self, key):
        s = self.semmap.get(key)
        if s is None:
            s = self.free.pop()
            self.semmap[key] = s
        return s

    def _target(self, ref):
        if ref[0] == "c":
            _, e, i = ref
            return self._sem(("c", e, i // WIN)), (i % WIN) + 1
        _, k, c = ref
        return self._sem(("d", k, (c - 1) // DWIN)), (((c - 1) % DWIN) + 1) * 16

    def op(self, eng, fn, r=(), w=(), dma=None):
        deps = set()
        for k in r:
            if k in self.lastw:
                deps.add(self.lastw[k])
        for k in w:
            if k in self.lastw:
                deps.add(self.lastw[k])
            for e, i in self.rd_c.get(k, {}).items():
                deps.add(("c", e, i))
            for d in self.rd_d.get(k, ()):
                deps.add(d)
        if dma is not None and dma in self.dlast:
            deps.add(self.dlast[dma])
        idx = self.cc[eng]
        if dma is None:
            ref = ("c", eng, idx)
            self.cc[eng] += 1
        else:
            c = self.dcnt.get(dma, 0) + 1
            self.dcnt[dma] = c
            ref = ("d", dma, c)
            self.dlast[dma] = ref
        waits = []
        for d in deps:
            if d[0] == "c":
                if d[1] == eng and eng == "pe":
                    continue
                wk = (eng, "c", d[1])
                if self.waited.get(wk, -1) >= d[2]:
                    continue
                self.waited[wk] = d[2]
            else:
                wk = (eng, "d", d[1])
                if self.waited.get(wk, 0) >= d[2]:
                    continue
                self.waited[wk] = d[2]
            waits.append(self._target(d))
        self.q[eng].append((fn, waits, ref, dma is not None))
        for k in r:
            if dma is None:
                self.rd_c.setdefault(k, {})[eng] = idx
            else:
                self.rd_d.setdefault(k, []).append(ref)
        for k in w:
            self.lastw[k] = ref
            self.rd_c[k] = {}
            self.rd_d[k] = []
        return ref

    def barrier(self, final=False):
        refs = []
        for e in ENGS:
            if self.cc[e] > 0:
                refs.append(("c", e, self.cc[e] - 1))
        for k, c in self.dcnt.items():
            refs.append(("d", k, c))
        engs = ("sp",) if final else ENGS
        for e in engs:
            waits = []
            for d in refs:
                if d[0] == "c":
                    if d[1] == e and e == "pe":
                        continue
                    wk = (e, "c", d[1])
                    if self.waited.get(wk, -1) >= d[2]:
                        continue
                    self.waited[wk] = d[2]
                else:
                    wk = (e, "d", d[1])
                    if self.waited.get(wk, 0) >= d[2]:
                        continue
                    self.waited[wk] = d[2]
                waits.append(self._target(d))
            idx = self.cc[e]
            self.cc[e] += 1
            self.q[e].append((lambda en: en.nop(), waits, ("c", e, idx), False))
        if not final:
            self.lastw.clear()
            self.rd_c.clear()
            self.rd_d.clear()

    def emit(self, block):
        decos = {"pe": block.tensor, "act": block.scalar, "dve": block.vector,
                 "pool": block.gpsimd, "sp": block.sync}
        for e in ENGS:
            ops = self.q[e]

            def body(en, ops=ops):
                for fn, waits, ref, isd in ops:
                    for s, v in waits:
                        en.wait_ge(s, v)
                    ins = fn(en)
                    s, v = self._target(ref)
                    ins.then_inc(s, 16 if isd else 1)
            decos[e](body)


class StopBuild(Exception):
    pass


def default_cfg():
    return dict(D=2048, TS=4096, P=256, PAST=256, DFF=5632, KINDS=[0, 1, 2, 3],
                NAH=16, LRUB=16, MLAH=16, QR=768, KVR=512, SWAH=32, SWAKV=4)


def ffn_blocks(L):
    if L <= 512:
        return [(0, L, 0, L)]
    starts = list(range(0, L - 512, 510)) + [L - 512]
    out, cur = [], 0
    for s in starts:
        hi = s + 511 if s + 512 < L else L
        out.append((s, 512, cur, hi))
        cur = hi
    return out


def plain_blocks(L):
    return [(s, min(512, L - s)) for s in range(0, L, 512)]


def build(C):
    nc = bass.Bass("TRN2", target_bir_lowering=False)
    D, TS, PL, PAST, DFF = C["D"], C["TS"], C["P"], C["PAST"], C["DFF"]
    KINDS = C["KINDS"]
    L = len(KINDS)
    KD = D // 128
    NF = DFF // 128
    TA = TS + 2 * PL
    NAH, LRUB, MLAH, QR, KVR, SWAH, SWAKV = (C[k] for k in ("NAH", "LRUB", "MLAH", "QR", "KVR", "SWAH", "SWAKV"))
    LW = LRUB * 168
    NKIND = [KINDS.count(k) for k in range(4)]
    ROWS = TS // GW
    NPT = PAST // 128
    NC5 = min(512, D)

    def din(name, shape, dt=F32):
        return nc.dram_tensor(name, list(shape), dt, kind="ExternalInput").ap()

    def dout(name, shape):
        return nc.dram_tensor(name, list(shape), F32, kind="ExternalOutput").ap()

    def dscr(name, shape, dt):
        return nc.dram_tensor(name, list(shape), dt, kind="Internal").ap()

    I = {}
    I["xs"] = din("xs", [TS, D])
    I["xp"] = din("xp", [2 * PL, D])
    I["condT"] = din("condT", [128, KD, 2])
    I["w_mod"] = din("w_mod", [L, D, 6 * D])
    I["b_mod"] = din("b_mod", [L, 6 * D])
    I["bmodT"] = din("bmodT", [L, 128, 6 * KD])
    I["nmixT"] = din("nmixT", [L, 128, KD])
    I["nffnT"] = din("nffnT", [L, 128, KD])
    I["ffn_w_in"] = din("ffn_w_in", [L, D, 2 * DFF])
    I["ffn_w_out"] = din("ffn_w_out", [L, DFF, D])
    I["fcwT"] = din("fcwT", [L, 128, NF, 3])
    I["fcbT"] = din("fcbT", [L, 128, NF])
    I["ident"] = din("ident", [128, 128])
    I["selc"] = din("selc", [2, 2, 128])
    O = {}
    O["ys"] = dout("ys", [TS, D])
    O["yp"] = dout("yp", [2 * PL, D])
    if NKIND[0]:
        n = NKIND[0]
        I["nat_w_qkv"] = din("nat_w_qkv", [n, D, 3 * NAH * 128])
        I["nat_w_o"] = din("nat_w_o", [n, NAH * 128, D])
        I["nat_gqk"] = din("nat_gqk", [n, 128, 2])
        I["nat_bt"] = din("nat_bt", [n, NAH, 128, 14 * 64])
        I["cnk"] = din("cnk", [n, NAH, PAST, 128])
        I["cnv"] = din("cnv", [n, NAH, PAST, 128])
        O["nat_k"] = dout("nat_k", [2, n, NAH, PL, 128])
        O["nat_v"] = dout("nat_v", [2, n, NAH, PL, 128])
    if NKIND[1]:
        n = NKIND[1]
        NLC = 2 * LRUB
        I["lru_w_in"] = din("lru_w_in", [n, D, 2 * LW])
        I["lru_w_out"] = din("lru_w_out", [n, LW, D])
        I["lru_w_a"] = din("lru_w_a", [n, 2, LRUB, 168, 168])
        I["lru_w_i"] = din("lru_w_i", [n, 2, LRUB, 168, 168])
        I["lru_pp"] = din("lru_pp", [n, 128, NLC, 13])
        O["lru_state"] = dout("lru_state", [2, n, 2, LW])
    if NKIND[2]:
        n = NKIND[2]
        I["mla_w_down"] = din("mla_w_down", [n, D, QR + KVR + 64])
        I["mla_w_uq"] = din("mla_w_uq", [n, QR, MLAH * 192])
        I["mla_w_ukv"] = din("mla_w_ukv", [n, KVR, MLAH * 256])
        I["mla_w_o"] = din("mla_w_o", [n, MLAH * 128, D])
        I["mla_gaT"] = din("mla_gaT", [n, 128, (QR + KVR) // 128])
        I["mla_gqk"] = din("mla_gqk", [n, 128, 4])
        I["cckv"] = din("cckv", [n, PAST, KVR])
        I["ckr"] = din("ckr", [n, PAST, 64])
        I["rope64"] = din("rope64", [64, 2, TS])
        I["perm64"] = din("perm64", [64, 64])
        O["mla_ckv"] = dout("mla_ckv", [2, n, PL, KVR])
        O["mla_krope"] = dout("mla_krope", [2, n, PL, 64])
    if NKIND[3]:
        n = NKIND[3]
        NQK = SWAH + 2 * SWAKV
        I["swa_w_qkv"] = din("swa_w_qkv", [n, D, NQK * 64])
        I["swa_w_o"] = din("swa_w_o", [n, SWAH * 64, D])
        I["swa_gqk"] = din("swa_gqk", [n, 128, 2])
        I["swa_sinks"] = din("swa_sinks", [n, SWAH])
        I["csk"] = din("csk", [n, SWAKV, PAST, 64])
        I["csv"] = din("csv", [n, SWAKV, PAST, 64])
        I["rope128"] = din("rope128", [128, 2, TS])
        I["perm128"] = din("perm128", [128, 128])
        I["swamask"] = din("swamask", [128, 2, 128])
        O["swa_k"] = dout("swa_k", [2, n, SWAKV, PL, 64])
        O["swa_v"] = dout("swa_v", [2, n, SWAKV, PL, 64])

    if C.get("DBGOUT"):
        O["dbg1"] = dout("dbg1", [128, TS])
        O["dbg2"] = dout("dbg2", [128, TS])
        O["dbg3"] = dout("dbg3", [128, TS])
        O["dbg4"] = dout("dbg4", [TS, D])
    xr = [dscr("xr0", [TA, D], F32), dscr("xr1", [TA, D], F32)]
    SC_FM = dscr("sc_fm", [12288, TA + PAST], BF)
    SC_TM = dscr("sc_tm", [TA + PAST, 4096], BF)
    SC_OT = dscr("sc_ot", [4096, TA], BF)

    seqs = [dict(t0=0, L=TS, cond=0, smp=True, pi=None),
            dict(t0=TS, L=PL, cond=1, smp=False, pi=0),
            dict(t0=TS + PL, L=PL, cond=1, smp=False, pi=1)]
    NSUB = 2 * L

    def src_rows(sub, sq, a, b):
        if sub == 0:
            return I["xs"][a:b, :] if sq["smp"] else I["xp"][sq["pi"] * PL + a: sq["pi"] * PL + b, :]
        return xr[sub % 2][sq["t0"] + a: sq["t0"] + b, :]

    def dst_rows(sub, sq, a, b):
        if sub == NSUB - 1:
            return O["ys"][a:b, :] if sq["smp"] else O["yp"][sq["pi"] * PL + a: sq["pi"] * PL + b, :]
        return xr[(sub + 1) % 2][sq["t0"] + a: sq["t0"] + b, :]

    with ExitStack() as es:
        def sb(name, shape, dt):
            return es.enter_context(nc.sbuf_tensor("sb_" + name, list(shape), dt))

        sems = [es.enter_context(nc.semaphore(f"s{i}")) for i in range(100)]
        p = Prog(nc, sems)
        ps = [es.enter_context(nc.psum_tensor(f"ps{i}", [128, 512], F32)) for i in range(8)]
        PK = [("ps", i) for i in range(8)]

        ident32 = sb("ident32", [128, 128], F32)
        identb = sb("identb", [128, 128], BF)
        ones_b = sb("ones_b", [128, 128], BF)
        ones2_b = sb("ones2_b", [128, 128], BF)
        sel = sb("sel", [2, 2, 128], F32)
        epsb = sb("epsb", [128, 1], F32)
        condT = sb("condT", [128, KD, 2], F32)
        scondT = sb("scondT", [128, KD, 2], BF)
        modT = sb("modT", [128, 6 * KD, 2], F32)
        bmodT = sb("bmodT_s", [128, 6 * KD], F32)
        gT = sb("gT", [128, 2, KD], F32)
        gmodT = sb("gmodT", [128, KD, 2], F32)
        shiftT = sb("shiftT", [128, KD, 2], F32)
        gate_bc = sb("gate_bc", [128, 2, D], F32)
        xt = sb("xt", [128, 2, D], F32)
        ss = sb("ss", [128, 8], F32)
        xn = sb("xn", [128, 4, D], BF)
        hT = sb("hT", [128, KD, 512], BF)
        wst = sb("wst", [128, 2, 16, 512], BF)
        xo = sb("xo", [128, 2, 512], F32)
        xo2 = sb("xo2", [128, 2, 512], F32)
        ARENA = sb("arena", [128, 40960], BF)
        ARENA32 = ARENA[:].bitcast(F32)

        wslot = [0]

        def stop_at(k):
            if C.get("STOP") == k:
                raise StopBuild()
        grow = ARENA32[0:2, 0:D]
        brow = ARENA32[0:2, D:2 * D]

        def a16(off, n):
            return ARENA[:, off:off + n]

        def a32(off, n):
            return ARENA32[:, off:off + n]

        def dma(eng, out, in_, r, w, key, **kw):
            p.op(eng, lambda e: e.dma_start(out=out, in_=in_, **kw), r=r, w=w, dma=key)

        def load_w(Wap, r0, nk, c0, ncols, kpart=128):
            s = wslot[0] % 2
            wslot[0] += 1
            key = ("wst", s)
            src = Wap[r0:r0 + nk * kpart, c0:c0 + ncols].rearrange("(k p) c -> p k c", p=kpart)
            dst = wst[0:kpart, s, 0:nk, 0:ncols]
            dma("pool", dst, src, r=(), w=(key,), key=key)
            return (lambda k, a, b: wst[0:kpart, s, k, a:b]), key

        def mm(out, lhsT, rhs, start, stop, r, w):
            p.op("pe", lambda e: e.matmul(out, lhsT=lhsT, rhs=rhs, start=start, stop=stop), r=r, w=w)

        def act(out, in_, func, r, w, **kw):
            p.op("act", lambda e: e.activation(out=out, in_=in_, func=func, **kw), r=r, w=w)

        def dve(fn, r, w):
            p.op("dve", fn, r=r, w=w)

        dma("sp", ident32[:], I["ident"][:, :], (), ("ident32",), "c0")
        dma("pool", identb[:], I["ident"][:, :], (), ("identb",), "c1")
        dma("sp", condT[:], I["condT"][:, :, :], (), ("condT",), "c2")
        p.op("dve", lambda e: e.memset(ones_b[:], 1.0), w=("ones",))
        p.op("dve", lambda e: e.memset(ones2_b[:], 0.0), w=("ones2",))
        p.op("dve", lambda e: e.memset(ones2_b[0:64, 0:64], 1.0), w=("ones2",))
        p.op("dve", lambda e: e.memset(ones2_b[64:128, 64:128], 1.0), w=("ones2",))
        dma("sp", sel[:], I["selc"][:, :, :], (), ("sel",), "c3")
        p.op("dve", lambda e: e.memset(epsb[:], EPS), w=("epsb",))
        act(scondT[:], condT[:], AF.Silu, r=("condT",), w=("scondT",))

        def modulation(l):
            dma("sp", bmodT[:], I["bmodT"][l], (), ("bmodT",), "bm")
            nj = 6 * KD
            for g in range(0, nj, 4):
                wv, wk = load_w(I["w_mod"][l], 0, KD, g * 128, 512)
                for jj in range(4):
                    jk = g + jj
                    bank = jk % 2
                    for k in range(KD):
                        mm(ps[bank][:, 0:2], wv(k, jj * 128, (jj + 1) * 128), scondT[:, k, :], k == 0, k == KD - 1,
                           r=(wk, "scondT"), w=(PK[bank],))
                    dve(lambda e, jk=jk, bank=bank: e.tensor_scalar(out=modT[:, jk, :], in0=ps[bank][:, 0:2],
                                                                     scalar1=bmodT[:, jk:jk + 1], scalar2=None, op0=ALU.add),
                        r=(PK[bank], "bmodT"), w=("modT",))

        def sub_vectors(l, half):
            j0 = 3 * half
            dma("sp", gT[:, half, :], (I["nmixT"] if half == 0 else I["nffnT"])[l], (), ("gT",), "gt")
            for c in range(2):
                dve(lambda e, c=c: e.scalar_tensor_tensor(out=gmodT[:, :, c], in0=modT[:, (j0 + 1) * KD:(j0 + 2) * KD, c],
                                                          scalar=1.0, in1=gT[:, half, :], op0=ALU.add, op1=ALU.mult),
                    r=("modT", "gT"), w=("gmodT",))
                dve(lambda e, c=c: e.tensor_copy(out=shiftT[:, :, c], in_=modT[:, j0 * KD:(j0 + 1) * KD, c]),
                    r=("modT",), w=("shiftT",))
            jg = j0 + 2
            dma("sp", brow[:], I["b_mod"][l:l + 1, jg * D:(jg + 1) * D].to_broadcast([2, D]), (), ("brow",), "br")
            for n0 in range(0, D, NC5):
                wv, wk = load_w(I["w_mod"][l], 0, KD, jg * D + n0, NC5)
                for k in range(KD):
                    mm(ps[2][0:2, 0:NC5], scondT[:, k, :], wv(k, 0, NC5), k == 0, k == KD - 1, r=(wk, "scondT"), w=(PK[2],))
                dve(lambda e, n0=n0: e.tensor_tensor(out=grow[:, n0:n0 + NC5], in0=ps[2][0:2, 0:NC5], in1=brow[:, n0:n0 + NC5], op=ALU.add),
                    r=(PK[2], "brow"), w=("grow",))
            for c in range(2):
                for n0 in range(0, D, NC5):
                    bank = 3 + (n0 // NC5) % 2
                    mm(ps[bank][:, 0:NC5], sel[:, c, :], grow[:, n0:n0 + NC5], True, True, r=("sel", "grow"), w=(PK[bank],))
                    act(gate_bc[:, c, n0:n0 + NC5], ps[bank][:, 0:NC5], AF.Copy, r=(PK[bank],), w=("gate_bc",))
            p.barrier()

        def norm_block(sub, sq, a, n):
            nt = n // 128
            c = sq["cond"]
            for j in range(nt):
                s = j % 2
                dma("sp", xt[:, s, :], src_rows(sub, sq, a + j * 128, a + (j + 1) * 128), (), (("xt", s),), ("xt", s))
                act(xn[:, j, :], xt[:, s, :], AF.Square, r=(("xt", s),), w=(("xn", j), ("ss", j)), accum_out=ss[:, j:j + 1])
                dve(lambda e, j=j: e.tensor_scalar(out=ss[:, j:j + 1], in0=ss[:, j:j + 1], scalar1=1.0 / D, scalar2=EPS,
                                                    op0=ALU.mult, op1=ALU.add), r=(("ss", j),), w=(("ss", j),))
                act(ss[:, j:j + 1], ss[:, j:j + 1], AF.Sqrt, r=(("ss", j),), w=(("ss", j),))
                dve(lambda e, j=j: e.reciprocal(out=ss[:, j:j + 1], in_=ss[:, j:j + 1]), r=(("ss", j),), w=(("ss", j),))
                act(xn[:, j, :], xt[:, s, :], AF.Copy, r=(("xt", s), ("ss", j)), w=(("xn", j),), scale=ss[:, j:j + 1])
            psT = ps[4][:].bitcast(BF)
            for k in range(KD):
                for j in range(nt):
                    p.op("pe", lambda e, k=k, j=j: e.transpose(out=psT[:, j * 128:(j + 1) * 128], in_=xn[:, j, k * 128:(k + 1) * 128],
                                                              identity=identb[:]),
                         r=(("xn", j), "identb"), w=(PK[4],))
                act(hT[:, k, 0:n], psT[:, 0:n], AF.Identity, r=(PK[4], "gmodT", "shiftT"), w=("hT",),
                    scale=gmodT[:, k, c:c + 1], bias=shiftT[:, k, c:c + 1])

        ocnt = [0]

        def resid_out(sub, sq, a, j, n0, ncols, pbank, lo, hi):
            r0 = a + j * 128
            va, vb = max(lo, r0), min(hi, r0 + 128)
            if va >= vb:
                return
            s = ocnt[0] % 2
            ocnt[0] += 1
            c = sq["cond"]
            dma("sp", xo2[:, s, 0:ncols], src_rows(sub, sq, r0, r0 + 128)[:, n0:n0 + ncols], (), (("xo2", s),), ("xo2", s))
            dve(lambda e: e.tensor_tensor(out=xo[:, s, 0:ncols], in0=ps[pbank][:, 0:ncols], in1=gate_bc[:, c, n0:n0 + ncols], op=ALU.mult),
                r=(PK[pbank], "gate_bc"), w=(("xo", s),))
            dve(lambda e: e.tensor_tensor(out=xo[:, s, 0:ncols], in0=xo[:, s, 0:ncols], in1=xo2[:, s, 0:ncols], op=ALU.add),
                r=(("xo", s), ("xo2", s)), w=(("xo", s),))
            dma("sp", dst_rows(sub, sq, va, vb)[:, n0:n0 + ncols], xo[va - r0:vb - r0, s, 0:ncols], (("xo", s),), (), ("xo", s))

        def out_proj(sub, sq, a, n, lhs_fn, nk, Wap, kparts=None, lo=None, hi=None):
            nt = n // 128
            lo = a if lo is None else lo
            hi = a + n if hi is None else hi
            kparts = kparts or [128] * nk
            roff = [0]
            for kp in kparts:
                roff.append(roff[-1] + kp)
            uniform = all(kp == 128 for kp in kparts)
            for n0 in range(0, D, NC5):
                if uniform:
                    groups = [(g, min(16, nk - g)) for g in range(0, nk, 16)]
                    for gi, (g0, gn) in enumerate(groups):
                        wv, wk = load_w(Wap, g0 * 128, gn, n0, NC5)
                        for j in range(nt):
                            for kk in range(gn):
                                k = g0 + kk
                                l_ap, l_keys = lhs_fn(k, j)
                                mm(ps[4 + j][:, 0:NC5], l_ap, wv(kk, 0, NC5), k == 0, k == nk - 1, r=(wk,) + tuple(l_keys), w=(PK[4 + j],))
                else:
                    for k in range(nk):
                        kp = kparts[k]
                        wv, wk = load_w(Wap, roff[k], 1, n0, NC5, kpart=kp)
                        for j in range(nt):
                            l_ap, l_keys = lhs_fn(k, j)
                            mm(ps[4 + j][:, 0:NC5], l_ap, wv(0, 0, NC5), k == 0, k == nk - 1, r=(wk,) + tuple(l_keys), w=(PK[4 + j],))
                for j in range(nt):
                    resid_out(sub, sq, a, j, n0, NC5, 4 + j, lo, hi)

        def ffn(l, sub):
            sub_vectors(l, 1)
            fcw = a32(0, NF * 3)
            fcb = a32(NF * 3, NF)
            dma("sp", fcw, I["fcwT"][l].rearrange("p f k -> p (f k)"), (), ("fcw",), "fcw")
            dma("sp", fcb, I["fcbT"][l], (), ("fcb",), "fcb")
            fcw = fcw.rearrange("p (f k) -> p f k", k=3)
            ctmp = a32(256, 1024).rearrange("p (s n) -> p s n", s=2)
            sil = a32(1280, 1024).rearrange("p (s n) -> p s n", s=2)
            c2 = a32(2304, 1024).rearrange("p (s n) -> p s n", s=2)
            uT = a16(8192, NF * 512).rearrange("p (f n) -> p f n", f=NF)
            it = [0]
            for sq in seqs:
                for (a, n, lo, hi) in ffn_blocks(sq["L"]):
                    norm_block(sub, sq, a, n)
                    for g in range(0, NF, 4):
                        gn = min(4, NF - g)
                        wa, wak = load_w(I["ffn_w_in"][l], 0, KD, g * 128, gn * 128)
                        wb, wbk = load_w(I["ffn_w_in"][l], 0, KD, DFF + g * 128, gn * 128)
                        for fc in range(gn):
                            f = g + fc
                            s = it[0] % 2
                            it[0] += 1
                            pa, pb = ps[s], ps[2 + s]
                            for k in range(KD):
                                mm(pa[:, 0:n], wa(k, fc * 128, (fc + 1) * 128), hT[:, k, 0:n], k == 0, k == KD - 1, r=(wak, "hT"), w=(PK[s],))
                            for k in range(KD):
                                mm(pb[:, 0:n], wb(k, fc * 128, (fc + 1) * 128), hT[:, k, 0:n], k == 0, k == KD - 1, r=(wbk, "hT"), w=(PK[2 + s],))
                            ck = ("ctmp", s)
                            if C.get("DBG") == 3:
                                dve(lambda e, s=s, f=f, pa=pa, n=n: e.tensor_scalar(out=ctmp[:, s, 0:n], in0=pa[:, 0:n], scalar1=fcw[:, f, 1:2], scalar2=fcb[:, f:f + 1],
                                                                              op0=ALU.mult, op1=ALU.add), r=(PK[s], "fcw", "fcb"), w=(ck,))
                            else:
                                act(ctmp[:, s, 0:n], pa[:, 0:n], AF.Identity, r=(PK[s], "fcw", "fcb"), w=(ck,),
                                    scale=fcw[:, f, 1:2], bias=fcb[:, f:f + 1])
                            if C.get('DBG') not in (2, 3):
                                c2k = ("c2", s)
                                dve(lambda e, s=s, f=f, pa=pa, n=n: e.scalar_tensor_tensor(out=c2[:, s, 1:n], in0=pa[:, 0:n - 1], scalar=fcw[:, f, 0:1],
                                                                                       in1=ctmp[:, s, 1:n], op0=ALU.mult, op1=ALU.add),
                                    r=(PK[s], ck, "fcw"), w=(c2k,))
                                act(c2[:, s, 0:1], ctmp[:, s, 0:1], AF.Copy, r=(ck,), w=(c2k,))
                                dve(lambda e, s=s, f=f, pa=pa, n=n: e.scalar_tensor_tensor(out=ctmp[:, s, 0:n - 1], in0=pa[:, 1:n], scalar=fcw[:, f, 2:3],
                                                                                       in1=c2[:, s, 0:n - 1], op0=ALU.mult, op1=ALU.add),
                                    r=(PK[s], c2k, "fcw"), w=(ck,))
                                act(ctmp[:, s, n - 1:n], c2[:, s, n - 1:n], AF.Copy, r=(c2k,), w=(ck,))
                            act(sil[:, s, 0:n], ctmp[:, s, 0:n], AF.Silu, r=(ck,), w=(("sil", s),))
                            dve(lambda e, s=s, f=f, pb=pb, n=n: e.tensor_tensor(out=uT[:, f, 0:n], in0=pb[:, 0:n], in1=sil[:, s, 0:n], op=ALU.mult),
                                r=(PK[2 + s], ("sil", s)), w=(("uT", f),))
                    if C.get("DBGOUT") and sq["smp"] and a == 0:
                        dma("pool", O["dbg1"][:, 0:512], uT[:, 0, 0:512], (("uT", 0),), (), "dbg1")
                        dma("pool", O["dbg3"][:, 0:512], hT[:, 0, 0:512], ("hT",), (), "dbg3")
                    out_proj(sub, sq, a, n, lambda k, j: (uT[:, k, j * 128:(j + 1) * 128], (("uT", k),)), NF, I["ffn_w_out"][l], lo=lo, hi=hi)
            p.barrier()

        MIX = {}
        SQ_OFF, RS_OFF, E_OFF, RD_OFF, S32_OFF = 38912, 19968, 36864, 17920, 15360
        ctr = dict(att=0, s=0, st=0)

        def qk_norm(parts, ones_l, tot, nq, pbank):
            sq = a16(SQ_OFF, 1024)
            for i, (pap, pk, m, oap, okeys, gcol, gk) in enumerate(parts):
                act(sq[0:m, i * 512:i * 512 + nq], pap, AF.Square, r=(pk,), w=(("sq", i),))
            for i, (pap, pk, m, oap, okeys, gcol, gk) in enumerate(parts):
                mm(ps[pbank][:, 0:nq], ones_l[i], sq[0:m, i * 512:i * 512 + nq], i == 0, i == len(parts) - 1,
                   r=(("sq", i), "ones", "ones2"), w=(PK[pbank],))
            rs = a32(RS_OFF, 512)
            act(rs[:, 0:nq], ps[pbank][:, 0:nq], AF.Ln, r=(PK[pbank], "epsb"), w=("rs",), scale=1.0 / tot, bias=epsb[:, 0:1])
            act(rs[:, 0:nq], rs[:, 0:nq], AF.Exp, r=("rs",), w=("rs",), scale=-0.5)
            for i, (pap, pk, m, oap, okeys, gcol, gk) in enumerate(parts):
                dve(lambda e, pap=pap, oap=oap, gcol=gcol, m=m: e.scalar_tensor_tensor(out=oap, in0=pap, scalar=gcol, in1=rs[0:m, 0:nq],
                                                                                   op0=ALU.mult, op1=ALU.mult),
                    r=(pk, "rs") + tuple(gk), w=tuple(okeys))

        def attend(ktiles, qparts, nq, Mv, out_ap, out_keys, rq, sink_ap=None, sink_keys=()):
            it = ctr["att"]
            ctr["att"] += 1
            ob, db = 4 + it % 2, 6 + it % 2
            E = a16(E_OFF, 2048).rearrange("p (s n) -> p s n", s=4)
            S32 = a32(S32_OFF, 2048).rearrange("p (s n) -> p s n", s=4)
            gsz = max(1, 512 // nq)
            nt = len(ktiles)
            for g0 in range(0, nt, gsz):
                grp = ktiles[g0:g0 + gsz]
                sbk = ctr["s"] % 4
                ctr["s"] += 1
                for ti, kt in enumerate(grp):
                    c0, c1 = ti * nq, (ti + 1) * nq
                    nparts = len(kt["kT"])
                    for i, kT in enumerate(kt["kT"]):
                        mm(ps[sbk][:, c0:c1], kT, qparts[i], i == 0, i == nparts - 1, r=tuple(kt["rk"]) + tuple(rq), w=(PK[sbk],))
                    if kt.get("bias") is not None:
                        dve(lambda e, sbk=sbk, c0=c0, c1=c1, b=kt["bias"]: e.tensor_tensor(out=S32[:, sbk, c0:c1], in0=ps[sbk][:, c0:c1], in1=b, op=ALU.add),
                            r=(PK[sbk],) + tuple(kt["rb"]), w=(("S32", sbk),))
                ti = 0
                while ti < len(grp):
                    hb = grp[ti].get("bias") is not None
                    tj = ti
                    while tj < len(grp) and (grp[tj].get("bias") is not None) == hb:
                        tj += 1
                    c0, c1 = ti * nq, tj * nq
                    if hb:
                        act(E[:, sbk, c0:c1], S32[:, sbk, c0:c1], AF.Exp, r=(("S32", sbk),), w=(("E", sbk),))
                    else:
                        act(E[:, sbk, c0:c1], ps[sbk][:, c0:c1], AF.Exp, r=(PK[sbk],), w=(("E", sbk),))
                    ti = tj
                for ti, kt in enumerate(grp):
                    t = g0 + ti
                    c0, c1 = ti * nq, (ti + 1) * nq
                    mm(ps[ob][0:Mv, 0:nq], kt["v"], E[:, sbk, c0:c1], t == 0, t == nt - 1, r=(("E", sbk),) + tuple(kt["rv"]), w=(PK[ob],))
                    mm(ps[db][0:Mv, 0:nq], ones_b[:, 0:Mv], E[:, sbk, c0:c1], t == 0, t == nt - 1, r=(("E", sbk), "ones"), w=(PK[db],))
            rden = a32(RD_OFF, 512)
            if sink_ap is not None:
                dve(lambda e: e.tensor_tensor(out=rden[0:Mv, 0:nq], in0=ps[db][0:Mv, 0:nq], in1=sink_ap, op=ALU.add),
                    r=(PK[db],) + tuple(sink_keys), w=("rden",))
                dve(lambda e: e.reciprocal(out=rden[0:Mv, 0:nq], in_=rden[0:Mv, 0:nq]), r=("rden",), w=("rden",))
            else:
                dve(lambda e: e.reciprocal(out=rden[0:Mv, 0:nq], in_=ps[db][0:Mv, 0:nq]), r=(PK[db],), w=("rden",))
            dve(lambda e: e.tensor_tensor(out=out_ap, in0=ps[ob][0:Mv, 0:nq], in1=rden[0:Mv, 0:nq], op=ALU.mult),
                r=(PK[ob], "rden"), w=tuple(out_keys))

        def transpose_out32(src32, srck, m, ntok_tiles, dst_fn, bank=3):
            tst = a32(17400, 256).rearrange("p (s n) -> p s n", s=2)
            for jt in range(ntok_tiles):
                s = ctr["st"] % 2
                ctr["st"] += 1
                p.op("pe", lambda e, jt=jt: e.transpose(out=ps[bank][:, 0:m], in_=src32[0:m, jt * 128:(jt + 1) * 128], identity=ident32[0:m, 0:m]),
                     r=(srck, "ident32"), w=(PK[bank],))
                act(tst[:, s, 0:m], ps[bank][:, 0:m], AF.Copy, r=(PK[bank],), w=(("tst", s),))
                d_ = dst_fn(jt)
                if isinstance(d_, list):
                    for (dap, c0, c1) in d_:
                        dma("sp", dap, tst[:, s, c0:c1], (("tst", s),), (), ("tst", s))
                else:
                    dma("sp", d_, tst[:, s, 0:m], (("tst", s),), (), ("tst", s))

        def load_oT_and_project(sub, nk, Wap, kparts=None):
            for sq_ in seqs:
                t0 = sq_["t0"]
                for (a, n) in plain_blocks(sq_["L"]):
                    if kparts is None:
                        oTb = a16(0, nk * 512).rearrange("p (k n) -> p k n", k=nk)
                        dma("sp", oTb[:, :, 0:n], SC_OT[0:nk * 128, t0 + a:t0 + a + n].rearrange("(k p) t -> p k t", p=128), (), ("oTb",), "oTb")
                    else:
                        oTb = a16(0, nk * 512).rearrange("p (k n) -> p k n", k=nk)
                        for k in range(nk):
                            dma("sp", oTb[0:kparts[k], k, 0:n], SC_OT[k * 128:k * 128 + kparts[k], t0 + a:t0 + a + n], (), ("oTb",), ("oTb", k % 4))
                    kp = kparts or [128] * nk
                    out_proj(sub, sq_, a, n, lambda k, j, oTb=oTb, kp=kp: (oTb[0:kp[k], k, j * 128:(j + 1) * 128], ("oTb",)), nk, Wap, kparts=kparts)

        def mixer_nat(l, jn, sub):
            sub_vectors(l, 0)
            stop_at(2)
            Wq, Wo = I["nat_w_qkv"][jn], I["nat_w_o"][jn]
            H = NAH
            HD = H * 128
            gq = a32(13300, 4)
            dma("sp", gq[:, 0:2], I["nat_gqk"][jn], (), ("gq",), "gq")
            dve(lambda e: e.tensor_scalar(out=gq[:, 2:3], in0=gq[:, 0:1], scalar1=128.0 ** -0.5, scalar2=None, op0=ALU.mult), r=("gq",), w=("gq",))
            qst = a16(20000, 2048).rearrange("p (s n) -> p s n", s=4)
            k32 = a32(11100, 512)
            vst = a16(23300, 1024).rearrange("p (s n) -> p s n", s=2)
            v32 = a32(12200, 1024).rearrange("p (s n) -> p s n", s=2)
            for sq_ in seqs:
                t0 = sq_["t0"]
                for (a, n) in plain_blocks(sq_["L"]):
                    norm_block(sub, sq_, a, n)
                    stop_at(21)
                    for which in (0, 1):
                        for g0 in range(0, H, 4):
                            gh = min(4, H - g0)
                            wv, wk = load_w(Wq, 0, KD, which * HD + g0 * 128, gh * 128)
                            for hh in range(gh):
                                h = g0 + hh
                                bank = hh % 2
                                for k in range(KD):
                                    mm(ps[bank][:, 0:n], wv(k, hh * 128, (hh + 1) * 128), hT[:, k, 0:n], k == 0, k == KD - 1, r=(wk, "hT"), w=(PK[bank],))
                                s = ctr["st"] % 4
                                ctr["st"] += 1
                                gcol = gq[:, 2:3] if which == 0 else gq[:, 1:2]
                                if which == 1 and not sq_["smp"]:
                                    qk_norm([(ps[bank][:, 0:n], PK[bank], 128, k32[:, 0:n], ("k32",), gcol, ("gq",))], [ones_b[:, :]], 128, n, 2)
                                    dve(lambda e, s=s, n=n: e.tensor_copy(out=qst[:, s, 0:n], in_=k32[:, 0:n]), r=("k32",), w=(("qst", s),))
                                    transpose_out32(k32, "k32", 128, n // 128,
                                                    lambda jt, h=h, a=a, pi=sq_["pi"]: O["nat_k"][pi, jn, h, a + jt * 128:a + (jt + 1) * 128, :])
                                else:
                                    qk_norm([(ps[bank][:, 0:n], PK[bank], 128, qst[:, s, 0:n], (("qst", s),), gcol, ("gq",))], [ones_b[:, :]], 128, n, 2)
                                stop_at(22)
                                dma("sp", SC_FM[which * HD + h * 128:which * HD + (h + 1) * 128, t0 + a:t0 + a + n], qst[:, s, 0:n],
                                    (("qst", s),), (), ("qst", s))
                                stop_at(23)
                    stop_at(24)
                    if not sq_["smp"]:
                        stop_at(27)
                    for n0 in range(0, HD, 512):
                        ncol = min(512, HD - n0)
                        wv, wk = load_w(Wq, 0, KD, 2 * HD + n0, ncol)
                        for jt in range(n // 128):
                            bank = jt % 2
                            for k in range(KD):
                                mm(ps[bank][:, 0:ncol], hT[:, k, jt * 128:(jt + 1) * 128], wv(k, 0, ncol), k == 0, k == KD - 1, r=(wk, "hT"), w=(PK[bank],))
                            s = ctr["st"] % 2
                            ctr["st"] += 1
                            act(vst[:, s, 0:ncol], ps[bank][:, 0:ncol], AF.Copy, r=(PK[bank],), w=(("vst", s),))
                            r0 = t0 + a + jt * 128
                            dma("sp", SC_TM[r0:r0 + 128, n0:n0 + ncol], vst[:, s, 0:ncol], (("vst", s),), (), ("vst", s))
                            if not sq_["smp"]:
                                act(v32[:, s, 0:ncol], ps[bank][:, 0:ncol], AF.Copy, r=(PK[bank],), w=(("v32", s),))
                                for hh in range(ncol // 128):
                                    dst = O["nat_v"][sq_["pi"], jn, n0 // 128 + hh, a + jt * 128:a + (jt + 1) * 128, :]
                                    if C.get("DBG") != 1:
                                        dma("sp", dst, v32[:, s, hh * 128:(hh + 1) * 128], (("v32", s),), (), ("v32", s))
                    stop_at(25)
                    if not sq_["smp"]:
                        stop_at(28)
                if sq_["smp"]:
                    stop_at(26)
            p.barrier()
            stop_at(3)
            TT = TS // 128
            qh, kh = a16(0, TS), a16(TS, TS)
            vh0 = a16(2 * TS, TS).rearrange("p (n d) -> p n d", d=128)
            vh1 = a16(3 * TS, TS).rearrange("p (n d) -> p n d", d=128)
            B0 = 4 * TS
            bt = a32(B0 // 2, 896)
            kc32 = a32(B0 // 2 + 896, NPT * 128).rearrange("p (n d) -> p n d", d=128)
            vc = a16(B0 + 1792 + NPT * 256, NPT * 128).rearrange("p (n d) -> p n d", d=128)
            kcT = a16(B0 + 1792 + NPT * 384, NPT * 128)
            ost = a16(B0 + 1792 + NPT * 512, 1024).rearrange("p (s n) -> p s n", s=2)
            assert B0 + 1792 + NPT * 512 + 1024 <= 30720
            oc = 0
            for h in range(H):
                dma("sp", qh, SC_FM[h * 128:(h + 1) * 128, 0:TS], (), ("qh",), "qh")
                dma("sp", kh, SC_FM[HD + h * 128:HD + (h + 1) * 128, 0:TS], (), ("kh",), "kh")
                dma("sp", vh0, SC_TM[0:TS, h * 128:(h + 1) * 128].rearrange("(n p) d -> p n d", p=128), (), ("vh",), "vh0")
                dma("sp", vh1[:, 0:TT - 1, :], SC_TM[64:TS - 64, h * 128:(h + 1) * 128].rearrange("(n p) d -> p n d", p=128), (), ("vh",), "vh1")
                dma("sp", bt, I["nat_bt"][jn, h], (), ("bt",), "bt")
                dma("sp", kc32, I["cnk"][jn, h].rearrange("(n p) d -> p n d", p=128), (), ("kc32",), "kc32")
                dma("pool", vc, I["cnv"][jn, h].rearrange("(n p) d -> p n d", p=128), (), ("vc",), "vc")
                for t in range(NPT):
                    p.op("pe", lambda e, t=t: e.transpose(out=ps[0][:, t * 128:(t + 1) * 128], in_=kc32[:, t, :], identity=ident32[:, :]),
                         r=("kc32", "ident32"), w=(PK[0],))
                act(kcT[:, 0:NPT * 128], ps[0][:, 0:NPT * 128], AF.Copy, r=(PK[0],), w=("kcT",))
                for r in range(ROWS):
                    rs_ = min(max(r - 4, 0), ROWS - 8)
                    kts = []
                    for kt in range(4):
                        gr = rs_ + 2 * kt
                        tok = gr * 64
                        vt = vh0[:, tok // 128, :] if tok % 128 == 0 else vh1[:, (tok - 64) // 128, :]
                        d = gr - r + 7
                        kts.append(dict(kT=[kh[:, tok:tok + 128]], v=vt, bias=bt[:, d * 64:(d + 1) * 64], rk=("kh",), rv=("vh",), rb=("bt",)))
                    for t in range(NPT):
                        kts.append(dict(kT=[kcT[:, t * 128:(t + 1) * 128]], v=vc[:, t, :], rk=("kcT",), rv=("vc",)))
                    s = oc % 2
                    oc += 1
                    qtmp = a16(B0 + 1792 + NPT * 512 + 1024, 1024).rearrange("p (s n) -> p s n", s=2)
                    act(qtmp[:, s, 0:64], qh[:, r * 64:(r + 1) * 64], AF.Copy, r=("qh",), w=(("qtmp", s),))
                    attend(kts, [qtmp[:, s, 0:64]], 64, 128, ost[:, s, 0:64], (("ost", s),), (("qtmp", s),))
                    dma("sp", SC_OT[h * 128:(h + 1) * 128, r * 64:(r + 1) * 64], ost[:, s, 0:64], (("ost", s),), (), ("ost", s))
            stop_at(4)
            for sq_ in seqs[1:]:
                t0 = sq_["t0"]
                npt = PL // 128
                qP = a16(0, H * PL).rearrange("p (h t) -> p h t", h=H)
                kP = a16(H * PL, H * PL).rearrange("p (h t) -> p h t", h=H)
                vP = a16(2 * H * PL, npt * HD).rearrange("p (n c) -> p n c", n=npt)
                dma("sp", qP, SC_FM[0:HD, t0:t0 + PL].rearrange("(h p) t -> p h t", p=128), (), ("qh",), "qh")
                dma("sp", kP, SC_FM[HD:2 * HD, t0:t0 + PL].rearrange("(h p) t -> p h t", p=128), (), ("kh",), "kh")
                dma("sp", vP, SC_TM[t0:t0 + PL, 0:HD].rearrange("(n p) c -> p n c", p=128), (), ("vh",), "vh0")
                for h in range(H):
                    kts = [dict(kT=[kP[:, h, t * 128:(t + 1) * 128]], v=vP[:, t, h * 128:(h + 1) * 128], rk=("kh",), rv=("vh",)) for t in range(npt)]
                    s = oc % 2
                    oc += 1
                    attend(kts, [qP[:, h, :]], PL, 128, ost[:, s, 0:PL], (("ost", s),), ("qh",))
                    dma("sp", SC_OT[h * 128:(h + 1) * 128, t0:t0 + PL], ost[:, s, 0:PL], (("ost", s),), (), ("ost", s))
            p.barrier()
            stop_at(5)
            load_oT_and_project(sub, H, Wo)
            p.barrier()

        MIX[0] = mixer_nat

        def rope_apply(x16, psbank, perm_b, tab, a, n, m, out_ap, out_keys, xkey):
            mm(ps[psbank][0:m, 0:n], perm_b[0:m, 0:m], x16, True, True, r=(xkey, "perm"), w=(PK[psbank],))
            t1 = a32(15360, 512)
            dve(lambda e: e.tensor_tensor(out=t1[0:m, 0:n], in0=x16, in1=tab[0:m, 0, 0:n], op=ALU.mult), r=(xkey, "ropetab"), w=("ropet1",))
            t2 = a32(15872, 512)
            dve(lambda e: e.tensor_tensor(out=t2[0:m, 0:n], in0=ps[psbank][0:m, 0:n], in1=tab[0:m, 1, 0:n], op=ALU.mult), r=(PK[psbank], "ropetab"), w=("ropet2",))
            dve(lambda e: e.tensor_tensor(out=out_ap, in0=t1[0:m, 0:n], in1=t2[0:m, 0:n], op=ALU.add), r=("ropet1", "ropet2"), w=tuple(out_keys))

        def mixer_swa(l, jn, sub):
            sub_vectors(l, 0)
            Wq, Wo = I["swa_w_qkv"][jn], I["swa_w_o"][jn]
            G = SWAH // SWAKV
            NQC = SWAH * 64 // 128
            NKC = max(1, SWAKV * 64 // 128)
            KVW = SWAKV * 64
            K0 = SWAH * 64
            gq = a32(13300, 4)
            dma("sp", gq[:, 0:2], I["swa_gqk"][jn], (), ("gq",), "gq")
            dve(lambda e: e.tensor_scalar(out=gq[:, 2:3], in0=gq[:, 0:1], scalar1=64.0 ** -0.5, scalar2=None, op0=ALU.mult), r=("gq",), w=("gq",))
            permb = a16(27000, 128)
            dma("pool", permb, I["perm128"][:, :], (), ("perm",), "perm")
            tab = a32(14000, 1024).rearrange("p (c n) -> p c n", c=2)
            qst = a16(20000, 2048).rearrange("p (s n) -> p s n", s=4)
            x16 = a16(27200, 512)
            k32 = a32(11100, 512)
            vst = a16(23300, 1024).rearrange("p (s n) -> p s n", s=2)
            v32 = a32(12200, 1024).rearrange("p (s n) -> p s n", s=2)
            for sq_ in seqs:
                t0 = sq_["t0"]
                smp = sq_["smp"]
                for (a, n) in plain_blocks(sq_["L"]):
                    norm_block(sub, sq_, a, n)
                    if smp:
                        dma("sp", tab[:, :, 0:n], I["rope128"][:, :, a:a + n], (), ("ropetab",), "ropetab")
                    for ci in range(NQC + NKC):
                        isk = ci >= NQC
                        if ci % 4 == 0 or ci == NQC:
                            cbase = ci
                            ncw = min(4, (NQC if not isk else NQC + NKC) - ci) * 128
                            if isk:
                                ncw = min(ncw, KVW)
                            wv, wk = load_w(Wq, 0, KD, ci * 128, ncw)
                        m = 128 if not isk else min(128, KVW)
                        c0 = (ci - cbase) * 128
                        bank = ci % 2
                        for k in range(KD):
                            mm(ps[bank][0:m, 0:n], wv(k, c0, c0 + m), hT[:, k, 0:n], k == 0, k == KD - 1, r=(wk, "hT"), w=(PK[bank],))
                        s = ctr["st"] % 4
                        ctr["st"] += 1
                        gcol = gq[0:m, 1:2] if isk else gq[0:m, 2:3]
                        want32 = isk and not smp
                        if smp:
                            qk_norm([(ps[bank][0:m, 0:n], PK[bank], m, x16[0:m, 0:n], ("x16",), gcol, ("gq",))], [ones2_b[0:m, :]], 64, n, 2)
                            rope_apply(x16[0:m, 0:n], 3, permb, tab, a, n, m, qst[0:m, s, 0:n], (("qst", s),), "x16")
                        elif want32:
                            qk_norm([(ps[bank][0:m, 0:n], PK[bank], m, k32[0:m, 0:n], ("k32",), gcol, ("gq",))], [ones2_b[0:m, :]], 64, n, 2)
                            dve(lambda e, s=s, n=n, m=m: e.tensor_copy(out=qst[0:m, s, 0:n], in_=k32[0:m, 0:n]), r=("k32",), w=(("qst", s),))
                            kv0 = (ci - NQC) * 2

                            def dsts(jt, kv0=kv0, a=a, pi=sq_["pi"], m=m):
                                return [(O["swa_k"][pi, jn, kv0 + i, a + jt * 128:a + (jt + 1) * 128, :], i * 64, (i + 1) * 64) for i in range(m // 64)]
                            transpose_out32(k32, "k32", m, n // 128, dsts)
                        else:
                            qk_norm([(ps[bank][0:m, 0:n], PK[bank], m, qst[0:m, s, 0:n], (("qst", s),), gcol, ("gq",))], [ones2_b[0:m, :]], 64, n, 2)
                        dma("sp", SC_FM[ci * 128:ci * 128 + m, t0 + a:t0 + a + n], qst[0:m, s, 0:n], (("qst", s),), (), ("qst", s))
                    wv, wk = load_w(Wq, 0, KD, K0 + KVW, KVW)
                    for jt in range(n // 128):
                        bank = jt % 2
                        for k in range(KD):
                            mm(ps[bank][:, 0:KVW], hT[:, k, jt * 128:(jt + 1) * 128], wv(k, 0, KVW), k == 0, k == KD - 1, r=(wk, "hT"), w=(PK[bank],))
                        s = ctr["st"] % 2
                        ctr["st"] += 1
                        act(vst[:, s, 0:KVW], ps[bank][:, 0:KVW], AF.Copy, r=(PK[bank],), w=(("vst", s),))
                        r0 = t0 + a + jt * 128
                        dma("sp", SC_TM[r0:r0 + 128, 0:KVW], vst[:, s, 0:KVW], (("vst", s),), (), ("vst", s))
                        if not smp:
                            act(v32[:, s, 0:KVW], ps[bank][:, 0:KVW], AF.Copy, r=(PK[bank],), w=(("v32", s),))
                            for kv in range(SWAKV):
                                dma("sp", O["swa_v"][sq_["pi"], jn, kv, a + jt * 128:a + (jt + 1) * 128, :], v32[:, s, kv * 64:(kv + 1) * 64],
                                    (("v32", s),), (), ("v32", s))
            p.barrier()
            TB = TS // 128
            kh = a16(0, TS)
            vh = a16(TS, TS // 2).rearrange("p (n d) -> p n d", d=64)
            B0 = TS + TS // 2
            kc32 = a32(B0 // 2, NPT * 64).rearrange("p (n d) -> p n d", d=64)
            vc = a16(B0 + NPT * 128, NPT * 64).rearrange("p (n d) -> p n d", d=64)
            kcT = a16(B0 + NPT * 192, NPT * 128)
            Qb = a16(B0 + NPT * 320, 2 * G * 128).rearrange("p (s n) -> p s n", s=2)
            B1 = B0 + NPT * 320 + 2 * G * 128
            ost = a16(B1, 1024).rearrange("p (s n) -> p s n", s=2)
            mask4 = a32((B1 + 1024) // 2, 1024).rearrange("p (w n) -> p w n", w=2)
            es = a32((B1 + 1024) // 2 + 1024, SWAH)
            es4 = a32((B1 + 1024) // 2 + 1024 + SWAH, SWAH * 128).rearrange("p (h n) -> p h n", n=128)
            ones32 = a32((B1 + 1024) // 2 + 1024 + SWAH + SWAH * 128, 128)
            assert (B1 + 1024) + 2 * (1024 + SWAH + SWAH * 128 + 128) <= 30720
            for w_ in range(2):
                for i in range(4):
                    dma("sp", mask4[:, w_, i * 128:(i + 1) * 128], I["swamask"][:, w_, :], (), ("mask4",), "mask4")
            dma("sp", es[0:64, :], I["swa_sinks"][jn:jn + 1, :].to_broadcast([64, SWAH]), (), ("es",), "es")
            act(es[0:64, :], es[0:64, :], AF.Exp, r=("es",), w=("es",))
            p.op("dve", lambda e: e.memset(ones32[:], 1.0), w=("ones32",))
            for h in range(SWAH):
                dve(lambda e, h=h: e.tensor_scalar(out=es4[0:64, h, :], in0=ones32[0:64, :], scalar1=es[0:64, h:h + 1], scalar2=None, op0=ALU.mult),
                    r=("es", "ones32"), w=("es4",))
            es4f = a32((B1 + 1024) // 2 + 1024 + SWAH, SWAH * 128)
            oc = 0
            for sq_ in seqs:
                t0, Lq, smp = sq_["t0"], sq_["L"], sq_["smp"]
                nb = Lq // 128
                for g_ in range(SWAKV):
                    dma("sp", kh[0:64, 0:Lq], SC_FM[K0 + g_ * 64:K0 + (g_ + 1) * 64, t0:t0 + Lq], (), ("kh",), "kh")
                    dma("sp", vh[:, 0:nb, :], SC_TM[t0:t0 + Lq, g_ * 64:(g_ + 1) * 64].rearrange("(n p) d -> p n d", p=128), (), ("vh",), "vh0")
                    if smp:
                        dma("sp", kc32, I["csk"][jn, g_].rearrange("(n p) d -> p n d", p=128), (), ("kc32",), "kc32")
                        dma("pool", vc, I["csv"][jn, g_].rearrange("(n p) d -> p n d", p=128), (), ("vc",), "vc")
                        for t in range(NPT):
                            p.op("pe", lambda e, t=t: e.transpose(out=ps[0][0:64, t * 128:(t + 1) * 128], in_=kc32[:, t, :], identity=ident32[:, :]),
                                 r=("kc32", "ident32"), w=(PK[0],))
                        act(kcT[0:64, 0:NPT * 128], ps[0][0:64, 0:NPT * 128], AF.Copy, r=(PK[0],), w=("kcT",))
                    for b in range(nb):
                        qs = oc % 2
                        dma("sp", Qb[0:64, qs, :].rearrange("d (h t) -> d h t", t=128),
                            SC_FM[g_ * G * 64:(g_ + 1) * G * 64, t0 + b * 128:t0 + (b + 1) * 128].rearrange("(h d) t -> d h t", d=64),
                            (), (("Qb", qs),), ("Qb", qs))
                        for hh in range(G // 4):
                            kts = []
                            kbs = (b - 1, b, b + 1) if smp else tuple(range(nb))
                            for kb in kbs:
                                if kb < 0 or kb >= nb:
                                    continue
                                d_ = dict(kT=[kh[0:64, kb * 128:(kb + 1) * 128]], v=vh[:, kb, :], rk=("kh",), rv=("vh",))
                                if smp and kb != b:
                                    d_["bias"] = mask4[:, 0 if kb < b else 1, :]
                                    d_["rb"] = ("mask4",)
                                kts.append(d_)
                            if smp:
                                for t in range(NPT):
                                    kts.append(dict(kT=[kcT[0:64, t * 128:(t + 1) * 128]], v=vc[:, t, :], rk=("kcT",), rv=("vc",)))
                            s = oc % 2
                            oc += 1
                            h0 = g_ * G + hh * 4
                            attend(kts, [Qb[0:64, qs, hh * 512:(hh + 1) * 512]], 512, 64, ost[0:64, s, :], (("ost", s),), (("Qb", qs),),
                                   sink_ap=es4f[0:64, h0 * 128:(h0 + 4) * 128], sink_keys=("es4",))
                            dma("sp", SC_OT[h0 * 64:(h0 + 4) * 64, t0 + b * 128:t0 + (b + 1) * 128].rearrange("(h d) t -> d h t", d=64),
                                ost[0:64, s, :].rearrange("d (h t) -> d h t", t=128), (("ost", s),), (), ("ost", s))
            p.barrier()
            load_oT_and_project(sub, SWAH * 64 // 128, Wo)
            p.barrier()

        MIX[3] = mixer_swa

        def mixer_mla(l, jn, sub):
            sub_vectors(l, 0)
            Wd, Wuq, Wukv, Wo = I["mla_w_down"][jn], I["mla_w_uq"][jn], I["mla_w_ukv"][jn], I["mla_w_o"][jn]
            H = MLAH
            NQ, NKV = QR // 128, KVR // 128
            QN0, QR0, KN0, KR0 = 0, H * 128, H * 192, H * 320
            gq = a32(13300, 8)
            dma("sp", gq[:, 0:4], I["mla_gqk"][jn], (), ("gq",), "gq")
            dve(lambda e: e.tensor_scalar(out=gq[:, 4:6], in0=gq[:, 0:2], scalar1=192.0 ** -0.5, scalar2=None, op0=ALU.mult), r=("gq",), w=("gq",))
            ga = a32(13320, NQ + NKV)
            dma("sp", ga, I["mla_gaT"][jn], (), ("ga",), "ga")
            permb = a16(27000, 64)
            dma("pool", permb[0:64, :], I["perm64"][:, :], (), ("perm",), "perm")
            tab = a32(14000, 1024).rearrange("p (c n) -> p c n", c=2)
            qst = a16(20000, 2048).rearrange("p (s n) -> p s n", s=4)
            x16 = a16(27200, 512)
            vst = a16(23300, 1024).rearrange("p (s n) -> p s n", s=2)
            d32 = a32(0, (NQ + NKV + 1) * 512).rearrange("p (c n) -> p c n", n=512)
            dn16 = a16(2 * (NQ + NKV + 1) * 512, (NQ + NKV + 1) * 512).rearrange("p (c n) -> p c n", n=512)
            assert 3 * (NQ + NKV + 1) * 512 <= 20000
            kc32 = a32(12200, 512)

            def kv_heads(n, tcol, rope, a):
                for h in range(H):
                    wv, wk = load_w(Wukv, 0, NKV, h * 256, 256)
                    bank = h % 2
                    for c in range(NKV):
                        mm(ps[bank][:, 0:n], wv(c, 0, 128), dn16[:, NQ + c, 0:n], c == 0, c == NKV - 1, r=(wk, "dn16"), w=(PK[bank],))
                    s = ctr["st"] % 4
                    ctr["st"] += 1
                    s2 = (s + 1) % 4
                    ctr["st"] += 1
                    o1 = (x16[0:64, 0:n], ("x16",)) if rope else (qst[0:64, s2, 0:n], (("qst", s2),))
                    qk_norm([(ps[bank][:, 0:n], PK[bank], 128, qst[:, s, 0:n], (("qst", s),), gq[:, 2:3], ("gq",)),
                             (d32[0:64, NQ + NKV, 0:n], "d32", 64, o1[0], o1[1], gq[0:64, 3:4], ("gq",))],
                            [ones_b[:, :], ones_b[0:64, :]], 192, n, 2)
                    if rope:
                        rope_apply(x16[0:64, 0:n], 3, permb, tab, a, n, 64, qst[0:64, s2, 0:n], (("qst", s2),), "x16")
                    dma("sp", SC_FM[KN0 + h * 128:KN0 + (h + 1) * 128, tcol:tcol + n], qst[:, s, 0:n], (("qst", s),), (), ("qst", s))
                    dma("sp", SC_FM[KR0 + h * 64:KR0 + (h + 1) * 64, tcol:tcol + n], qst[0:64, s2, 0:n], (("qst", s2),), (), ("qst", s2))
                    for jt in range(n // 128):
                        bank2 = 4 + jt % 2
                        for c in range(NKV):
                            mm(ps[bank2][:, 0:128], dn16[:, NQ + c, jt * 128:(jt + 1) * 128], wv(c, 128, 256), c == 0, c == NKV - 1, r=(wk, "dn16"), w=(PK[bank2],))
                        sv = ctr["st"] % 2
                        ctr["st"] += 1
                        act(vst[:, sv, 0:128], ps[bank2][:, 0:128], AF.Copy, r=(PK[bank2],), w=(("vst", sv),))
                        dma("sp", SC_TM[tcol + jt * 128:tcol + (jt + 1) * 128, h * 128:(h + 1) * 128], vst[:, sv, 0:128], (("vst", sv),), (), ("vst", sv))

            for sq_ in seqs:
                t0, smp = sq_["t0"], sq_["smp"]
                for (a, n) in plain_blocks(sq_["L"]):
                    norm_block(sub, sq_, a, n)
                    if smp:
                        dma("sp", tab[0:64, :, 0:n], I["rope64"][:, :, a:a + n], (), ("ropetab",), "ropetab")
                    nch = NQ + NKV + 1
                    for c in range(nch):
                        if c % 4 == 0:
                            ncw = min(512, QR + KVR + 64 - c * 128)
                            wv, wk = load_w(Wd, 0, KD, c * 128, ncw)
                        m = 128 if c < nch - 1 else 64
                        c0 = (c % 4) * 128
                        bank = c % 2
                        for k in range(KD):
                            mm(ps[bank][0:m, 0:n], wv(k, c0, c0 + m), hT[:, k, 0:n], k == 0, k == KD - 1, r=(wk, "hT"), w=(PK[bank],))
                        act(d32[0:m, c, 0:n], ps[bank][0:m, 0:n], AF.Copy, r=(PK[bank],), w=("d32",))
                    for (c_lo, c_hi) in ((0, NQ), (NQ, NQ + NKV)):
                        sqb = a16(SQ_OFF, 1024)
                        for c in range(c_lo, c_hi):
                            s_ = c % 2
                            act(sqb[:, s_ * 512:s_ * 512 + n], d32[:, c, 0:n], AF.Square, r=("d32",), w=(("sq", s_),))
                            mm(ps[2][:, 0:n], ones_b[:, :], sqb[:, s_ * 512:s_ * 512 + n], c == c_lo, c == c_hi - 1, r=(("sq", s_), "ones"), w=(PK[2],))
                        rs = a32(RS_OFF, 512)
                        act(rs[:, 0:n], ps[2][:, 0:n], AF.Ln, r=(PK[2], "epsb"), w=("rs",), scale=1.0 / ((c_hi - c_lo) * 128), bias=epsb[:, 0:1])
                        act(rs[:, 0:n], rs[:, 0:n], AF.Exp, r=("rs",), w=("rs",), scale=-0.5)
                        for c in range(c_lo, c_hi):
                            dve(lambda e, c=c, n=n: e.scalar_tensor_tensor(out=d32[:, c, 0:n], in0=d32[:, c, 0:n], scalar=ga[:, c:c + 1], in1=rs[:, 0:n],
                                                                          op0=ALU.mult, op1=ALU.mult), r=("d32", "rs", "ga"), w=("d32",))
                    for c in range(nch):
                        m = 128 if c < nch - 1 else 64
                        act(dn16[0:m, c, 0:n], d32[0:m, c, 0:n], AF.Copy, r=("d32",), w=("dn16",))
                    if not smp:
                        for c in range(NKV):
                            transpose_out32(d32[:, NQ + c, :], "d32", 128, n // 128,
                                            lambda jt, c=c, a=a, pi=sq_["pi"]: O["mla_ckv"][pi, jn, a + jt * 128:a + (jt + 1) * 128, c * 128:(c + 1) * 128])
                        transpose_out32(d32[:, NQ + NKV, :], "d32", 64, n // 128,
                                        lambda jt, a=a, pi=sq_["pi"]: O["mla_krope"][pi, jn, a + jt * 128:a + (jt + 1) * 128, :])
                    for h in range(H):
                        wv, wk = load_w(Wuq, 0, NQ, h * 192, 192)
                        for c in range(NQ):
                            mm(ps[0][:, 0:n], wv(c, 0, 128), dn16[:, c, 0:n], c == 0, c == NQ - 1, r=(wk, "dn16"), w=(PK[0],))
                        for c in range(NQ):
                            mm(ps[1][0:64, 0:n], wv(c, 128, 192), dn16[:, c, 0:n], c == 0, c == NQ - 1, r=(wk, "dn16"), w=(PK[1],))
                        s = ctr["st"] % 4
                        ctr["st"] += 1
                        s2 = (s + 1) % 4
                        ctr["st"] += 1
                        o1 = (x16[0:64, 0:n], ("x16",)) if smp else (qst[0:64, s2, 0:n], (("qst", s2),))
                        qk_norm([(ps[0][:, 0:n], PK[0], 128, qst[:, s, 0:n], (("qst", s),), gq[:, 4:5], ("gq",)),
                                 (ps[1][0:64, 0:n], PK[1], 64, o1[0], o1[1], gq[0:64, 5:6], ("gq",))],
                                [ones_b[:, :], ones_b[0:64, :]], 192, n, 2)
                        if smp:
                            rope_apply(x16[0:64, 0:n], 3, permb, tab, a, n, 64, qst[0:64, s2, 0:n], (("qst", s2),), "x16")
                        dma("sp", SC_FM[QN0 + h * 128:QN0 + (h + 1) * 128, t0 + a:t0 + a + n], qst[:, s, 0:n], (("qst", s),), (), ("qst", s))
                        dma("sp", SC_FM[QR0 + h * 64:QR0 + (h + 1) * 64, t0 + a:t0 + a + n], qst[0:64, s2, 0:n], (("qst", s2),), (), ("qst", s2))
                    kv_heads(n, t0 + a, smp, a)
            for t in range(NPT):
                dma("sp", kc32[:, 0:KVR], I["cckv"][jn, t * 128:(t + 1) * 128, :], (), ("kc32",), "kc32")
                for c in range(NKV):
                    p.op("pe", lambda e, c=c: e.transpose(out=ps[0][:, c * 128:(c + 1) * 128], in_=kc32[:, c * 128:(c + 1) * 128], identity=ident32[:, :]),
                         r=("kc32", "ident32"), w=(PK[0],))
                for c in range(NKV):
                    act(dn16[:, NQ + c, t * 128:(t + 1) * 128], ps[0][:, c * 128:(c + 1) * 128], AF.Copy, r=(PK[0],), w=("dn16",))
                dma("sp", kc32[:, 0:64], I["ckr"][jn, t * 128:(t + 1) * 128, :], (), ("kc32",), "kc32")
                p.op("pe", lambda e: e.transpose(out=ps[1][0:64, 0:128], in_=kc32[:, 0:64], identity=ident32[:, :]), r=("kc32", "ident32"), w=(PK[1],))
                act(d32[0:64, NQ + NKV, t * 128:(t + 1) * 128], ps[1][0:64, 0:128], AF.Copy, r=(PK[1],), w=("d32",))
            kv_heads(PAST, TA, False, 0)
            p.barrier()
            TK = TS + PAST
            qn = a16(0, TS)
            qr = a16(TS, TS)
            kn = a16(2 * TS, TK)
            kr = a16(2 * TS + TK, TK)
            vh = a16(2 * TS + 2 * TK, TK).rearrange("p (n d) -> p n d", d=128)
            B1 = 2 * TS + 3 * TK
            ost = a16(B1, 1024).rearrange("p (s n) -> p s n", s=2)
            assert B1 + 1024 <= 30720
            oc = 0
            for sq_ in seqs:
                t0, Lq, smp = sq_["t0"], sq_["L"], sq_["smp"]
                for h in range(H):
                    dma("sp", qn[:, 0:Lq], SC_FM[QN0 + h * 128:QN0 + (h + 1) * 128, t0:t0 + Lq], (), ("qh",), "qh")
                    dma("sp", qr[0:64, 0:Lq], SC_FM[QR0 + h * 64:QR0 + (h + 1) * 64, t0:t0 + Lq], (), ("qh",), "qr")
                    dma("sp", kn[:, 0:Lq], SC_FM[KN0 + h * 128:KN0 + (h + 1) * 128, t0:t0 + Lq], (), ("kh",), "kh")
                    dma("sp", kr[0:64, 0:Lq], SC_FM[KR0 + h * 64:KR0 + (h + 1) * 64, t0:t0 + Lq], (), ("kh",), "kr")
                    dma("sp", vh[:, 0:Lq // 128, :], SC_TM[t0:t0 + Lq, h * 128:(h + 1) * 128].rearrange("(n p) d -> p n d", p=128), (), ("vh",), "vh0")
                    nkt = Lq // 128
                    if smp:
                        dma("sp", kn[:, Lq:Lq + PAST], SC_FM[KN0 + h * 128:KN0 + (h + 1) * 128, TA:TA + PAST], (), ("kh",), "kh")
                        dma("sp", kr[0:64, Lq:Lq + PAST], SC_FM[KR0 + h * 64:KR0 + (h + 1) * 64, TA:TA + PAST], (), ("kh",), "kr")
                        dma("sp", vh[:, Lq // 128:Lq // 128 + NPT, :], SC_TM[TA:TA + PAST, h * 128:(h + 1) * 128].rearrange("(n p) d -> p n d", p=128), (), ("vh",), "vh1")
                        nkt += NPT
                    for (qa, qn_) in plain_blocks(Lq):
                        kts = [dict(kT=[kn[:, t * 128:(t + 1) * 128], kr[0:64, t * 128:(t + 1) * 128]], v=vh[:, t, :], rk=("kh",), rv=("vh",)) for t in range(nkt)]
                        s = oc % 2
                        oc += 1
                        attend(kts, [qn[:, qa:qa + qn_], qr[0:64, qa:qa + qn_]], qn_, 128, ost[:, s, 0:qn_], (("ost", s),), ("qh",))
                        dma("sp", SC_OT[h * 128:(h + 1) * 128, t0 + qa:t0 + qa + qn_], ost[:, s, 0:qn_], (("ost", s),), (), ("ost", s))
            p.barrier()
            load_oT_and_project(sub, H, Wo)
            p.barrier()

        MIX[2] = mixer_mla

        def mixer_lru(l, jn, sub):
            sub_vectors(l, 0)
            Win, Wout = I["lru_w_in"][jn], I["lru_w_out"][jn]
            NLC = 2 * LRUB
            G0 = NLC * 128
            xst = a16(20000, 2048).rearrange("p (s n) -> p s n", s=4)
            for sq_ in seqs:
                t0 = sq_["t0"]
                for (a, n) in plain_blocks(sq_["L"]):
                    norm_block(sub, sq_, a, n)
                    for br in range(2):
                        for blk in range(LRUB):
                            wv, wk = load_w(Win, 0, KD, br * LW + blk * 168, 168)
                            for part, (c0, m) in enumerate(((0, 128), (128, 40))):
                                bank = part
                                for k in range(KD):
                                    mm(ps[bank][0:m, 0:n], wv(k, c0, c0 + m), hT[:, k, 0:n], k == 0, k == KD - 1, r=(wk, "hT"), w=(PK[bank],))
                                s = ctr["st"] % 4
                                ctr["st"] += 1
                                act(xst[0:m, s, 0:n], ps[bank][0:m, 0:n], AF.Copy, r=(PK[bank],), w=(("qst", s),))
                                row = br * G0 + (2 * blk + part) * 128
                                dma("sp", SC_FM[row:row + m, t0 + a:t0 + a + n], xst[0:m, s, 0:n], (("qst", s),), (), ("qst", s))
            p.barrier()
            LM = TS
            xb = a16(0, 2 * LM).rearrange("p (q n) -> p q n", q=2)
            xc = a16(2 * LM, 2 * LM).rearrange("p (q n) -> p q n", q=2)
            hsf = a32(2 * LM, LM)
            T0 = 6 * LM
            gw = a16(T0, 8 * 168).rearrange("p (q n) -> p q n", q=8)
            pp = a32((T0 + 1344) // 2, 2 * 13).rearrange("p (q n) -> p q n", q=2)
            sp_ = a32((T0 + 1344) // 2 + 32, 8)
            F0 = (T0 + 1344) // 2 + 64

            def f32t(i):
                return a32(F0 + i * 512, 512)
            ctmp, gr, gi, aa, a2, bx, hb, gt, gtmp = (f32t(i) for i in range(9))
            gate16 = a16(2 * (F0 + 9 * 512), 512)
            mst = a16(2 * (F0 + 9 * 512) + 512, 1024).rearrange("p (s n) -> p s n", s=2)
            assert 2 * (F0 + 9 * 512) + 1536 <= 38912
            oc = 0
            for sq_ in seqs:
                t0, Lq, smp = sq_["t0"], sq_["L"], sq_["smp"]
                sl = plain_blocks(Lq)
                for blk in range(LRUB):
                    for part, m in enumerate((128, 40)):
                        row = (2 * blk + part) * 128
                        dma("sp", xb[0:m, part, 0:Lq], SC_FM[row:row + m, t0:t0 + Lq], (), ("xb",), ("xb", part))
                    dma("sp", pp, I["lru_pp"][jn, :, 2 * blk:2 * blk + 2, :], (), ("pp",), "pp")
                    qi = 0
                    for W_ in (I["lru_w_a"], I["lru_w_i"]):
                        for d_ in range(2):
                            dma("pool", gw[:, qi, :], W_[jn, d_, blk, 0:128, :], (), ("gw",), ("gw", qi % 2))
                            dma("pool", gw[0:40, 4 + qi, :], W_[jn, d_, blk, 128:168, :], (), ("gw",), ("gw", qi % 2))
                            qi += 1
                    for part, m in enumerate((128, 40)):
                        for (ta, w_) in sl:
                            tb = ta + w_
                            act(ctmp[0:m, 0:w_], xb[0:m, part, ta:tb], AF.Identity, r=("xb", "pp"), w=("ctmp",), scale=pp[0:m, part, 2:3], bias=pp[0:m, part, 4:5])
                            for (tap, sh) in ((0, -2), (1, -1), (3, 1)):
                                lo_, hi_ = max(ta, -sh), min(tb, Lq - sh) if sh > 0 else tb
                                lo_ = max(lo_, ta)
                                if lo_ >= hi_:
                                    continue
                                dve(lambda e, m=m, part=part, tap=tap, sh=sh, lo_=lo_, hi_=hi_, ta=ta: e.scalar_tensor_tensor(
                                    out=ctmp[0:m, lo_ - ta:hi_ - ta], in0=xb[0:m, part, lo_ + sh:hi_ + sh], scalar=pp[0:m, part, tap:tap + 1],
                                    in1=ctmp[0:m, lo_ - ta:hi_ - ta], op0=ALU.mult, op1=ALU.add), r=("xb", "pp", "ctmp"), w=("ctmp",))
                            act(xc[0:m, part, ta:tb], ctmp[0:m, 0:w_], AF.Copy, r=("ctmp",), w=("xc",))
                    for part, (c0, m) in enumerate(((0, 128), (128, 40))):
                        cp_ = 2 * blk + part
                        for d_ in range(2):
                            act(sp_[0:m, d_:d_ + 1], pp[0:m, part, 9 + d_:10 + d_], AF.Exp, r=("pp",), w=("sp",), scale=-1.0)
                            dve(lambda e, m=m, d_=d_: e.tensor_scalar(out=sp_[0:m, d_:d_ + 1], in0=sp_[0:m, d_:d_ + 1], scalar1=1.0, scalar2=None, op0=ALU.add), r=("sp",), w=("sp",))
                            act(sp_[0:m, d_:d_ + 1], sp_[0:m, d_:d_ + 1], AF.Ln, r=("sp",), w=("sp",))
                            dve(lambda e, m=m, d_=d_: e.tensor_scalar(out=sp_[0:m, 2 + d_:3 + d_], in0=sp_[0:m, d_:d_ + 1], scalar1=-8.0, scalar2=None, op0=ALU.mult), r=("sp",), w=("sp",))
                            dve(lambda e, m=m, d_=d_: e.tensor_scalar(out=sp_[0:m, 4 + d_:5 + d_], in0=sp_[0:m, d_:d_ + 1], scalar1=-16.0, scalar2=None, op0=ALU.mult), r=("sp",), w=("sp",))
                        for d_ in range(2):
                            order = sl if d_ == 0 else sl[::-1]
                            for si, (ta, w_) in enumerate(order):
                                tb = ta + w_
                                for gsel, (gps, bcol) in enumerate(((0, 5 + d_), (1, 7 + d_))):
                                    qi = gsel * 2 + d_
                                    mm(ps[gps][0:m, 0:w_], gw[:, qi, c0:c0 + m], xc[:, 0, ta:tb], True, False, r=("gw", "xc"), w=(PK[gps],))
                                    mm(ps[gps][0:m, 0:w_], gw[0:40, 4 + qi, c0:c0 + m], xc[0:40, 1, ta:tb], False, True, r=("gw", "xc"), w=(PK[gps],))
                                act(gr[0:m, 0:w_], ps[0][0:m, 0:w_], AF.Sigmoid, r=(PK[0], "pp"), w=("gr",), bias=pp[0:m, part, 5 + d_:6 + d_])
                                act(gi[0:m, 0:w_], ps[1][0:m, 0:w_], AF.Sigmoid, r=(PK[1], "pp"), w=("gi",), bias=pp[0:m, part, 7 + d_:8 + d_])
                                act(aa[0:m, 0:w_], gr[0:m, 0:w_], AF.Exp, r=("gr", "sp"), w=("aa",), scale=sp_[0:m, 2 + d_:3 + d_])
                                act(a2[0:m, 0:w_], gr[0:m, 0:w_], AF.Exp, r=("gr", "sp"), w=("a2",), scale=sp_[0:m, 4 + d_:5 + d_])
                                dve(lambda e, m=m, w_=w_: e.tensor_scalar(out=a2[0:m, 0:w_], in0=a2[0:m, 0:w_], scalar1=-1.0, scalar2=1.0, op0=ALU.mult, op1=ALU.add),
                                    r=("a2",), w=("a2",))
                                act(a2[0:m, 0:w_], a2[0:m, 0:w_], AF.Sqrt, r=("a2",), w=("a2",))
                                dve(lambda e, m=m, w_=w_, part=part, ta=ta, tb=tb: e.tensor_tensor(out=bx[0:m, 0:w_], in0=gi[0:m, 0:w_], in1=xc[0:m, part, ta:tb], op=ALU.mult),
                                    r=("gi", "xc"), w=("bx",))
                                dve(lambda e, m=m, w_=w_: e.tensor_tensor(out=bx[0:m, 0:w_], in0=bx[0:m, 0:w_], in1=a2[0:m, 0:w_], op=ALU.mult), r=("bx", "a2"), w=("bx",))
                                if d_ == 0:
                                    init = (pp[0:m, part, 11:12] if smp else 0.0) if si == 0 else hsf[0:m, ta - 1:ta]
                                    dve(lambda e, m=m, w_=w_, ta=ta, tb=tb, init=init: e.tensor_tensor_scan(out=hsf[0:m, ta:tb], data0=aa[0:m, 0:w_], data1=bx[0:m, 0:w_],
                                                                                                      initial=init, op0=ALU.mult, op1=ALU.add),
                                        r=("aa", "bx", "pp", "hsf"), w=("hsf",))
                                    if (not smp) and tb == Lq:
                                        off = blk * 168 + c0
                                        dma("sp", O["lru_state"][sq_["pi"], jn, 0, off:off + m].rearrange("(p o) -> p o", o=1), hsf[0:m, Lq - 1:Lq], ("hsf",), (), "lst")
                                else:
                                    init = (pp[0:m, part, 12:13] if smp else 0.0) if si == 0 else hb[0:m, 0:1]
                                    if si > 0:
                                        dve(lambda e, m=m: e.tensor_scalar(out=gtmp[0:m, 0:1], in0=hb[0:m, 0:1], scalar1=1.0, scalar2=None, op0=ALU.mult), r=("hb",), w=("gtmp0",))
                                        init = gtmp[0:m, 0:1]
                                    dve(lambda e, m=m, w_=w_, init=init: e.tensor_tensor_scan(out=hb[0:m, 0:w_][:, ::-1], data0=aa[0:m, 0:w_][:, ::-1], data1=bx[0:m, 0:w_][:, ::-1],
                                                                                          initial=init, op0=ALU.mult, op1=ALU.add),
                                        r=("aa", "bx", "pp", "gtmp0"), w=("hb",))
                                    if (not smp) and ta == 0:
                                        off = blk * 168 + c0
                                        dma("sp", O["lru_state"][sq_["pi"], jn, 1, off:off + m].rearrange("(p o) -> p o", o=1), hb[0:m, 0:1], ("hb",), (), "lst")
                                    row = G0 + cp_ * 128
                                    dma("sp", gate16[0:m, 0:w_], SC_FM[row:row + m, t0 + ta:t0 + tb], (), ("gate16",), "gate16")
                                    dve(lambda e, m=m, w_=w_: e.tensor_tensor(out=gt[0:m, 0:w_], in0=gate16[0:m, 0:w_], in1=gate16[0:m, 0:w_], op=ALU.mult), r=("gate16",), w=("gt",))
                                    dve(lambda e, m=m, w_=w_: e.tensor_scalar(out=gt[0:m, 0:w_], in0=gt[0:m, 0:w_], scalar1=0.044715, scalar2=1.0, op0=ALU.mult, op1=ALU.add), r=("gt",), w=("gt",))
                                    dve(lambda e, m=m, w_=w_: e.tensor_tensor(out=gt[0:m, 0:w_], in0=gt[0:m, 0:w_], in1=gate16[0:m, 0:w_], op=ALU.mult), r=("gt", "gate16"), w=("gt",))
                                    act(gt[0:m, 0:w_], gt[0:m, 0:w_], AF.Sigmoid, r=("gt",), w=("gt",), scale=1.5957691216057308)
                                    dve(lambda e, m=m, w_=w_: e.tensor_tensor(out=gt[0:m, 0:w_], in0=gt[0:m, 0:w_], in1=gate16[0:m, 0:w_], op=ALU.mult), r=("gt", "gate16"), w=("gt",))
                                    dve(lambda e, m=m, w_=w_, ta=ta, tb=tb: e.tensor_tensor(out=ctmp[0:m, 0:w_], in0=hb[0:m, 0:w_], in1=hsf[0:m, ta:tb], op=ALU.add), r=("hb", "hsf"), w=("ctmp",))
                                    s = oc % 2
                                    oc += 1
                                    dve(lambda e, m=m, w_=w_, s=s: e.tensor_tensor(out=mst[0:m, s, 0:w_], in0=ctmp[0:m, 0:w_], in1=gt[0:m, 0:w_], op=ALU.mult), r=("ctmp", "gt"), w=(("ost", s),))
                                    dma("sp", SC_OT[cp_ * 128:cp_ * 128 + m, t0 + ta:t0 + tb], mst[0:m, s, 0:w_], (("ost", s),), (), ("ost", s))
            p.barrier()
            load_oT_and_project(sub, NLC, Wout, kparts=[128, 40] * LRUB)
            p.barrier()

        MIX[1] = mixer_lru

        cnt = [0, 0, 0, 0]
        try:
            for l in range(L):
                modulation(l)
                stop_at(1)
                kind = KINDS[l]
                MIX[kind](l, cnt[kind], 2 * l)
                cnt[kind] += 1
                stop_at(6)
                ffn(l, 2 * l + 1)
        except StopBuild:
            pass
        if C.get("DBGOUT"):
            p.barrier()
            dma("pool", O["dbg2"][:, :], SC_OT[128:256, 0:TS], (), (), "dbg2")
            dma("sp", O["dbg4"][:, :], xr[1][0:TS, :], (), (), "dbg4")
        p.barrier(final=True)
        print("ops per engine:", {e: len(p.q[e]) for e in ENGS}, "sems used:", len(p.semmap), flush=True)
        with nc.Block() as block:
            p.emit(block)
    return nc


def _fm(v, ):
    sh = v.shape
    return np.ascontiguousarray(np.swapaxes(v.reshape(sh[:-1] + (sh[-1] // 128, 128)), -1, -2))


def _rope_tables(n_tokens, rot_dim):
    t = np.arange(n_tokens)
    row = (t // GW).astype(np.float32)
    col = (t % GW).astype(np.float32)
    half = rot_dim // 2
    inv = (10000.0 ** (-np.arange(0, half, 2, dtype=np.float32) / half)).astype(np.float32)
    ar = row[:, None] * inv
    ac = col[:, None] * inv
    ang = np.concatenate([ar, ar, ac, ac], axis=-1)
    return np.cos(ang).astype(np.float32), np.sin(ang).astype(np.float32)


def _rope_consts(TS, rot_dim, reps):
    cos, sin = _rope_tables(TS, rot_dim)
    q = rot_dim // 4
    sign = np.concatenate([-np.ones(q), np.ones(q), -np.ones(q), np.ones(q)]).astype(np.float32)
    tab = np.stack([cos.T, (sin * sign[None, :]).T], axis=1)
    tab = np.concatenate([tab] * reps, axis=0)
    n = rot_dim * reps
    perm = np.zeros((n, n), np.float32)
    for m in range(n):
        b, i = divmod(m, rot_dim)
        hlf, ii = divmod(i, 2 * q)
        src = ii + q if ii < q else ii - q
        perm[b * rot_dim + hlf * 2 * q + src, m] = 1.0
    return np.ascontiguousarray(tab), perm


def _nat_bias_tables(rpb):
    H = rpb.shape[0]
    c = np.arange(GW)
    cstart = np.clip(c - 8, 0, GW - 16)
    cp = np.arange(GW)[:, None]
    inwin = (cp >= cstart[None, :]) & (cp < cstart[None, :] + 16)
    idx = np.clip(cp - c[None, :] + 15, 0, 30)
    B = np.where(inwin[None, None], rpb[:, :, idx], np.float32(NEGB)).astype(np.float32)
    lo = B[:, 0:14].transpose(0, 2, 1, 3)
    hi = B[:, 1:15].transpose(0, 2, 1, 3)
    return np.ascontiguousarray(np.concatenate([lo, hi], axis=1).reshape(H, 128, 14 * GW))


def prep_core(C, inp, i):
    D, PL = C["D"], C["P"]
    KINDS = C["KINDS"]
    m = {}
    m["xs"] = inp["x_sample"][i]
    m["xp"] = inp["x_prompt"][2 * i:2 * i + 2].reshape(2 * PL, D)
    m["condT"] = np.ascontiguousarray(np.stack([_fm(inp["c"][i]), _fm(inp["c_ctx"])], axis=-1))
    return m


def prep_shared(C, inp):
    D, PL, TS = C["D"], C["P"], C["TS"]
    KINDS = C["KINDS"]
    NK = [KINDS.count(k) for k in range(4)]
    m = {}
    m["w_mod"] = inp["w_mod"]
    m["b_mod"] = inp["b_mod"]
    m["bmodT"] = _fm(inp["b_mod"])
    m["nmixT"] = _fm(inp["norm_mix"])
    m["nffnT"] = _fm(inp["norm_ffn"])
    m["ffn_w_in"] = inp["ffn_w_in"]
    m["ffn_w_out"] = inp["ffn_w_out"]
    m["fcwT"] = np.ascontiguousarray(_fm(inp["ffn_conv_w"]).transpose(0, 2, 3, 1))
    m["fcbT"] = _fm(inp["ffn_conv_b"])
    m["ident"] = np.eye(128, dtype=np.float32)
    selc = np.zeros((2, 2, 128), np.float32)
    selc[0, 0, :] = 1.0
    selc[1, 1, :] = 1.0
    m["selc"] = selc
    if NK[0]:
        m["nat_w_qkv"] = inp["nat_w_qkv"]
        m["nat_w_o"] = inp["nat_w_o"]
        m["nat_gqk"] = np.ascontiguousarray(np.stack([inp["nat_q_norm"], inp["nat_k_norm"]], axis=-1))
        m["nat_bt"] = np.stack([_nat_bias_tables(inp["nat_rpb"][j]) for j in range(NK[0])])
    if NK[1]:
        LB = C["LRUB"]
        m["lru_w_in"] = inp["lru_w_in"]
        m["lru_w_out"] = inp["lru_w_out"]
        m["lru_w_a"] = inp["lru_w_a"]
        m["lru_w_i"] = inp["lru_w_i"]
    if NK[2]:
        m["mla_w_down"] = inp["mla_w_down"]
        m["mla_w_uq"] = inp["mla_w_uq"]
        m["mla_w_ukv"] = inp["mla_w_ukv"]
        m["mla_w_o"] = inp["mla_w_o"]
        m["mla_gaT"] = _fm(np.concatenate([inp["mla_q_a_norm"], inp["mla_kv_a_norm"]], axis=-1))
        gq, gk = inp["mla_q_norm"], inp["mla_k_norm"]
        pad = lambda v: np.concatenate([v, np.zeros((v.shape[0], 64), np.float32)], axis=-1)
        m["mla_gqk"] = np.ascontiguousarray(np.stack([gq[:, :128], pad(gq[:, 128:]), gk[:, :128], pad(gk[:, 128:])], axis=-1))
        m["rope64"], m["perm64"] = _rope_consts(TS, 64, 1)
    if NK[3]:
        m["swa_w_qkv"] = inp["swa_w_qkv"]
        m["swa_w_o"] = inp["swa_w_o"]
        gq, gk = inp["swa_q_norm"], inp["swa_k_norm"]
        m["swa_gqk"] = np.ascontiguousarray(np.stack([np.concatenate([gq, gq], -1), np.concatenate([gk, gk], -1)], axis=-1))
        m["swa_sinks"] = inp["swa_sinks"]
        m["rope128"], m["perm128"] = _rope_consts(TS, 64, 2)
        ii = np.arange(128)
        mk = np.zeros((128, 2, 128), np.float32)
        mk[:, 0, :] = np.where(ii[None, :] <= ii[:, None], 0.0, NEGB)
        mk[:, 1, :] = np.where(ii[:, None] <= ii[None, :], 0.0, NEGB)
        m["swamask"] = mk
    return m


def prep_core_mix(C, inp, i, m):
    KINDS = C["KINDS"]
    NK = [KINDS.count(k) for k in range(4)]
    if NK[0]:
        m["cnk"] = inp["cache_nat_k"][i]
        m["cnv"] = inp["cache_nat_v"][i]
    if NK[1]:
        LB = C["LRUB"]
        n = NK[1]

        def cp(v):
            v = v.reshape(n, LB, 168)
            out = np.zeros((n, 128, 2 * LB), np.float32)
            out[:, :, 0::2] = v[:, :, 0:128].transpose(0, 2, 1)
            out[:, 0:40, 1::2] = v[:, :, 128:168].transpose(0, 2, 1)
            return out
        cols = [cp(inp["lru_conv_w"][:, k]) for k in range(4)] + [cp(inp["lru_conv_b"])]
        cols += [cp(inp["lru_b_a"][:, 0]), cp(inp["lru_b_a"][:, 1]), cp(inp["lru_b_i"][:, 0]), cp(inp["lru_b_i"][:, 1])]
        cols += [cp(inp["lru_lambda"][:, 0]), cp(inp["lru_lambda"][:, 1])]
        cols += [cp(inp["state_lru"][i][:, 0]), cp(inp["state_lru"][i][:, 1])]
        m["lru_pp"] = np.ascontiguousarray(np.stack(cols, axis=-1))
    if NK[2]:
        m["cckv"] = inp["cache_mla_ckv"][i]
        m["ckr"] = inp["cache_mla_krope"][i]
    if NK[3]:
        m["csk"] = inp["cache_swa_k"][i]
        m["csv"] = inp["cache_swa_v"][i]
    return m


_NC_CACHE = {}


def run(C, inputs):
    key = repr(sorted(C.items()))
    if key not in _NC_CACHE:
        _NC_CACHE[key] = build(C)
    nc = _NC_CACHE[key]
    inp = {k: np.asarray(v) for k, v in inputs.items()}
    shared = prep_shared(C, inp)
    in_maps = []
    for i in range(8):
        m = dict(shared)
        m.update(prep_core(C, inp, i))
        prep_core_mix(C, inp, i, m)
        in_maps.append({k: np.ascontiguousarray(v, dtype=np.float32) for k, v in m.items()})
    res = run_bass_kernel_spmd(nc, in_maps, core_ids=list(range(8)))
    R = res.results
    if C.get("DBGOUT"):
        global DBG_R
        DBG_R = R
    D, PL, TS = C["D"], C["P"], C["TS"]
    KINDS = C["KINDS"]
    NK = [KINDS.count(k) for k in range(4)]
    cat = lambda name: np.concatenate([R[i][name] for i in range(8)], axis=0)
    yp = cat("yp").reshape(16, PL, D)
    ys = np.stack([R[i]["ys"] for i in range(8)], axis=0)
    outs = [yp, ys]
    z = lambda *s: np.zeros(s, np.float32)
    outs.append(cat("nat_k") if NK[0] else None)
    outs.append(cat("nat_v") if NK[0] else None)
    outs.append(cat("lru_state") if NK[1] else None)
    outs.append(cat("mla_ckv") if NK[2] else None)
    outs.append(cat("mla_krope") if NK[2] else None)
    outs.append(cat("swa_k") if NK[3] else None)
    outs.append(cat("swa_v") if NK[3] else None)
    return tuple(outs)


def kernel(**inputs):
    return run(default_cfg(), inputs)
```

```python
import numpy as np
from contextlib import ExitStack
import concourse.bass as bass
import concourse.mybir as mybir
from concourse.bass_utils import run_bass_kernel_spmd

F32 = mybir.dt.float32
BF = mybir.dt.bfloat16
AF = mybir.ActivationFunctionType
ALU = mybir.AluOpType
EPS = 1e-6
NEGB = -30000.0
GW = 64
ENGS = ("pe", "act", "dve", "pool", "sp")
WIN = 1 << 30
DWIN = 1 << 26


class Prog:
    def __init__(self, nc, sems):
        self.nc = nc
        self.free = list(sems)
        self.q = {e: [] for e in ENGS}
        self.lastw = {}
        self.rd_c = {}
        self.rd_d = {}
        self.waited = {}
        self.semmap = {}
        self.dcnt = {}
        self.dlast = {}
        self.cc = {e: 0 for e in ENGS}

    def _sem(self, key):
        s = self.semmap.get(key)
        if s is None:
            s = self.free.pop()
            self.semmap[key] = s
        return s

    def _target(self, ref):
        if ref[0] == "c":
            _, e, i = ref
            return self._sem(("c", e, i // WIN)), (i % WIN) + 1
        _, k, c = ref
        return self._sem(("d", k, (c - 1) // DWIN)), (((c - 1) % DWIN) + 1) * 16

    def op(self, eng, fn, r=(), w=(), dma=None):
        deps = set()
        for k in r:
            if k in self.lastw:
                deps.add(self.lastw[k])
        for k in w:
            if k in self.lastw:
                deps.add(self.lastw[k])
            for e, i in self.rd_c.get(k, {}).items():
                deps.add(("c", e, i))
            for d in self.rd_d.get(k, ()):
                deps.add(d)
        if dma is not None and dma in self.dlast:
            deps.add(self.dlast[dma])
        idx = self.cc[eng]
        if dma is None:
            ref = ("c", eng, idx)
            self.cc[eng] += 1
        else:
            c = self.dcnt.get(dma, 0) + 1
            self.dcnt[dma] = c
            ref = ("d", dma, c)
            self.dlast[dma] = ref
        waits = []
        for d in deps:
            if d[0] == "c":
                if d[1] == eng and eng == "pe":
                    continue
                wk = (eng, "c", d[1])
                if self.waited.get(wk, -1) >= d[2]:
                    continue
                self.waited[wk] = d[2]
            else:
                wk = (eng, "d", d[1])
                if self.waited.get(wk, 0) >= d[2]:
                    continue
                self.waited[wk] = d[2]
            waits.append(self._target(d))
        self.q[eng].append((fn, waits, ref, dma is not None))
        for k in r:
            if dma is None:
                self.rd_c.setdefault(k, {})[eng] = idx
            else:
                self.rd_d.setdefault(k, []).append(ref)
        for k in w:
            self.lastw[k] = ref
            self.rd_c[k] = {}
            self.rd_d[k] = []
        return ref

    def barrier(self, final=False):
        refs = []
        for e in ENGS:
            if self.cc[e] > 0:
                refs.append(("c", e, self.cc[e] - 1))
        for k, c in self.dcnt.items():
            refs.append(("d", k, c))
        engs = ("sp",) if final else ENGS
        for e in engs:
            waits = []
            for d in refs:
                if d[0] == "c":
                    if d[1] == e and e == "pe":
                        continue
                    wk = (e, "c", d[1])
                    if self.waited.get(wk, -1) >= d[2]:
                        continue
                    self.waited[wk] = d[2]
                else:
                    wk = (e, "d", d[1])
                    if self.waited.get(wk, 0) >= d[2]:
                        continue
                    self.waited[wk] = d[2]
                waits.append(self._target(d))
            idx = self.cc[e]
            self.cc[e] += 1
            self.q[e].append((lambda en: en.nop(), waits, ("c", e, idx), False))
        if not final:
            self.lastw.clear()
            self.rd_c.clear()
            self.rd_d.clear()

    def emit(self, block):
        decos = {"pe": block.tensor, "act": block.scalar, "dve": block.vector,
                 "pool": block.gpsimd, "sp": block.sync}
        for e in ENGS:
            ops = self.q[e]

            def body(en, ops=ops):
                for fn, waits, ref, isd in ops:
                    for s, v in waits:
                        en.wait_ge(s, v)
                    ins = fn(en)
                    s, v = self._target(ref)
                    ins.then_inc(s, 16 if isd else 1)
            decos[e](body)


class StopBuild(Exception):
    pass


def default_cfg():
    return dict(D=2048, TS=4096, P=256, PAST=256, DFF=5632, KINDS=[0, 1, 2, 3],
                NAH=16, LRUB=16, MLAH=16, QR=768, KVR=512, SWAH=32, SWAKV=4)


def ffn_blocks(L):
    if L <= 512:
        return [(0, L, 0, L)]
    starts = list(range(0, L - 512, 510)) + [L - 512]
    out, cur = [], 0
    for s in starts:
        hi = s + 511 if s + 512 < L else L
        out.append((s, 512, cur, hi))
        cur = hi
    return out


def plain_blocks(L):
    return [(s, min(512, L - s)) for s in range(0, L, 512)]


def build(C):
    nc = bass.Bass("TRN2", target_bir_lowering=False)
    D, TS, PL, PAST, DFF = C["D"], C["TS"], C["P"], C["PAST"], C["DFF"]
    KINDS = C["KINDS"]
    L = len(KINDS)
    KD = D // 128
    NF = DFF // 128
    TA = TS + 2 * PL
    NAH, LRUB, MLAH, QR, KVR, SWAH, SWAKV = (C[k] for k in ("NAH", "LRUB", "MLAH", "QR", "KVR", "SWAH", "SWAKV"))
    LW = LRUB * 168
    NKIND = [KINDS.count(k) for k in range(4)]
    ROWS = TS // GW
    NPT = PAST // 128
    NC5 = min(512, D)

    def din(name, shape, dt=F32):
        return nc.dram_tensor(name, list(shape), dt, kind="ExternalInput").ap()

    def dout(name, shape):
        return nc.dram_tensor(name, list(shape), F32, kind="ExternalOutput").ap()

    def dscr(name, shape, dt):
        return nc.dram_tensor(name, list(shape), dt, kind="Internal").ap()

    I = {}
    I["xs"] = din("xs", [TS, D])
    I["xp"] = din("xp", [2 * PL, D])
    I["condT"] = din("condT", [128, KD, 2])
    I["w_mod"] = din("w_mod", [L, D, 6 * D])
    I["b_mod"] = din("b_mod", [L, 6 * D])
    I["bmodT"] = din("bmodT", [L, 128, 6 * KD])
    I["nmixT"] = din("nmixT", [L, 128, KD])
    I["nffnT"] = din("nffnT", [L, 128, KD])
    I["ffn_w_in"] = din("ffn_w_in", [L, D, 2 * DFF])
    I["ffn_w_out"] = din("ffn_w_out", [L, DFF, D])
    I["fcwT"] = din("fcwT", [L, 128, NF, 3])
    I["fcbT"] = din("fcbT", [L, 128, NF])
    I["ident"] = din("ident", [128, 128])
    I["selc"] = din("selc", [2, 2, 128])
    O = {}
    O["ys"] = dout("ys", [TS, D])
    O["yp"] = dout("yp", [2 * PL, D])
    if NKIND[0]:
        n = NKIND[0]
        I["nat_w_qkv"] = din("nat_w_qkv", [n, D, 3 * NAH * 128])
        I["nat_w_o"] = din("nat_w_o", [n, NAH * 128, D])
        I["nat_gqk"] = din("nat_gqk", [n, 128, 2])
        I["nat_bt"] = din("nat_bt", [n, NAH, 128, 14 * 64])
        I["cnk"] = din("cnk", [n, NAH, PAST, 128])
        I["cnv"] = din("cnv", [n, NAH, PAST, 128])
        O["nat_k"] = dout("nat_k", [2, n, NAH, PL, 128])
        O["nat_v"] = dout("nat_v", [2, n, NAH, PL, 128])
    if NKIND[1]:
        n = NKIND[1]
        NLC = 2 * LRUB
        I["lru_w_in"] = din("lru_w_in", [n, D, 2 * LW])
        I["lru_w_out"] = din("lru_w_out", [n, LW, D])
        I["lru_w_a"] = din("lru_w_a", [n, 2, LRUB, 168, 168])
        I["lru_w_i"] = din("lru_w_i", [n, 2, LRUB, 168, 168])
        I["lru_pp"] = din("lru_pp", [n, 128, NLC, 13])
        O["lru_state"] = dout("lru_state", [2, n, 2, LW])
    if NKIND[2]:
        n = NKIND[2]
        I["mla_w_down"] = din("mla_w_down", [n, D, QR + KVR + 64])
        I["mla_w_uq"] = din("mla_w_uq", [n, QR, MLAH * 192])
        I["mla_w_ukv"] = din("mla_w_ukv", [n, KVR, MLAH * 256])
        I["mla_w_o"] = din("mla_w_o", [n, MLAH * 128, D])
        I["mla_gaT"] = din("mla_gaT", [n, 128, (QR + KVR) // 128])
        I["mla_gqk"] = din("mla_gqk", [n, 128, 4])
        I["cckv"] = din("cckv", [n, PAST, KVR])
        I["ckr"] = din("ckr", [n, PAST, 64])
        I["rope64"] = din("rope64", [64, 2, TS])
        I["perm64"] = din("perm64", [64, 64])
        O["mla_ckv"] = dout("mla_ckv", [2, n, PL, KVR])
        O["mla_krope"] = dout("mla_krope", [2, n, PL, 64])
    if NKIND[3]:
        n = NKIND[3]
        NQK = SWAH + 2 * SWAKV
        I["swa_w_qkv"] = din("swa_w_qkv", [n, D, NQK * 64])
        I["swa_w_o"] = din("swa_w_o", [n, SWAH * 64, D])
        I["swa_gqk"] = din("swa_gqk", [n, 128, 2])
        I["swa_sinks"] = din("swa_sinks", [n, SWAH])
        I["csk"] = din("csk", [n, SWAKV, PAST, 64])
        I["csv"] = din("csv", [n, SWAKV, PAST, 64])
        I["rope128"] = din("rope128", [128, 2, TS])
        I["perm128"] = din("perm128", [128, 128])
        I["swamask"] = din("swamask", [128, 2, 128])
        O["swa_k"] = dout("swa_k", [2, n, SWAKV, PL, 64])
        O["swa_v"] = dout("swa_v", [2, n, SWAKV, PL, 64])

    if C.get("DBGOUT"):
        O["dbg1"] = dout("dbg1", [128, TS])
        O["dbg2"] = dout("dbg2", [128, TS])
        O["dbg3"] = dout("dbg3", [128, TS])
        O["dbg4"] = dout("dbg4", [TS, D])
    xr = [dscr("xr0", [TA, D], F32), dscr("xr1", [TA, D], F32)]
    SC_FM = dscr("sc_fm", [12288, TA + PAST], BF)
    SC_TM = dscr("sc_tm", [TA + PAST, 4096], BF)
    SC_OT = dscr("sc_ot", [4096, TA], BF)

    seqs = [dict(t0=0, L=TS, cond=0, smp=True, pi=None),
            dict(t0=TS, L=PL, cond=1, smp=False, pi=0),
            dict(t0=TS + PL, L=PL, cond=1, smp=False, pi=1)]
    NSUB = 2 * L

    def src_rows(sub, sq, a, b):
        if sub == 0:
            return I["xs"][a:b, :] if sq["smp"] else I["xp"][sq["pi"] * PL + a: sq["pi"] * PL + b, :]
        return xr[sub % 2][sq["t0"] + a: sq["t0"] + b, :]

    def dst_rows(sub, sq, a, b):
        if sub == NSUB - 1:
            return O["ys"][a:b, :] if sq["smp"] else O["yp"][sq["pi"] * PL + a: sq["pi"] * PL + b, :]
        return xr[(sub + 1) % 2][sq["t0"] + a: sq["t0"] + b, :]

    with ExitStack() as es:
        def sb(name, shape, dt):
            return es.enter_context(nc.sbuf_tensor("sb_" + name, list(shape), dt))

        sems = [es.enter_context(nc.semaphore(f"s{i}")) for i in range(100)]
        p = Prog(nc, sems)
        ps = [es.enter_context(nc.psum_tensor(f"ps{i}", [128, 512], F32)) for i in range(8)]
        PK = [("ps", i) for i in range(8)]

        ident32 = sb("ident32", [128, 128], F32)
        identb = sb("identb", [128, 128], BF)
        ones_b = sb("ones_b", [128, 128], BF)
        ones2_b = sb("ones2_b", [128, 128], BF)
        sel = sb("sel", [2, 2, 128], F32)
        epsb = sb("epsb", [128, 1], F32)
        condT = sb("condT", [128, KD, 2], F32)
        scondT = sb("scondT", [128, KD, 2], BF)
        modT = sb("modT", [128, 6 * KD, 2], F32)
        bmodT = sb("bmodT_s", [128, 6 * KD], F32)
        gT = sb("gT", [128, 2, KD], F32)
        gmodT = sb("gmodT", [128, KD, 2], F32)
        shiftT = sb("shiftT", [128, KD, 2], F32)
        gate_bc = sb("gate_bc", [128, 2, D], F32)
        xt = sb("xt", [128, 2, D], F32)
        ss = sb("ss", [128, 8], F32)
        xn = sb("xn", [128, 4, D], BF)
        hT = sb("hT", [128, KD, 512], BF)
        wst = sb("wst", [128, 2, 16, 512], BF)
        xo = sb("xo", [128, 2, 512], F32)
        xo2 = sb("xo2", [128, 2, 512], F32)
        ARENA = sb("arena", [128, 40960], BF)
        ARENA32 = ARENA[:].bitcast(F32)

        wslot = [0]

        def stop_at(k):
            if C.get("STOP") == k:
                raise StopBuild()
        grow = ARENA32[0:2, 0:D]
        brow = ARENA32[0:2, D:2 * D]

        def a16(off, n):
            return ARENA[:, off:off + n]

        def a32(off, n):
            return ARENA32[:, off:off + n]

        def dma(eng, out, in_, r, w, key, **kw):
            if eng == "sp" and SWAPQ:
                eng = "pool"
            p.op(eng, lambda e: e.dma_start(out=out, in_=in_, **kw), r=r, w=w, dma=key)

        def load_w(Wap, r0, nk, c0, ncols, kpart=128):
            s = wslot[0] % 2
            wslot[0] += 1
            key = ("wst", s)
            src = Wap[r0:r0 + nk * kpart, c0:c0 + ncols].rearrange("(k p) c -> p k c", p=kpart)
            dst = wst[0:kpart, s, 0:nk, 0:ncols]
            p.op("sp" if SWAPQ else "pool", lambda e: e.dma_start(out=dst, in_=src), r=(), w=(key,), dma=key)
            return (lambda k, a, b: wst[0:kpart, s, k, a:b]), key

        def mm(out, lhsT, rhs, start, stop, r, w):
            p.op("pe", lambda e: e.matmul(out, lhsT=lhsT, rhs=rhs, start=start, stop=stop), r=r, w=w)

        def act(out, in_, func, r, w, **kw):
            p.op("act", lambda e: e.activation(out=out, in_=in_, func=func, **kw), r=r, w=w)

        def dve(fn, r, w):
            p.op("dve", fn, r=r, w=w)

        SWAPQ = bool(C.get("PRECAST", 1))
        if SWAPQ:
            wnames = ["w_mod", "ffn_w_in", "ffn_w_out", "nat_w_qkv", "nat_w_o", "lru_w_in", "lru_w_out",
                      "mla_w_down", "mla_w_uq", "mla_w_ukv", "mla_w_o", "swa_w_qkv", "swa_w_o"]
            nw = 0
            for wn_ in wnames:
                if wn_ not in I:
                    continue
                src = I[wn_]
                shp = list(src.shape)
                wb = dscr("wb_" + wn_, shp, BF)
                R_, C_ = shp[1], shp[2]
                for li in range(shp[0]):
                    for r0 in range(0, R_, 512):
                        r1 = min(R_, r0 + 512)
                        for c0 in range(0, C_, 2048):
                            c1 = min(C_, c0 + 2048)
                            k_ = ("wc", nw % 6)
                            nw += 1
                            p.op("pool", lambda e, o_=wb[li, r0:r1, c0:c1], i_=src[li, r0:r1, c0:c1]: e.dma_start(out=o_, in_=i_), r=(), w=(), dma=k_)
                I[wn_] = wb
            p.barrier()
        dma("sp", ident32[:], I["ident"][:, :], (), ("ident32",), "c0")
        dma("pool", identb[:], I["ident"][:, :], (), ("identb",), "c1")
        dma("sp", condT[:], I["condT"][:, :, :], (), ("condT",), "c2")
        p.op("dve", lambda e: e.memset(ones_b[:], 1.0), w=("ones",))
        p.op("dve", lambda e: e.memset(ones2_b[:], 0.0), w=("ones2",))
        p.op("dve", lambda e: e.memset(ones2_b[0:64, 0:64], 1.0), w=("ones2",))
        p.op("dve", lambda e: e.memset(ones2_b[64:128, 64:128], 1.0), w=("ones2",))
        dma("sp", sel[:], I["selc"][:, :, :], (), ("sel",), "c3")
        p.op("dve", lambda e: e.memset(epsb[:], EPS), w=("epsb",))
        act(scondT[:], condT[:], AF.Silu, r=("condT",), w=("scondT",))

        def modulation(l):
            dma("sp", bmodT[:], I["bmodT"][l], (), ("bmodT",), "bm")
            nj = 6 * KD
            for g in range(0, nj, 4):
                wv, wk = load_w(I["w_mod"][l], 0, KD, g * 128, 512)
                for jj in range(4):
                    jk = g + jj
                    bank = jk % 2
                    for k in range(KD):
                        mm(ps[bank][:, 0:2], wv(k, jj * 128, (jj + 1) * 128), scondT[:, k, :], k == 0, k == KD - 1,
                           r=(wk, "scondT"), w=(PK[bank],))
                    dve(lambda e, jk=jk, bank=bank: e.tensor_scalar(out=modT[:, jk, :], in0=ps[bank][:, 0:2],
                                                                     scalar1=bmodT[:, jk:jk + 1], scalar2=None, op0=ALU.add),
                        r=(PK[bank], "bmodT"), w=("modT",))

        def sub_vectors(l, half):
            j0 = 3 * half
            dma("sp", gT[:, half, :], (I["nmixT"] if half == 0 else I["nffnT"])[l], (), ("gT",), "gt")
            for c in range(2):
                dve(lambda e, c=c: e.scalar_tensor_tensor(out=gmodT[:, :, c], in0=modT[:, (j0 + 1) * KD:(j0 + 2) * KD, c],
                                                          scalar=1.0, in1=gT[:, half, :], op0=ALU.add, op1=ALU.mult),
                    r=("modT", "gT"), w=("gmodT",))
                dve(lambda e, c=c: e.tensor_copy(out=shiftT[:, :, c], in_=modT[:, j0 * KD:(j0 + 1) * KD, c]),
                    r=("modT",), w=("shiftT",))
            jg = j0 + 2
            dma("sp", brow[:], I["b_mod"][l:l + 1, jg * D:(jg + 1) * D].to_broadcast([2, D]), (), ("brow",), "br")
            for n0 in range(0, D, NC5):
                wv, wk = load_w(I["w_mod"][l], 0, KD, jg * D + n0, NC5)
                for k in range(KD):
                    mm(ps[2][0:2, 0:NC5], scondT[:, k, :], wv(k, 0, NC5), k == 0, k == KD - 1, r=(wk, "scondT"), w=(PK[2],))
                dve(lambda e, n0=n0: e.tensor_tensor(out=grow[:, n0:n0 + NC5], in0=ps[2][0:2, 0:NC5], in1=brow[:, n0:n0 + NC5], op=ALU.add),
                    r=(PK[2], "brow"), w=("grow",))
            for c in range(2):
                for n0 in range(0, D, NC5):
                    bank = 3 + (n0 // NC5) % 2
                    mm(ps[bank][:, 0:NC5], sel[:, c, :], grow[:, n0:n0 + NC5], True, True, r=("sel", "grow"), w=(PK[bank],))
                    act(gate_bc[:, c, n0:n0 + NC5], ps[bank][:, 0:NC5], AF.Copy, r=(PK[bank],), w=("gate_bc",))
            p.barrier()

        def norm_block(sub, sq, a, n):
            nt = n // 128
            c = sq["cond"]
            for j in range(nt):
                s = j % 2
                dma("sp", xt[:, s, :], src_rows(sub, sq, a + j * 128, a + (j + 1) * 128), (), (("xt", s),), ("xt", s))
                act(xn[:, j, :], xt[:, s, :], AF.Square, r=(("xt", s),), w=(("xn", j), ("ss", j)), accum_out=ss[:, j:j + 1])
                dve(lambda e, j=j: e.tensor_scalar(out=ss[:, j:j + 1], in0=ss[:, j:j + 1], scalar1=1.0 / D, scalar2=EPS,
                                                    op0=ALU.mult, op1=ALU.add), r=(("ss", j),), w=(("ss", j),))
                act(ss[:, j:j + 1], ss[:, j:j + 1], AF.Sqrt, r=(("ss", j),), w=(("ss", j),))
                dve(lambda e, j=j: e.reciprocal(out=ss[:, j:j + 1], in_=ss[:, j:j + 1]), r=(("ss", j),), w=(("ss", j),))
                act(xn[:, j, :], xt[:, s, :], AF.Copy, r=(("xt", s), ("ss", j)), w=(("xn", j),), scale=ss[:, j:j + 1])
            psT = ps[4][:].bitcast(BF)
            for k in range(KD):
                for j in range(nt):
                    p.op("pe", lambda e, k=k, j=j: e.transpose(out=psT[:, j * 128:(j + 1) * 128], in_=xn[:, j, k * 128:(k + 1) * 128],
                                                              identity=identb[:]),
                         r=(("xn", j), "identb"), w=(PK[4],))
                act(hT[:, k, 0:n], psT[:, 0:n], AF.Identity, r=(PK[4], "gmodT", "shiftT"), w=("hT",),
                    scale=gmodT[:, k, c:c + 1], bias=shiftT[:, k, c:c + 1])

        ocnt = [0]

        def resid_out(sub, sq, a, j, n0, ncols, pbank, lo, hi):
            r0 = a + j * 128
            va, vb = max(lo, r0), min(hi, r0 + 128)
            if va >= vb:
                return
            s = ocnt[0] % 2
            ocnt[0] += 1
            c = sq["cond"]
            dma("sp", xo2[:, s, 0:ncols], src_rows(sub, sq, r0, r0 + 128)[:, n0:n0 + ncols], (), (("xo2", s),), ("xo2", s))
            dve(lambda e: e.tensor_tensor(out=xo[:, s, 0:ncols], in0=ps[pbank][:, 0:ncols], in1=gate_bc[:, c, n0:n0 + ncols], op=ALU.mult),
                r=(PK[pbank], "gate_bc"), w=(("xo", s),))
            dve(lambda e: e.tensor_tensor(out=xo[:, s, 0:ncols], in0=xo[:, s, 0:ncols], in1=xo2[:, s, 0:ncols], op=ALU.add),
                r=(("xo", s), ("xo2", s)), w=(("xo", s),))
            dma("sp", dst_rows(sub, sq, va, vb)[:, n0:n0 + ncols], xo[va - r0:vb - r0, s, 0:ncols], (("xo", s),), (), ("xo", s))

        def out_proj(sub, sq, a, n, lhs_fn, nk, Wap, kparts=None, lo=None, hi=None):
            nt = n // 128
            lo = a if lo is None else lo
            hi = a + n if hi is None else hi
            kparts = kparts or [128] * nk
            roff = [0]
            for kp in kparts:
                roff.append(roff[-1] + kp)
            uniform = all(kp == 128 for kp in kparts)
            for n0 in range(0, D, NC5):
                if uniform:
                    groups = [(g, min(16, nk - g)) for g in range(0, nk, 16)]
                    for gi, (g0, gn) in enumerate(groups):
                        wv, wk = load_w(Wap, g0 * 128, gn, n0, NC5)
                        for j in range(nt):
                            for kk in range(gn):
                                k = g0 + kk
                                l_ap, l_keys = lhs_fn(k, j)
                                mm(ps[4 + j][:, 0:NC5], l_ap, wv(kk, 0, NC5), k == 0, k == nk - 1, r=(wk,) + tuple(l_keys), w=(PK[4 + j],))
                else:
                    for k in range(nk):
                        kp = kparts[k]
                        wv, wk = load_w(Wap, roff[k], 1, n0, NC5, kpart=kp)
                        for j in range(nt):
                            l_ap, l_keys = lhs_fn(k, j)
                            mm(ps[4 + j][:, 0:NC5], l_ap, wv(0, 0, NC5), k == 0, k == nk - 1, r=(wk,) + tuple(l_keys), w=(PK[4 + j],))
                for j in range(nt):
                    resid_out(sub, sq, a, j, n0, NC5, 4 + j, lo, hi)

        def ffn(l, sub):
            sub_vectors(l, 1)
            fcw = a32(0, NF * 3)
            fcb = a32(NF * 3, NF)
            dma("sp", fcw, I["fcwT"][l].rearrange("p f k -> p (f k)"), (), ("fcw",), "fcw")
            dma("sp", fcb, I["fcbT"][l], (), ("fcb",), "fcb")
            fcw = fcw.rearrange("p (f k) -> p f k", k=3)
            ctmp = a32(256, 1024).rearrange("p (s n) -> p s n", s=2)
            sil = a32(1280, 1024).rearrange("p (s n) -> p s n", s=2)
            c2 = a32(2304, 1024).rearrange("p (s n) -> p s n", s=2)
            uT = a16(8192, NF * 512).rearrange("p (f n) -> p f n", f=NF)
            it = [0]
            for sq in seqs:
                for (a, n, lo, hi) in ffn_blocks(sq["L"]):
                    norm_block(sub, sq, a, n)
                    for g in range(0, NF, 4):
                        gn = min(4, NF - g)
                        wa, wak = load_w(I["ffn_w_in"][l], 0, KD, g * 128, gn * 128)
                        wb, wbk = load_w(I["ffn_w_in"][l], 0, KD, DFF + g * 128, gn * 128)
                        for fc in range(gn):
                            f = g + fc
                            s = it[0] % 2
                            it[0] += 1
                            pa, pb = ps[s], ps[2 + s]
                            for k in range(KD):
                                mm(pa[:, 0:n], wa(k, fc * 128, (fc + 1) * 128), hT[:, k, 0:n], k == 0, k == KD - 1, r=(wak, "hT"), w=(PK[s],))
                            for k in range(KD):
                                mm(pb[:, 0:n], wb(k, fc * 128, (fc + 1) * 128), hT[:, k, 0:n], k == 0, k == KD - 1, r=(wbk, "hT"), w=(PK[2 + s],))
                            ck = ("ctmp", s)
                            if C.get("DBG") == 3:
                                dve(lambda e, s=s, f=f, pa=pa, n=n: e.tensor_scalar(out=ctmp[:, s, 0:n], in0=pa[:, 0:n], scalar1=fcw[:, f, 1:2], scalar2=fcb[:, f:f + 1],
                                                                              op0=ALU.mult, op1=ALU.add), r=(PK[s], "fcw", "fcb"), w=(ck,))
                            else:
                                act(ctmp[:, s, 0:n], pa[:, 0:n], AF.Identity, r=(PK[s], "fcw", "fcb"), w=(ck,),
                                    scale=fcw[:, f, 1:2], bias=fcb[:, f:f + 1])
                            if C.get('DBG') not in (2, 3):
                                c2k = ("c2", s)
                                dve(lambda e, s=s, f=f, pa=pa, n=n: e.scalar_tensor_tensor(out=c2[:, s, 1:n], in0=pa[:, 0:n - 1], scalar=fcw[:, f, 0:1],
                                                                                       in1=ctmp[:, s, 1:n], op0=ALU.mult, op1=ALU.add),
                                    r=(PK[s], ck, "fcw"), w=(c2k,))
                                act(c2[:, s, 0:1], ctmp[:, s, 0:1], AF.Copy, r=(ck,), w=(c2k,))
                                dve(lambda e, s=s, f=f, pa=pa, n=n: e.scalar_tensor_tensor(out=ctmp[:, s, 0:n - 1], in0=pa[:, 1:n], scalar=fcw[:, f, 2:3],
                                                                                       in1=c2[:, s, 0:n - 1], op0=ALU.mult, op1=ALU.add),
                                    r=(PK[s], c2k, "fcw"), w=(ck,))
                                act(ctmp[:, s, n - 1:n], c2[:, s, n - 1:n], AF.Copy, r=(c2k,), w=(ck,))
                            act(sil[:, s, 0:n], ctmp[:, s, 0:n], AF.Silu, r=(ck,), w=(("sil", s),))
                            dve(lambda e, s=s, f=f, pb=pb, n=n: e.tensor_tensor(out=uT[:, f, 0:n], in0=pb[:, 0:n], in1=sil[:, s, 0:n], op=ALU.mult),
                                r=(PK[2 + s], ("sil", s)), w=(("uT", f),))
                    if C.get("DBGOUT") and sq["smp"] and a == 0:
                        dma("pool", O["dbg1"][:, 0:512], uT[:, 0, 0:512], (("uT", 0),), (), "dbg1")
                        dma("pool", O["dbg3"][:, 0:512], hT[:, 0, 0:512], ("hT",), (), "dbg3")
                    out_proj(sub, sq, a, n, lambda k, j: (uT[:, k, j * 128:(j + 1) * 128], (("uT", k),)), NF, I["ffn_w_out"][l], lo=lo, hi=hi)
            p.barrier()

        MIX = {}
        SQ_OFF, RS_OFF, E_OFF, RD_OFF, S32_OFF = 38912, 19968, 36864, 17920, 15360
        ctr = dict(att=0, s=0, st=0)

        def qk_norm(parts, ones_l, tot, nq, pbank):
            sq = a16(SQ_OFF, 1024)
            for i, (pap, pk, m, oap, okeys, gcol, gk) in enumerate(parts):
                act(sq[0:m, i * 512:i * 512 + nq], pap, AF.Square, r=(pk,), w=(("sq", i),))
            for i, (pap, pk, m, oap, okeys, gcol, gk) in enumerate(parts):
                mm(ps[pbank][:, 0:nq], ones_l[i], sq[0:m, i * 512:i * 512 + nq], i == 0, i == len(parts) - 1,
                   r=(("sq", i), "ones", "ones2"), w=(PK[pbank],))
            rs = a32(RS_OFF, 512)
            act(rs[:, 0:nq], ps[pbank][:, 0:nq], AF.Ln, r=(PK[pbank], "epsb"), w=("rs",), scale=1.0 / tot, bias=epsb[:, 0:1])
            act(rs[:, 0:nq], rs[:, 0:nq], AF.Exp, r=("rs",), w=("rs",), scale=-0.5)
            for i, (pap, pk, m, oap, okeys, gcol, gk) in enumerate(parts):
                dve(lambda e, pap=pap, oap=oap, gcol=gcol, m=m: e.scalar_tensor_tensor(out=oap, in0=pap, scalar=gcol, in1=rs[0:m, 0:nq],
                                                                                   op0=ALU.mult, op1=ALU.mult),
                    r=(pk, "rs") + tuple(gk), w=tuple(okeys))

        def attend(ktiles, qparts, nq, Mv, out_ap, out_keys, rq, sink_ap=None, sink_keys=()):
            it = ctr["att"]
            ctr["att"] += 1
            ob, db = 4 + it % 2, 6 + it % 2
            E = a16(E_OFF, 2048).rearrange("p (s n) -> p s n", s=4)
            S32 = a32(S32_OFF, 2048).rearrange("p (s n) -> p s n", s=4)
            gsz = max(1, 512 // nq)
            nt = len(ktiles)
            for g0 in range(0, nt, gsz):
                grp = ktiles[g0:g0 + gsz]
                sbk = ctr["s"] % 4
                ctr["s"] += 1
                for ti, kt in enumerate(grp):
                    c0, c1 = ti * nq, (ti + 1) * nq
                    nparts = len(kt["kT"])
                    for i, kT in enumerate(kt["kT"]):
                        mm(ps[sbk][:, c0:c1], kT, qparts[i], i == 0, i == nparts - 1, r=tuple(kt["rk"]) + tuple(rq), w=(PK[sbk],))
                    if kt.get("bias") is not None:
                        dve(lambda e, sbk=sbk, c0=c0, c1=c1, b=kt["bias"]: e.tensor_tensor(out=S32[:, sbk, c0:c1], in0=ps[sbk][:, c0:c1], in1=b, op=ALU.add),
                            r=(PK[sbk],) + tuple(kt["rb"]), w=(("S32", sbk),))
                ti = 0
                while ti < len(grp):
                    hb = grp[ti].get("bias") is not None
                    tj = ti
                    while tj < len(grp) and (grp[tj].get("bias") is not None) == hb:
                        tj += 1
                    c0, c1 = ti * nq, tj * nq
                    if hb:
                        act(E[:, sbk, c0:c1], S32[:, sbk, c0:c1], AF.Exp, r=(("S32", sbk),), w=(("E", sbk),))
                    else:
                        act(E[:, sbk, c0:c1], ps[sbk][:, c0:c1], AF.Exp, r=(PK[sbk],), w=(("E", sbk),))
                    ti = tj
                for ti, kt in enumerate(grp):
                    t = g0 + ti
                    c0, c1 = ti * nq, (ti + 1) * nq
                    mm(ps[ob][0:Mv, 0:nq], kt["v"], E[:, sbk, c0:c1], t == 0, t == nt - 1, r=(("E", sbk),) + tuple(kt["rv"]), w=(PK[ob],))
                    mm(ps[db][0:Mv, 0:nq], ones_b[:, 0:Mv], E[:, sbk, c0:c1], t == 0, t == nt - 1, r=(("E", sbk), "ones"), w=(PK[db],))
            rden = a32(RD_OFF, 512)
            if sink_ap is not None:
                dve(lambda e: e.tensor_tensor(out=rden[0:Mv, 0:nq], in0=ps[db][0:Mv, 0:nq], in1=sink_ap, op=ALU.add),
                    r=(PK[db],) + tuple(sink_keys), w=("rden",))
                dve(lambda e: e.reciprocal(out=rden[0:Mv, 0:nq], in_=rden[0:Mv, 0:nq]), r=("rden",), w=("rden",))
            else:
                dve(lambda e: e.reciprocal(out=rden[0:Mv, 0:nq], in_=ps[db][0:Mv, 0:nq]), r=(PK[db],), w=("rden",))
            dve(lambda e: e.tensor_tensor(out=out_ap, in0=ps[ob][0:Mv, 0:nq], in1=rden[0:Mv, 0:nq], op=ALU.mult),
                r=(PK[ob], "rden"), w=tuple(out_keys))

        def transpose_out32(src32, srck, m, ntok_tiles, dst_fn, bank=3):
            tst = a32(17400, 256).rearrange("p (s n) -> p s n", s=2)
            for jt in range(ntok_tiles):
                s = ctr["st"] % 2
                ctr["st"] += 1
                p.op("pe", lambda e, jt=jt: e.transpose(out=ps[bank][:, 0:m], in_=src32[0:m, jt * 128:(jt + 1) * 128], identity=ident32[0:m, 0:m]),
                     r=(srck, "ident32"), w=(PK[bank],))
                act(tst[:, s, 0:m], ps[bank][:, 0:m], AF.Copy, r=(PK[bank],), w=(("tst", s),))
                d_ = dst_fn(jt)
                if isinstance(d_, list):
                    for (dap, c0, c1) in d_:
                        dma("sp", dap, tst[:, s, c0:c1], (("tst", s),), (), ("tst", s))
                else:
                    dma("sp", d_, tst[:, s, 0:m], (("tst", s),), (), ("tst", s))

        def load_oT_and_project(sub, nk, Wap, kparts=None):
            for sq_ in seqs:
                t0 = sq_["t0"]
                for (a, n) in plain_blocks(sq_["L"]):
                    if kparts is None:
                        oTb = a16(0, nk * 512).rearrange("p (k n) -> p k n", k=nk)
                        dma("sp", oTb[:, :, 0:n], SC_OT[0:nk * 128, t0 + a:t0 + a + n].rearrange("(k p) t -> p k t", p=128), (), ("oTb",), "oTb")
                    else:
                        oTb = a16(0, nk * 512).rearrange("p (k n) -> p k n", k=nk)
                        for k in range(nk):
                            dma("sp", oTb[0:kparts[k], k, 0:n], SC_OT[k * 128:k * 128 + kparts[k], t0 + a:t0 + a + n], (), ("oTb",), ("oTb", k % 4))
                    kp = kparts or [128] * nk
                    out_proj(sub, sq_, a, n, lambda k, j, oTb=oTb, kp=kp: (oTb[0:kp[k], k, j * 128:(j + 1) * 128], ("oTb",)), nk, Wap, kparts=kparts)

        def mixer_nat(l, jn, sub):
            sub_vectors(l, 0)
            stop_at(2)
            Wq, Wo = I["nat_w_qkv"][jn], I["nat_w_o"][jn]
            H = NAH
            HD = H * 128
            gq = a32(13300, 4)
            dma("sp", gq[:, 0:2], I["nat_gqk"][jn], (), ("gq",), "gq")
            dve(lambda e: e.tensor_scalar(out=gq[:, 2:3], in0=gq[:, 0:1], scalar1=128.0 ** -0.5, scalar2=None, op0=ALU.mult), r=("gq",), w=("gq",))
            qst = a16(20000, 2048).rearrange("p (s n) -> p s n", s=4)
            k32 = a32(11100, 512)
            vst = a16(23300, 1024).rearrange("p (s n) -> p s n", s=2)
            v32 = a32(12200, 1024).rearrange("p (s n) -> p s n", s=2)
            for sq_ in seqs:
                t0 = sq_["t0"]
                for (a, n) in plain_blocks(sq_["L"]):
                    norm_block(sub, sq_, a, n)
                    stop_at(21)
                    for which in (0, 1):
                        for g0 in range(0, H, 4):
                            gh = min(4, H - g0)
                            wv, wk = load_w(Wq, 0, KD, which * HD + g0 * 128, gh * 128)
                            for hh in range(gh):
                                h = g0 + hh
                                bank = hh % 2
                                for k in range(KD):
                                    mm(ps[bank][:, 0:n], wv(k, hh * 128, (hh + 1) * 128), hT[:, k, 0:n], k == 0, k == KD - 1, r=(wk, "hT"), w=(PK[bank],))
                                s = ctr["st"] % 4
                                ctr["st"] += 1
                                gcol = gq[:, 2:3] if which == 0 else gq[:, 1:2]
                                if which == 1 and not sq_["smp"]:
                                    qk_norm([(ps[bank][:, 0:n], PK[bank], 128, k32[:, 0:n], ("k32",), gcol, ("gq",))], [ones_b[:, :]], 128, n, 2)
                                    dve(lambda e, s=s, n=n: e.tensor_copy(out=qst[:, s, 0:n], in_=k32[:, 0:n]), r=("k32",), w=(("qst", s),))
                                    transpose_out32(k32, "k32", 128, n // 128,
                                                    lambda jt, h=h, a=a, pi=sq_["pi"]: O["nat_k"][pi, jn, h, a + jt * 128:a + (jt + 1) * 128, :])
                                else:
                                    qk_norm([(ps[bank][:, 0:n], PK[bank], 128, qst[:, s, 0:n], (("qst", s),), gcol, ("gq",))], [ones_b[:, :]], 128, n, 2)
                                stop_at(22)
                                dma("sp", SC_FM[which * HD + h * 128:which * HD + (h + 1) * 128, t0 + a:t0 + a + n], qst[:, s, 0:n],
                                    (("qst", s),), (), ("qst", s))
                                stop_at(23)
                    stop_at(24)
                    if not sq_["smp"]:
                        stop_at(27)
                    for n0 in range(0, HD, 512):
                        ncol = min(512, HD - n0)
                        wv, wk = load_w(Wq, 0, KD, 2 * HD + n0, ncol)
                        for jt in range(n // 128):
                            bank = jt % 2
                            for k in range(KD):
                                mm(ps[bank][:, 0:ncol], hT[:, k, jt * 128:(jt + 1) * 128], wv(k, 0, ncol), k == 0, k == KD - 1, r=(wk, "hT"), w=(PK[bank],))
                            s = ctr["st"] % 2
                            ctr["st"] += 1
                            act(vst[:, s, 0:ncol], ps[bank][:, 0:ncol], AF.Copy, r=(PK[bank],), w=(("vst", s),))
                            r0 = t0 + a + jt * 128
                            dma("sp", SC_TM[r0:r0 + 128, n0:n0 + ncol], vst[:, s, 0:ncol], (("vst", s),), (), ("vst", s))
                            if not sq_["smp"]:
                                act(v32[:, s, 0:ncol], ps[bank][:, 0:ncol], AF.Copy, r=(PK[bank],), w=(("v32", s),))
                                for hh in range(ncol // 128):
                                    dst = O["nat_v"][sq_["pi"], jn, n0 // 128 + hh, a + jt * 128:a + (jt + 1) * 128, :]
                                    if C.get("DBG") != 1:
                                        dma("sp", dst, v32[:, s, hh * 128:(hh + 1) * 128], (("v32", s),), (), ("v32", s))
                    stop_at(25)
                    if not sq_["smp"]:
                        stop_at(28)
                if sq_["smp"]:
                    stop_at(26)
            p.barrier()
            stop_at(3)
            TT = TS // 128
            qh, kh = a16(0, TS), a16(TS, TS)
            vh0 = a16(2 * TS, TS).rearrange("p (n d) -> p n d", d=128)
            vh1 = a16(3 * TS, TS).rearrange("p (n d) -> p n d", d=128)
            B0 = 4 * TS
            bt = a32(B0 // 2, 896)
            kc32 = a32(B0 // 2 + 896, NPT * 128).rearrange("p (n d) -> p n d", d=128)
            vc = a16(B0 + 1792 + NPT * 256, NPT * 128).rearrange("p (n d) -> p n d", d=128)
            kcT = a16(B0 + 1792 + NPT * 384, NPT * 128)
            ost = a16(B0 + 1792 + NPT * 512, 1024).rearrange("p (s n) -> p s n", s=2)
            assert B0 + 1792 + NPT * 512 + 1024 <= 30720
            oc = 0
            for h in range(H):
                dma("sp", qh, SC_FM[h * 128:(h + 1) * 128, 0:TS], (), ("qh",), "qh")
                dma("sp", kh, SC_FM[HD + h * 128:HD + (h + 1) * 128, 0:TS], (), ("kh",), "kh")
                dma("sp", vh0, SC_TM[0:TS, h * 128:(h + 1) * 128].rearrange("(n p) d -> p n d", p=128), (), ("vh",), "vh0")
                dma("sp", vh1[:, 0:TT - 1, :], SC_TM[64:TS - 64, h * 128:(h + 1) * 128].rearrange("(n p) d -> p n d", p=128), (), ("vh",), "vh1")
                dma("sp", bt, I["nat_bt"][jn, h], (), ("bt",), "bt")
                dma("sp", kc32, I["cnk"][jn, h].rearrange("(n p) d -> p n d", p=128), (), ("kc32",), "kc32")
                dma("pool", vc, I["cnv"][jn, h].rearrange("(n p) d -> p n d", p=128), (), ("vc",), "vc")
                for t in range(NPT):
                    p.op("pe", lambda e, t=t: e.transpose(out=ps[0][:, t * 128:(t + 1) * 128], in_=kc32[:, t, :], identity=ident32[:, :]),
                         r=("kc32", "ident32"), w=(PK[0],))
                act(kcT[:, 0:NPT * 128], ps[0][:, 0:NPT * 128], AF.Copy, r=(PK[0],), w=("kcT",))
                for r in range(ROWS):
                    rs_ = min(max(r - 4, 0), ROWS - 8)
                    kts = []
                    for kt in range(4):
                        gr = rs_ + 2 * kt
                        tok = gr * 64
                        vt = vh0[:, tok // 128, :] if tok % 128 == 0 else vh1[:, (tok - 64) // 128, :]
                        d = gr - r + 7
                        kts.append(dict(kT=[kh[:, tok:tok + 128]], v=vt, bias=bt[:, d * 64:(d + 1) * 64], rk=("kh",), rv=("vh",), rb=("bt",)))
                    for t in range(NPT):
                        kts.append(dict(kT=[kcT[:, t * 128:(t + 1) * 128]], v=vc[:, t, :], rk=("kcT",), rv=("vc",)))
                    s = oc % 2
                    oc += 1
                    qtmp = a16(B0 + 1792 + NPT * 512 + 1024, 1024).rearrange("p (s n) -> p s n", s=2)
                    act(qtmp[:, s, 0:64], qh[:, r * 64:(r + 1) * 64], AF.Copy, r=("qh",), w=(("qtmp", s),))
                    attend(kts, [qtmp[:, s, 0:64]], 64, 128, ost[:, s, 0:64], (("ost", s),), (("qtmp", s),))
                    dma("sp", SC_OT[h * 128:(h + 1) * 128, r * 64:(r + 1) * 64], ost[:, s, 0:64], (("ost", s),), (), ("ost", s))
            stop_at(4)
            for sq_ in seqs[1:]:
                t0 = sq_["t0"]
                npt = PL // 128
                qP = a16(0, H * PL).rearrange("p (h t) -> p h t", h=H)
                kP = a16(H * PL, H * PL).rearrange("p (h t) -> p h t", h=H)
                vP = a16(2 * H * PL, npt * HD).rearrange("p (n c) -> p n c", n=npt)
                dma("sp", qP, SC_FM[0:HD, t0:t0 + PL].rearrange("(h p) t -> p h t", p=128), (), ("qh",), "qh")
                dma("sp", kP, SC_FM[HD:2 * HD, t0:t0 + PL].rearrange("(h p) t -> p h t", p=128), (), ("kh",), "kh")
                dma("sp", vP, SC_TM[t0:t0 + PL, 0:HD].rearrange("(n p) c -> p n c", p=128), (), ("vh",), "vh0")
                for h in range(H):
                    kts = [dict(kT=[kP[:, h, t * 128:(t + 1) * 128]], v=vP[:, t, h * 128:(h + 1) * 128], rk=("kh",), rv=("vh",)) for t in range(npt)]
                    s = oc % 2
                    oc += 1
                    attend(kts, [qP[:, h, :]], PL, 128, ost[:, s, 0:PL], (("ost", s),), ("qh",))
                    dma("sp", SC_OT[h * 128:(h + 1) * 128, t0:t0 + PL], ost[:, s, 0:PL], (("ost", s),), (), ("ost", s))
            p.barrier()
            stop_at(5)
            load_oT_and_project(sub, H, Wo)
            p.barrier()

        MIX[0] = mixer_nat

        def rope_apply(x16, psbank, perm_b, tab, a, n, m, out_ap, out_keys, xkey):
            mm(ps[psbank][0:m, 0:n], perm_b[0:m, 0:m], x16, True, True, r=(xkey, "perm"), w=(PK[psbank],))
            t1 = a32(15360, 512)
            dve(lambda e: e.tensor_tensor(out=t1[0:m, 0:n], in0=x16, in1=tab[0:m, 0, 0:n], op=ALU.mult), r=(xkey, "ropetab"), w=("ropet1",))
            t2 = a32(15872, 512)
            dve(lambda e: e.tensor_tensor(out=t2[0:m, 0:n], in0=ps[psbank][0:m, 0:n], in1=tab[0:m, 1, 0:n], op=ALU.mult), r=(PK[psbank], "ropetab"), w=("ropet2",))
            dve(lambda e: e.tensor_tensor(out=out_ap, in0=t1[0:m, 0:n], in1=t2[0:m, 0:n], op=ALU.add), r=("ropet1", "ropet2"), w=tuple(out_keys))

        def mixer_swa(l, jn, sub):
            sub_vectors(l, 0)
            Wq, Wo = I["swa_w_qkv"][jn], I["swa_w_o"][jn]
            G = SWAH // SWAKV
            NQC = SWAH * 64 // 128
            NKC = max(1, SWAKV * 64 // 128)
            KVW = SWAKV * 64
            K0 = SWAH * 64
            gq = a32(13300, 4)
            dma("sp", gq[:, 0:2], I["swa_gqk"][jn], (), ("gq",), "gq")
            dve(lambda e: e.tensor_scalar(out=gq[:, 2:3], in0=gq[:, 0:1], scalar1=64.0 ** -0.5, scalar2=None, op0=ALU.mult), r=("gq",), w=("gq",))
            permb = a16(27000, 128)
            dma("pool", permb, I["perm128"][:, :], (), ("perm",), "perm")
            tab = a32(14000, 1024).rearrange("p (c n) -> p c n", c=2)
            qst = a16(20000, 2048).rearrange("p (s n) -> p s n", s=4)
            x16 = a16(27200, 512)
            k32 = a32(11100, 512)
            vst = a16(23300, 1024).rearrange("p (s n) -> p s n", s=2)
            v32 = a32(12200, 1024).rearrange("p (s n) -> p s n", s=2)
            for sq_ in seqs:
                t0 = sq_["t0"]
                smp = sq_["smp"]
                for (a, n) in plain_blocks(sq_["L"]):
                    norm_block(sub, sq_, a, n)
                    if smp:
                        dma("sp", tab[:, :, 0:n], I["rope128"][:, :, a:a + n], (), ("ropetab",), "ropetab")
                    for ci in range(NQC + NKC):
                        isk = ci >= NQC
                        if ci % 4 == 0 or ci == NQC:
                            cbase = ci
                            ncw = min(4, (NQC if not isk else NQC + NKC) - ci) * 128
                            if isk:
                                ncw = min(ncw, KVW)
                            wv, wk = load_w(Wq, 0, KD, ci * 128, ncw)
                        m = 128 if not isk else min(128, KVW)
                        c0 = (ci - cbase) * 128
                        bank = ci % 2
                        for k in range(KD):
                            mm(ps[bank][0:m, 0:n], wv(k, c0, c0 + m), hT[:, k, 0:n], k == 0, k == KD - 1, r=(wk, "hT"), w=(PK[bank],))
                        s = ctr["st"] % 4
                        ctr["st"] += 1
                        gcol = gq[0:m, 1:2] if isk else gq[0:m, 2:3]
                        want32 = isk and not smp
                        if smp:
                            qk_norm([(ps[bank][0:m, 0:n], PK[bank], m, x16[0:m, 0:n], ("x16",), gcol, ("gq",))], [ones2_b[0:m, :]], 64, n, 2)
                            rope_apply(x16[0:m, 0:n], 3, permb, tab, a, n, m, qst[0:m, s, 0:n], (("qst", s),), "x16")
                        elif want32:
                            qk_norm([(ps[bank][0:m, 0:n], PK[bank], m, k32[0:m, 0:n], ("k32",), gcol, ("gq",))], [ones2_b[0:m, :]], 64, n, 2)
                            dve(lambda e, s=s, n=n, m=m: e.tensor_copy(out=qst[0:m, s, 0:n], in_=k32[0:m, 0:n]), r=("k32",), w=(("qst", s),))
                            kv0 = (ci - NQC) * 2

                            def dsts(jt, kv0=kv0, a=a, pi=sq_["pi"], m=m):
                                return [(O["swa_k"][pi, jn, kv0 + i, a + jt * 128:a + (jt + 1) * 128, :], i * 64, (i + 1) * 64) for i in range(m // 64)]
                            transpose_out32(k32, "k32", m, n // 128, dsts)
                        else:
                            qk_norm([(ps[bank][0:m, 0:n], PK[bank], m, qst[0:m, s, 0:n], (("qst", s),), gcol, ("gq",))], [ones2_b[0:m, :]], 64, n, 2)
                        dma("sp", SC_FM[ci * 128:ci * 128 + m, t0 + a:t0 + a + n], qst[0:m, s, 0:n], (("qst", s),), (), ("qst", s))
                    wv, wk = load_w(Wq, 0, KD, K0 + KVW, KVW)
                    for jt in range(n // 128):
                        bank = jt % 2
                        for k in range(KD):
                            mm(ps[bank][:, 0:KVW], hT[:, k, jt * 128:(jt + 1) * 128], wv(k, 0, KVW), k == 0, k == KD - 1, r=(wk, "hT"), w=(PK[bank],))
                        s = ctr["st"] % 2
                        ctr["st"] += 1
                        act(vst[:, s, 0:KVW], ps[bank][:, 0:KVW], AF.Copy, r=(PK[bank],), w=(("vst", s),))
                        r0 = t0 + a + jt * 128
                        dma("sp", SC_TM[r0:r0 + 128, 0:KVW], vst[:, s, 0:KVW], (("vst", s),), (), ("vst", s))
                        if not smp:
                            act(v32[:, s, 0:KVW], ps[bank][:, 0:KVW], AF.Copy, r=(PK[bank],), w=(("v32", s),))
                            for kv in range(SWAKV):
                                dma("sp", O["swa_v"][sq_["pi"], jn, kv, a + jt * 128:a + (jt + 1) * 128, :], v32[:, s, kv * 64:(kv + 1) * 64],
                                    (("v32", s),), (), ("v32", s))
            p.barrier()
            TB = TS // 128
            kh = a16(0, TS)
            vh = a16(TS, TS // 2).rearrange("p (n d) -> p n d", d=64)
            B0 = TS + TS // 2
            kc32 = a32(B0 // 2, NPT * 64).rearrange("p (n d) -> p n d", d=64)
            vc = a16(B0 + NPT * 128, NPT * 64).rearrange("p (n d) -> p n d", d=64)
            kcT = a16(B0 + NPT * 192, NPT * 128)
            Qb = a16(B0 + NPT * 320, 2 * G * 128).rearrange("p (s n) -> p s n", s=2)
            B1 = B0 + NPT * 320 + 2 * G * 128
            ost = a16(B1, 1024).rearrange("p (s n) -> p s n", s=2)
            mask4 = a32((B1 + 1024) // 2, 1024).rearrange("p (w n) -> p w n", w=2)
            es = a32((B1 + 1024) // 2 + 1024, SWAH)
            es4 = a32((B1 + 1024) // 2 + 1024 + SWAH, SWAH * 128).rearrange("p (h n) -> p h n", n=128)
            ones32 = a32((B1 + 1024) // 2 + 1024 + SWAH + SWAH * 128, 128)
            assert (B1 + 1024) + 2 * (1024 + SWAH + SWAH * 128 + 128) <= 30720
            for w_ in range(2):
                for i in range(4):
                    dma("sp", mask4[:, w_, i * 128:(i + 1) * 128], I["swamask"][:, w_, :], (), ("mask4",), "mask4")
            dma("sp", es[0:64, :], I["swa_sinks"][jn:jn + 1, :].to_broadcast([64, SWAH]), (), ("es",), "es")
            act(es[0:64, :], es[0:64, :], AF.Exp, r=("es",), w=("es",))
            p.op("dve", lambda e: e.memset(ones32[:], 1.0), w=("ones32",))
            for h in range(SWAH):
                dve(lambda e, h=h: e.tensor_scalar(out=es4[0:64, h, :], in0=ones32[0:64, :], scalar1=es[0:64, h:h + 1], scalar2=None, op0=ALU.mult),
                    r=("es", "ones32"), w=("es4",))
            es4f = a32((B1 + 1024) // 2 + 1024 + SWAH, SWAH * 128)
            oc = 0
            for sq_ in seqs:
                t0, Lq, smp = sq_["t0"], sq_["L"], sq_["smp"]
                nb = Lq // 128
                for g_ in range(SWAKV):
                    dma("sp", kh[0:64, 0:Lq], SC_FM[K0 + g_ * 64:K0 + (g_ + 1) * 64, t0:t0 + Lq], (), ("kh",), "kh")
                    dma("sp", vh[:, 0:nb, :], SC_TM[t0:t0 + Lq, g_ * 64:(g_ + 1) * 64].rearrange("(n p) d -> p n d", p=128), (), ("vh",), "vh0")
                    if smp:
                        dma("sp", kc32, I["csk"][jn, g_].rearrange("(n p) d -> p n d", p=128), (), ("kc32",), "kc32")
                        dma("pool", vc, I["csv"][jn, g_].rearrange("(n p) d -> p n d", p=128), (), ("vc",), "vc")
                        for t in range(NPT):
                            p.op("pe", lambda e, t=t: e.transpose(out=ps[0][0:64, t * 128:(t + 1) * 128], in_=kc32[:, t, :], identity=ident32[:, :]),
                                 r=("kc32", "ident32"), w=(PK[0],))
                        act(kcT[0:64, 0:NPT * 128], ps[0][0:64, 0:NPT * 128], AF.Copy, r=(PK[0],), w=("kcT",))
                    for b in range(nb):
                        qs = oc % 2
                        dma("sp", Qb[0:64, qs, :].rearrange("d (h t) -> d h t", t=128),
                            SC_FM[g_ * G * 64:(g_ + 1) * G * 64, t0 + b * 128:t0 + (b + 1) * 128].rearrange("(h d) t -> d h t", d=64),
                            (), (("Qb", qs),), ("Qb", qs))
                        for hh in range(G // 4):
                            kts = []
                            kbs = (b - 1, b, b + 1) if smp else tuple(range(nb))
                            for kb in kbs:
                                if kb < 0 or kb >= nb:
                                    continue
                                d_ = dict(kT=[kh[0:64, kb * 128:(kb + 1) * 128]], v=vh[:, kb, :], rk=("kh",), rv=("vh",))
                                if smp and kb != b:
                                    d_["bias"] = mask4[:, 0 if kb < b else 1, :]
                                    d_["rb"] = ("mask4",)
                                kts.append(d_)
                            if smp:
                                for t in range(NPT):
                                    kts.append(dict(kT=[kcT[0:64, t * 128:(t + 1) * 128]], v=vc[:, t, :], rk=("kcT",), rv=("vc",)))
                            s = oc % 2
                            oc += 1
                            h0 = g_ * G + hh * 4
                            attend(kts, [Qb[0:64, qs, hh * 512:(hh + 1) * 512]], 512, 64, ost[0:64, s, :], (("ost", s),), (("Qb", qs),),
                                   sink_ap=es4f[0:64, h0 * 128:(h0 + 4) * 128], sink_keys=("es4",))
                            dma("sp", SC_OT[h0 * 64:(h0 + 4) * 64, t0 + b * 128:t0 + (b + 1) * 128].rearrange("(h d) t -> d h t", d=64),
                                ost[0:64, s, :].rearrange("d (h t) -> d h t", t=128), (("ost", s),), (), ("ost", s))
            p.barrier()
            load_oT_and_project(sub, SWAH * 64 // 128, Wo)
            p.barrier()

        MIX[3] = mixer_swa

        def mixer_mla(l, jn, sub):
            sub_vectors(l, 0)
            Wd, Wuq, Wukv, Wo = I["mla_w_down"][jn], I["mla_w_uq"][jn], I["mla_w_ukv"][jn], I["mla_w_o"][jn]
            H = MLAH
            NQ, NKV = QR // 128, KVR // 128
            QN0, QR0, KN0, KR0 = 0, H * 128, H * 192, H * 320
            gq = a32(13300, 8)
            dma("sp", gq[:, 0:4], I["mla_gqk"][jn], (), ("gq",), "gq")
            dve(lambda e: e.tensor_scalar(out=gq[:, 4:6], in0=gq[:, 0:2], scalar1=192.0 ** -0.5, scalar2=None, op0=ALU.mult), r=("gq",), w=("gq",))
            ga = a32(13320, NQ + NKV)
            dma("sp", ga, I["mla_gaT"][jn], (), ("ga",), "ga")
            permb = a16(27000, 64)
            dma("pool", permb[0:64, :], I["perm64"][:, :], (), ("perm",), "perm")
            tab = a32(14000, 1024).rearrange("p (c n) -> p c n", c=2)
            qst = a16(20000, 2048).rearrange("p (s n) -> p s n", s=4)
            x16 = a16(27200, 512)
            vst = a16(23300, 1024).rearrange("p (s n) -> p s n", s=2)
            d32 = a32(0, (NQ + NKV + 1) * 512).rearrange("p (c n) -> p c n", n=512)
            dn16 = a16(2 * (NQ + NKV + 1) * 512, (NQ + NKV + 1) * 512).rearrange("p (c n) -> p c n", n=512)
            assert 3 * (NQ + NKV + 1) * 512 <= 20000
            kc32 = a32(12200, 512)

            def kv_heads(n, tcol, rope, a):
                for h in range(H):
                    wv, wk = load_w(Wukv, 0, NKV, h * 256, 256)
                    bank = h % 2
                    for c in range(NKV):
                        mm(ps[bank][:, 0:n], wv(c, 0, 128), dn16[:, NQ + c, 0:n], c == 0, c == NKV - 1, r=(wk, "dn16"), w=(PK[bank],))
                    s = ctr["st"] % 4
                    ctr["st"] += 1
                    s2 = (s + 1) % 4
                    ctr["st"] += 1
                    o1 = (x16[0:64, 0:n], ("x16",)) if rope else (qst[0:64, s2, 0:n], (("qst", s2),))
                    qk_norm([(ps[bank][:, 0:n], PK[bank], 128, qst[:, s, 0:n], (("qst", s),), gq[:, 2:3], ("gq",)),
                             (d32[0:64, NQ + NKV, 0:n], "d32", 64, o1[0], o1[1], gq[0:64, 3:4], ("gq",))],
                            [ones_b[:, :], ones_b[0:64, :]], 192, n, 2)
                    if rope:
                        rope_apply(x16[0:64, 0:n], 3, permb, tab, a, n, 64, qst[0:64, s2, 0:n], (("qst", s2),), "x16")
                    dma("sp", SC_FM[KN0 + h * 128:KN0 + (h + 1) * 128, tcol:tcol + n], qst[:, s, 0:n], (("qst", s),), (), ("qst", s))
                    dma("sp", SC_FM[KR0 + h * 64:KR0 + (h + 1) * 64, tcol:tcol + n], qst[0:64, s2, 0:n], (("qst", s2),), (), ("qst", s2))
                    for jt in range(n // 128):
                        bank2 = 4 + jt % 2
                        for c in range(NKV):
                            mm(ps[bank2][:, 0:128], dn16[:, NQ + c, jt * 128:(jt + 1) * 128], wv(c, 128, 256), c == 0, c == NKV - 1, r=(wk, "dn16"), w=(PK[bank2],))
                        sv = ctr["st"] % 2
                        ctr["st"] += 1
                        act(vst[:, sv, 0:128], ps[bank2][:, 0:128], AF.Copy, r=(PK[bank2],), w=(("vst", sv),))
                        dma("sp", SC_TM[tcol + jt * 128:tcol + (jt + 1) * 128, h * 128:(h + 1) * 128], vst[:, sv, 0:128], (("vst", sv),), (), ("vst", sv))

            for sq_ in seqs:
                t0, smp = sq_["t0"], sq_["smp"]
                for (a, n) in plain_blocks(sq_["L"]):
                    norm_block(sub, sq_, a, n)
                    if smp:
                        dma("sp", tab[0:64, :, 0:n], I["rope64"][:, :, a:a + n], (), ("ropetab",), "ropetab")
                    nch = NQ + NKV + 1
                    for c in range(nch):
                        if c % 4 == 0:
                            ncw = min(512, QR + KVR + 64 - c * 128)
                            wv, wk = load_w(Wd, 0, KD, c * 128, ncw)
                        m = 128 if c < nch - 1 else 64
                        c0 = (c % 4) * 128
                        bank = c % 2
                        for k in range(KD):
                            mm(ps[bank][0:m, 0:n], wv(k, c0, c0 + m), hT[:, k, 0:n], k == 0, k == KD - 1, r=(wk, "hT"), w=(PK[bank],))
                        act(d32[0:m, c, 0:n], ps[bank][0:m, 0:n], AF.Copy, r=(PK[bank],), w=("d32",))
                    for (c_lo, c_hi) in ((0, NQ), (NQ, NQ + NKV)):
                        sqb = a16(SQ_OFF, 1024)
                        for c in range(c_lo, c_hi):
                            s_ = c % 2
                            act(sqb[:, s_ * 512:s_ * 512 + n], d32[:, c, 0:n], AF.Square, r=("d32",), w=(("sq", s_),))
                            mm(ps[2][:, 0:n], ones_b[:, :], sqb[:, s_ * 512:s_ * 512 + n], c == c_lo, c == c_hi - 1, r=(("sq", s_), "ones"), w=(PK[2],))
                        rs = a32(RS_OFF, 512)
                        act(rs[:, 0:n], ps[2][:, 0:n], AF.Ln, r=(PK[2], "epsb"), w=("rs",), scale=1.0 / ((c_hi - c_lo) * 128), bias=epsb[:, 0:1])
                        act(rs[:, 0:n], rs[:, 0:n], AF.Exp, r=("rs",), w=("rs",), scale=-0.5)
                        for c in range(c_lo, c_hi):
                            dve(lambda e, c=c, n=n: e.scalar_tensor_tensor(out=d32[:, c, 0:n], in0=d32[:, c, 0:n], scalar=ga[:, c:c + 1], in1=rs[:, 0:n],
                                                                          op0=ALU.mult, op1=ALU.mult), r=("d32", "rs", "ga"), w=("d32",))
                    for c in range(nch):
                        m = 128 if c < nch - 1 else 64
                        act(dn16[0:m, c, 0:n], d32[0:m, c, 0:n], AF.Copy, r=("d32",), w=("dn16",))
                    if not smp:
                        for c in range(NKV):
                            transpose_out32(d32[:, NQ + c, :], "d32", 128, n // 128,
                                            lambda jt, c=c, a=a, pi=sq_["pi"]: O["mla_ckv"][pi, jn, a + jt * 128:a + (jt + 1) * 128, c * 128:(c + 1) * 128])
                        transpose_out32(d32[:, NQ + NKV, :], "d32", 64, n // 128,
                                        lambda jt, a=a, pi=sq_["pi"]: O["mla_krope"][pi, jn, a + jt * 128:a + (jt + 1) * 128, :])
                    for h in range(H):
                        wv, wk = load_w(Wuq, 0, NQ, h * 192, 192)
                        for c in range(NQ):
                            mm(ps[0][:, 0:n], wv(c, 0, 128), dn16[:, c, 0:n], c == 0, c == NQ - 1, r=(wk, "dn16"), w=(PK[0],))
                        for c in range(NQ):
                            mm(ps[1][0:64, 0:n], wv(c, 128, 192), dn16[:, c, 0:n], c == 0, c == NQ - 1, r=(wk, "dn16"), w=(PK[1],))
                        s = ctr["st"] % 4
                        ctr["st"] += 1
                        s2 = (s + 1) % 4
                        ctr["st"] += 1
                        o1 = (x16[0:64, 0:n], ("x16",)) if smp else (qst[0:64, s2, 0:n], (("qst", s2),))
                        qk_norm([(ps[0][:, 0:n], PK[0], 128, qst[:, s, 0:n], (("qst", s),), gq[:, 4:5], ("gq",)),
                                 (ps[1][0:64, 0:n], PK[1], 64, o1[0], o1[1], gq[0:64, 5:6], ("gq",))],
                                [ones_b[:, :], ones_b[0:64, :]], 192, n, 2)
                        if smp:
                            rope_apply(x16[0:64, 0:n], 3, permb, tab, a, n, 64, qst[0:64, s2, 0:n], (("qst", s2),), "x16")
                        dma("sp", SC_FM[QN0 + h * 128:QN0 + (h + 1) * 128, t0 + a:t0 + a + n], qst[:, s, 0:n], (("qst", s),), (), ("qst", s))
                        dma("sp", SC_FM[QR0 + h * 64:QR0 + (h + 1) * 64, t0 + a:t0 + a + n], qst[0:64, s2, 0:n], (("qst", s2),), (), ("qst", s2))
                    kv_heads(n, t0 + a, smp, a)
            for t in range(NPT):
                dma("sp", kc32[:, 0:KVR], I["cckv"][jn, t * 128:(t + 1) * 128, :], (), ("kc32",), "kc32")
                for c in range(NKV):
                    p.op("pe", lambda e, c=c: e.transpose(out=ps[0][:, c * 128:(c + 1) * 128], in_=kc32[:, c * 128:(c + 1) * 128], identity=ident32[:, :]),
                         r=("kc32", "ident32"), w=(PK[0],))
                for c in range(NKV):
                    act(dn16[:, NQ + c, t * 128:(t + 1) * 128], ps[0][:, c * 128:(c + 1) * 128], AF.Copy, r=(PK[0],), w=("dn16",))
                dma("sp", kc32[:, 0:64], I["ckr"][jn, t * 128:(t + 1) * 128, :], (), ("kc32",), "kc32")
                p.op("pe", lambda e: e.transpose(out=ps[1][0:64, 0:128], in_=kc32[:, 0:64], identity=ident32[:, :]), r=("kc32", "ident32"), w=(PK[1],))
                act(d32[0:64, NQ + NKV, t * 128:(t + 1) * 128], ps[1][0:64, 0:128], AF.Copy, r=(PK[1],), w=("d32",))
            kv_heads(PAST, TA, False, 0)
            p.barrier()
            TK = TS + PAST
            qn = a16(0, TS)
            qr = a16(TS, TS)
            kn = a16(2 * TS, TK)
            kr = a16(2 * TS + TK, TK)
            vh = a16(2 * TS + 2 * TK, TK).rearrange("p (n d) -> p n d", d=128)
            B1 = 2 * TS + 3 * TK
            ost = a16(B1, 1024).rearrange("p (s n) -> p s n", s=2)
            assert B1 + 1024 <= 30720
            oc = 0
            for sq_ in seqs:
                t0, Lq, smp = sq_["t0"], sq_["L"], sq_["smp"]
                for h in range(H):
                    dma("sp", qn[:, 0:Lq], SC_FM[QN0 + h * 128:QN0 + (h + 1) * 128, t0:t0 + Lq], (), ("qh",), "qh")
                    dma("sp", qr[0:64, 0:Lq], SC_FM[QR0 + h * 64:QR0 + (h + 1) * 64, t0:t0 + Lq], (), ("qh",), "qr")
                    dma("sp", kn[:, 0:Lq], SC_FM[KN0 + h * 128:KN0 + (h + 1) * 128, t0:t0 + Lq], (), ("kh",), "kh")
                    dma("sp", kr[0:64, 0:Lq], SC_FM[KR0 + h * 64:KR0 + (h + 1) * 64, t0:t0 + Lq], (), ("kh",), "kr")
                    dma("sp", vh[:, 0:Lq // 128, :], SC_TM[t0:t0 + Lq, h * 128:(h + 1) * 128].rearrange("(n p) d -> p n d", p=128), (), ("vh",), "vh0")
                    nkt = Lq // 128
                    if smp:
                        dma("sp", kn[:, Lq:Lq + PAST], SC_FM[KN0 + h * 128:KN0 + (h + 1) * 128, TA:TA + PAST], (), ("kh",), "kh")
                        dma("sp", kr[0:64, Lq:Lq + PAST], SC_FM[KR0 + h * 64:KR0 + (h + 1) * 64, TA:TA + PAST], (), ("kh",), "kr")
                        dma("sp", vh[:, Lq // 128:Lq // 128 + NPT, :], SC_TM[TA:TA + PAST, h * 128:(h + 1) * 128].rearrange("(n p) d -> p n d", p=128), (), ("vh",), "vh1")
                        nkt += NPT
                    for (qa, qn_) in plain_blocks(Lq):
                        kts = [dict(kT=[kn[:, t * 128:(t + 1) * 128], kr[0:64, t * 128:(t + 1) * 128]], v=vh[:, t, :], rk=("kh",), rv=("vh",)) for t in range(nkt)]
                        s = oc % 2
                        oc += 1
                        attend(kts, [qn[:, qa:qa + qn_], qr[0:64, qa:qa + qn_]], qn_, 128, ost[:, s, 0:qn_], (("ost", s),), ("qh",))
                        dma("sp", SC_OT[h * 128:(h + 1) * 128, t0 + qa:t0 + qa + qn_], ost[:, s, 0:qn_], (("ost", s),), (), ("ost", s))
            p.barrier()
            load_oT_and_project(sub, H, Wo)
            p.barrier()

        MIX[2] = mixer_mla

        def mixer_lru(l, jn, sub):
            sub_vectors(l, 0)
            Win, Wout = I["lru_w_in"][jn], I["lru_w_out"][jn]
            NLC = 2 * LRUB
            G0 = NLC * 128
            xst = a16(20000, 2048).rearrange("p (s n) -> p s n", s=4)
            for sq_ in seqs:
                t0 = sq_["t0"]
                for (a, n) in plain_blocks(sq_["L"]):
                    norm_block(sub, sq_, a, n)
                    for br in range(2):
                        for blk in range(LRUB):
                            wv, wk = load_w(Win, 0, KD, br * LW + blk * 168, 168)
                            for part, (c0, m) in enumerate(((0, 128), (128, 40))):
                                bank = part
                                for k in range(KD):
                                    mm(ps[bank][0:m, 0:n], wv(k, c0, c0 + m), hT[:, k, 0:n], k == 0, k == KD - 1, r=(wk, "hT"), w=(PK[bank],))
                                s = ctr["st"] % 4
                                ctr["st"] += 1
                                act(xst[0:m, s, 0:n], ps[bank][0:m, 0:n], AF.Copy, r=(PK[bank],), w=(("qst", s),))
                                row = br * G0 + (2 * blk + part) * 128
                                dma("sp", SC_FM[row:row + m, t0 + a:t0 + a + n], xst[0:m, s, 0:n], (("qst", s),), (), ("qst", s))
            p.barrier()
            LM = TS
            xb = a16(0, 2 * LM).rearrange("p (q n) -> p q n", q=2)
            xc = a16(2 * LM, 2 * LM).rearrange("p (q n) -> p q n", q=2)
            hsf = a32(2 * LM, LM)
            T0 = 6 * LM
            gw = a16(T0, 8 * 168).rearrange("p (q n) -> p q n", q=8)
            pp = a32((T0 + 1344) // 2, 2 * 13).rearrange("p (q n) -> p q n", q=2)
            sp_ = a32((T0 + 1344) // 2 + 32, 8)
            F0 = (T0 + 1344) // 2 + 64

            def f32t(i):
                return a32(F0 + i * 512, 512)
            ctmp, gr, gi, aa, a2, bx, hb, gt, gtmp = (f32t(i) for i in range(9))
            gate16 = a16(2 * (F0 + 9 * 512), 512)
            mst = a16(2 * (F0 + 9 * 512) + 512, 1024).rearrange("p (s n) -> p s n", s=2)
            assert 2 * (F0 + 9 * 512) + 1536 <= 38912
            oc = 0
            for sq_ in seqs:
                t0, Lq, smp = sq_["t0"], sq_["L"], sq_["smp"]
                sl = plain_blocks(Lq)
                for blk in range(LRUB):
                    for part, m in enumerate((128, 40)):
                        row = (2 * blk + part) * 128
                        dma("sp", xb[0:m, part, 0:Lq], SC_FM[row:row + m, t0:t0 + Lq], (), ("xb",), ("xb", part))
                    dma("sp", pp, I["lru_pp"][jn, :, 2 * blk:2 * blk + 2, :], (), ("pp",), "pp")
                    qi = 0
                    for W_ in (I["lru_w_a"], I["lru_w_i"]):
                        for d_ in range(2):
                            dma("pool", gw[:, qi, :], W_[jn, d_, blk, 0:128, :], (), ("gw",), ("gw", qi % 2))
                            dma("pool", gw[0:40, 4 + qi, :], W_[jn, d_, blk, 128:168, :], (), ("gw",), ("gw", qi % 2))
                            qi += 1
                    for part, m in enumerate((128, 40)):
                        for (ta, w_) in sl:
                            tb = ta + w_
                            act(ctmp[0:m, 0:w_], xb[0:m, part, ta:tb], AF.Identity, r=("xb", "pp"), w=("ctmp",), scale=pp[0:m, part, 2:3], bias=pp[0:m, part, 4:5])
                            for (tap, sh) in ((0, -2), (1, -1), (3, 1)):
                                lo_, hi_ = max(ta, -sh), min(tb, Lq - sh) if sh > 0 else tb
                                lo_ = max(lo_, ta)
                                if lo_ >= hi_:
                                    continue
                                dve(lambda e, m=m, part=part, tap=tap, sh=sh, lo_=lo_, hi_=hi_, ta=ta: e.scalar_tensor_tensor(
                                    out=ctmp[0:m, lo_ - ta:hi_ - ta], in0=xb[0:m, part, lo_ + sh:hi_ + sh], scalar=pp[0:m, part, tap:tap + 1],
                                    in1=ctmp[0:m, lo_ - ta:hi_ - ta], op0=ALU.mult, op1=ALU.add), r=("xb", "pp", "ctmp"), w=("ctmp",))
                            act(xc[0:m, part, ta:tb], ctmp[0:m, 0:w_], AF.Copy, r=("ctmp",), w=("xc",))
                    for part, (c0, m) in enumerate(((0, 128), (128, 40))):
                        cp_ = 2 * blk + part
                        for d_ in range(2):
                            act(sp_[0:m, d_:d_ + 1], pp[0:m, part, 9 + d_:10 + d_], AF.Exp, r=("pp",), w=("sp",), scale=-1.0)
                            dve(lambda e, m=m, d_=d_: e.tensor_scalar(out=sp_[0:m, d_:d_ + 1], in0=sp_[0:m, d_:d_ + 1], scalar1=1.0, scalar2=None, op0=ALU.add), r=("sp",), w=("sp",))
                            act(sp_[0:m, d_:d_ + 1], sp_[0:m, d_:d_ + 1], AF.Ln, r=("sp",), w=("sp",))
                            dve(lambda e, m=m, d_=d_: e.tensor_scalar(out=sp_[0:m, 2 + d_:3 + d_], in0=sp_[0:m, d_:d_ + 1], scalar1=-8.0, scalar2=None, op0=ALU.mult), r=("sp",), w=("sp",))
                            dve(lambda e, m=m, d_=d_: e.tensor_scalar(out=sp_[0:m, 4 + d_:5 + d_], in0=sp_[0:m, d_:d_ + 1], scalar1=-16.0, scalar2=None, op0=ALU.mult), r=("sp",), w=("sp",))
                        for d_ in range(2):
                            order = sl if d_ == 0 else sl[::-1]
                            for si, (ta, w_) in enumerate(order):
                                tb = ta + w_
                                for gsel, (gps, bcol) in enumerate(((0, 5 + d_), (1, 7 + d_))):
                                    qi = gsel * 2 + d_
                                    mm(ps[gps][0:m, 0:w_], gw[:, qi, c0:c0 + m], xc[:, 0, ta:tb], True, False, r=("gw", "xc"), w=(PK[gps],))
                                    mm(ps[gps][0:m, 0:w_], gw[0:40, 4 + qi, c0:c0 + m], xc[0:40, 1, ta:tb], False, True, r=("gw", "xc"), w=(PK[gps],))
                                act(gr[0:m, 0:w_], ps[0][0:m, 0:w_], AF.Sigmoid, r=(PK[0], "pp"), w=("gr",), bias=pp[0:m, part, 5 + d_:6 + d_])
                                act(gi[0:m, 0:w_], ps[1][0:m, 0:w_], AF.Sigmoid, r=(PK[1], "pp"), w=("gi",), bias=pp[0:m, part, 7 + d_:8 + d_])
                                act(aa[0:m, 0:w_], gr[0:m, 0:w_], AF.Exp, r=("gr", "sp"), w=("aa",), scale=sp_[0:m, 2 + d_:3 + d_])
                                act(a2[0:m, 0:w_], gr[0:m, 0:w_], AF.Exp, r=("gr", "sp"), w=("a2",), scale=sp_[0:m, 4 + d_:5 + d_])
                                dve(lambda e, m=m, w_=w_: e.tensor_scalar(out=a2[0:m, 0:w_], in0=a2[0:m, 0:w_], scalar1=-1.0, scalar2=1.0, op0=ALU.mult, op1=ALU.add),
                                    r=("a2",), w=("a2",))
                                act(a2[0:m, 0:w_], a2[0:m, 0:w_], AF.Sqrt, r=("a2",), w=("a2",))
                                dve(lambda e, m=m, w_=w_, part=part, ta=ta, tb=tb: e.tensor_tensor(out=bx[0:m, 0:w_], in0=gi[0:m, 0:w_], in1=xc[0:m, part, ta:tb], op=ALU.mult),
                                    r=("gi", "xc"), w=("bx",))
                                dve(lambda e, m=m, w_=w_: e.tensor_tensor(out=bx[0:m, 0:w_], in0=bx[0:m, 0:w_], in1=a2[0:m, 0:w_], op=ALU.mult), r=("bx", "a2"), w=("bx",))
                                if d_ == 0:
                                    init = (pp[0:m, part, 11:12] if smp else 0.0) if si == 0 else hsf[0:m, ta - 1:ta]
                                    dve(lambda e, m=m, w_=w_, ta=ta, tb=tb, init=init: e.tensor_tensor_scan(out=hsf[0:m, ta:tb], data0=aa[0:m, 0:w_], data1=bx[0:m, 0:w_],
                                                                                                      initial=init, op0=ALU.mult, op1=ALU.add),
                                        r=("aa", "bx", "pp", "hsf"), w=("hsf",))
                                    if (not smp) and tb == Lq:
                                        off = blk * 168 + c0
                                        dma("sp", O["lru_state"][sq_["pi"], jn, 0, off:off + m].rearrange("(p o) -> p o", o=1), hsf[0:m, Lq - 1:Lq], ("hsf",), (), "lst")
                                else:
                                    init = (pp[0:m, part, 12:13] if smp else 0.0) if si == 0 else hb[0:m, 0:1]
                                    if si > 0:
                                        dve(lambda e, m=m: e.tensor_scalar(out=gtmp[0:m, 0:1], in0=hb[0:m, 0:1], scalar1=1.0, scalar2=None, op0=ALU.mult), r=("hb",), w=("gtmp0",))
                                        init = gtmp[0:m, 0:1]
                                    dve(lambda e, m=m, w_=w_, init=init: e.tensor_tensor_scan(out=hb[0:m, 0:w_][:, ::-1], data0=aa[0:m, 0:w_][:, ::-1], data1=bx[0:m, 0:w_][:, ::-1],
                                                                                          initial=init, op0=ALU.mult, op1=ALU.add),
                                        r=("aa", "bx", "pp", "gtmp0"), w=("hb",))
                                    if (not smp) and ta == 0:
                                        off = blk * 168 + c0
                                        dma("sp", O["lru_state"][sq_["pi"], jn, 1, off:off + m].rearrange("(p o) -> p o", o=1), hb[0:m, 0:1], ("hb",), (), "lst")
                                    row = G0 + cp_ * 128
                                    dma("sp", gate16[0:m, 0:w_], SC_FM[row:row + m, t0 + ta:t0 + tb], (), ("gate16",), "gate16")
                                    dve(lambda e, m=m, w_=w_: e.tensor_tensor(out=gt[0:m, 0:w_], in0=gate16[0:m, 0:w_], in1=gate16[0:m, 0:w_], op=ALU.mult), r=("gate16",), w=("gt",))
                                    dve(lambda e, m=m, w_=w_: e.tensor_scalar(out=gt[0:m, 0:w_], in0=gt[0:m, 0:w_], scalar1=0.044715, scalar2=1.0, op0=ALU.mult, op1=ALU.add), r=("gt",), w=("gt",))
                                    dve(lambda e, m=m, w_=w_: e.tensor_tensor(out=gt[0:m, 0:w_], in0=gt[0:m, 0:w_], in1=gate16[0:m, 0:w_], op=ALU.mult), r=("gt", "gate16"), w=("gt",))
                                    act(gt[0:m, 0:w_], gt[0:m, 0:w_], AF.Sigmoid, r=("gt",), w=("gt",), scale=1.5957691216057308)
                                    dve(lambda e, m=m, w_=w_: e.tensor_tensor(out=gt[0:m, 0:w_], in0=gt[0:m, 0:w_], in1=gate16[0:m, 0:w_], op=ALU.mult), r=("gt", "gate16"), w=("gt",))
                                    dve(lambda e, m=m, w_=w_, ta=ta, tb=tb: e.tensor_tensor(out=ctmp[0:m, 0:w_], in0=hb[0:m, 0:w_], in1=hsf[0:m, ta:tb], op=ALU.add), r=("hb", "hsf"), w=("ctmp",))
                                    s = oc % 2
                                    oc += 1
                                    dve(lambda e, m=m, w_=w_, s=s: e.tensor_tensor(out=mst[0:m, s, 0:w_], in0=ctmp[0:m, 0:w_], in1=gt[0:m, 0:w_], op=ALU.mult), r=("ctmp", "gt"), w=(("ost", s),))
                                    dma("sp", SC_OT[cp_ * 128:cp_ * 128 + m, t0 + ta:t0 + tb], mst[0:m, s, 0:w_], (("ost", s),), (), ("ost", s))
            p.barrier()
            load_oT_and_project(sub, NLC, Wout, kparts=[128, 40] * LRUB)
            p.barrier()

        MIX[1] = mixer_lru

        cnt = [0, 0, 0, 0]
        try:
            for l in range(L):
                modulation(l)
                stop_at(1)
                kind = KINDS[l]
                MIX[kind](l, cnt[kind], 2 * l)
                cnt[kind] += 1
                stop_at(6)
                ffn(l, 2 * l + 1)
        except StopBuild:
            pass
        if C.get("DBGOUT"):
            p.barrier()
            dma("pool", O["dbg2"][:, :], SC_OT[128:256, 0:TS], (), (), "dbg2")
            dma("sp", O["dbg4"][:, :], xr[1][0:TS, :], (), (), "dbg4")
        p.barrier(final=True)
        print("ops per engine:", {e: len(p.q[e]) for e in ENGS}, "sems used:", len(p.semmap), flush=True)
        with nc.Block() as block:
            p.emit(block)
    return nc


def _fm(v, ):
    sh = v.shape
    return np.ascontiguousarray(np.swapaxes(v.reshape(sh[:-1] + (sh[-1] // 128, 128)), -1, -2))


def _rope_tables(n_tokens, rot_dim):
    t = np.arange(n_tokens)
    row = (t // GW).astype(np.float32)
    col = (t % GW).astype(np.float32)
    half = rot_dim // 2
    inv = (10000.0 ** (-np.arange(0, half, 2, dtype=np.float32) / half)).astype(np.float32)
    ar = row[:, None] * inv
    ac = col[:, None] * inv
    ang = np.concatenate([ar, ar, ac, ac], axis=-1)
    return np.cos(ang).astype(np.float32), np.sin(ang).astype(np.float32)


def _rope_consts(TS, rot_dim, reps):
    cos, sin = _rope_tables(TS, rot_dim)
    q = rot_dim // 4
    sign = np.concatenate([-np.ones(q), np.ones(q), -np.ones(q), np.ones(q)]).astype(np.float32)
    tab = np.stack([cos.T, (sin * sign[None, :]).T], axis=1)
    tab = np.concatenate([tab] * reps, axis=0)
    n = rot_dim * reps
    perm = np.zeros((n, n), np.float32)
    for m in range(n):
        b, i = divmod(m, rot_dim)
        hlf, ii = divmod(i, 2 * q)
        src = ii + q if ii < q else ii - q
        perm[b * rot_dim + hlf * 2 * q + src, m] = 1.0
    return np.ascontiguousarray(tab), perm


def _nat_bias_tables(rpb):
    H = rpb.shape[0]
    c = np.arange(GW)
    cstart = np.clip(c - 8, 0, GW - 16)
    cp = np.arange(GW)[:, None]
    inwin = (cp >= cstart[None, :]) & (cp < cstart[None, :] + 16)
    idx = np.clip(cp - c[None, :] + 15, 0, 30)
    B = np.where(inwin[None, None], rpb[:, :, idx], np.float32(NEGB)).astype(np.float32)
    lo = B[:, 0:14].transpose(0, 2, 1, 3)
    hi = B[:, 1:15].transpose(0, 2, 1, 3)
    return np.ascontiguousarray(np.concatenate([lo, hi], axis=1).reshape(H, 128, 14 * GW))


def prep_core(C, inp, i):
    D, PL = C["D"], C["P"]
    KINDS = C["KINDS"]
    m = {}
    m["xs"] = inp["x_sample"][i]
    m["xp"] = inp["x_prompt"][2 * i:2 * i + 2].reshape(2 * PL, D)
    m["condT"] = np.ascontiguousarray(np.stack([_fm(inp["c"][i]), _fm(inp["c_ctx"])], axis=-1))
    return m


def prep_shared(C, inp):
    D, PL, TS = C["D"], C["P"], C["TS"]
    KINDS = C["KINDS"]
    NK = [KINDS.count(k) for k in range(4)]
    m = {}
    m["w_mod"] = inp["w_mod"]
    m["b_mod"] = inp["b_mod"]
    m["bmodT"] = _fm(inp["b_mod"])
    m["nmixT"] = _fm(inp["norm_mix"])
    m["nffnT"] = _fm(inp["norm_ffn"])
    m["ffn_w_in"] = inp["ffn_w_in"]
    m["ffn_w_out"] = inp["ffn_w_out"]
    m["fcwT"] = np.ascontiguousarray(_fm(inp["ffn_conv_w"]).transpose(0, 2, 3, 1))
    m["fcbT"] = _fm(inp["ffn_conv_b"])
    m["ident"] = np.eye(128, dtype=np.float32)
    selc = np.zeros((2, 2, 128), np.float32)
    selc[0, 0, :] = 1.0
    selc[1, 1, :] = 1.0
    m["selc"] = selc
    if NK[0]:
        m["nat_w_qkv"] = inp["nat_w_qkv"]
        m["nat_w_o"] = inp["nat_w_o"]
        m["nat_gqk"] = np.ascontiguousarray(np.stack([inp["nat_q_norm"], inp["nat_k_norm"]], axis=-1))
        m["nat_bt"] = np.stack([_nat_bias_tables(inp["nat_rpb"][j]) for j in range(NK[0])])
    if NK[1]:
        LB = C["LRUB"]
        m["lru_w_in"] = inp["lru_w_in"]
        m["lru_w_out"] = inp["lru_w_out"]
        m["lru_w_a"] = inp["lru_w_a"]
        m["lru_w_i"] = inp["lru_w_i"]
    if NK[2]:
        m["mla_w_down"] = inp["mla_w_down"]
        m["mla_w_uq"] = inp["mla_w_uq"]
        m["mla_w_ukv"] = inp["mla_w_ukv"]
        m["mla_w_o"] = inp["mla_w_o"]
        m["mla_gaT"] = _fm(np.concatenate([inp["mla_q_a_norm"], inp["mla_kv_a_norm"]], axis=-1))
        gq, gk = inp["mla_q_norm"], inp["mla_k_norm"]
        pad = lambda v: np.concatenate([v, np.zeros((v.shape[0], 64), np.float32)], axis=-1)
        m["mla_gqk"] = np.ascontiguousarray(np.stack([gq[:, :128], pad(gq[:, 128:]), gk[:, :128], pad(gk[:, 128:])], axis=-1))
        m["rope64"], m["perm64"] = _rope_consts(TS, 64, 1)
    if NK[3]:
        m["swa_w_qkv"] = inp["swa_w_qkv"]
        m["swa_w_o"] = inp["swa_w_o"]
        gq, gk = inp["swa_q_norm"], inp["swa_k_norm"]
        m["swa_gqk"] = np.ascontiguousarray(np.stack([np.concatenate([gq, gq], -1), np.concatenate([gk, gk], -1)], axis=-1))
        m["swa_sinks"] = inp["swa_sinks"]
        m["rope128"], m["perm128"] = _rope_consts(TS, 64, 2)
        ii = np.arange(128)
        mk = np.zeros((128, 2, 128), np.float32)
        mk[:, 0, :] = np.where(ii[None, :] <= ii[:, None], 0.0, NEGB)
        mk[:, 1, :] = np.where(ii[:, None] <= ii[None, :], 0.0, NEGB)
        m["swamask"] = mk
    return m


def prep_core_mix(C, inp, i, m):
    KINDS = C["KINDS"]
    NK = [KINDS.count(k) for k in range(4)]
    if NK[0]:
        m["cnk"] = inp["cache_nat_k"][i]
        m["cnv"] = inp["cache_nat_v"][i]
    if NK[1]:
        LB = C["LRUB"]
        n = NK[1]

        def cp(v):
            v = v.reshape(n, LB, 168)
            out = np.zeros((n, 128, 2 * LB), np.float32)
            out[:, :, 0::2] = v[:, :, 0:128].transpose(0, 2, 1)
            out[:, 0:40, 1::2] = v[:, :, 128:168].transpose(0, 2, 1)
            return out
        cols = [cp(inp["lru_conv_w"][:, k]) for k in range(4)] + [cp(inp["lru_conv_b"])]
        cols += [cp(inp["lru_b_a"][:, 0]), cp(inp["lru_b_a"][:, 1]), cp(inp["lru_b_i"][:, 0]), cp(inp["lru_b_i"][:, 1])]
        cols += [cp(inp["lru_lambda"][:, 0]), cp(inp["lru_lambda"][:, 1])]
        cols += [cp(inp["state_lru"][i][:, 0]), cp(inp["state_lru"][i][:, 1])]
        m["lru_pp"] = np.ascontiguousarray(np.stack(cols, axis=-1))
    if NK[2]:
        m["cckv"] = inp["cache_mla_ckv"][i]
        m["ckr"] = inp["cache_mla_krope"][i]
    if NK[3]:
        m["csk"] = inp["cache_swa_k"][i]
        m["csv"] = inp["cache_swa_v"][i]
    return m


_NC_CACHE = {}


def run(C, inputs):
    key = repr(sorted(C.items()))
    if key not in _NC_CACHE:
        _NC_CACHE[key] = build(C)
    nc = _NC_CACHE[key]
    inp = {k: np.asarray(v) for k, v in inputs.items()}
    shared = prep_shared(C, inp)
    in_maps = []
    for i in range(8):
        m = dict(shared)
        m.update(prep_core(C, inp, i))
        prep_core_mix(C, inp, i, m)
        in_maps.append({k: np.ascontiguousarray(v, dtype=np.float32) for k, v in m.items()})
    res = run_bass_kernel_spmd(nc, in_maps, core_ids=list(range(8)))
    R = res.results
    if C.get("DBGOUT"):
        global DBG_R
        DBG_R = R
    D, PL, TS = C["D"], C["P"], C["TS"]
    KINDS = C["KINDS"]
    NK = [KINDS.count(k) for k in range(4)]
    cat = lambda name: np.concatenate([R[i][name] for i in range(8)], axis=0)
    yp = cat("yp").reshape(16, PL, D)
    ys = np.stack([R[i]["ys"] for i in range(8)], axis=0)
    outs = [yp, ys]
    z = lambda *s: np.zeros(s, np.float32)
    outs.append(cat("nat_k") if NK[0] else None)
    outs.append(cat("nat_v") if NK[0] else None)
    outs.append(cat("lru_state") if NK[1] else None)
    outs.append(cat("mla_ckv") if NK[2] else None)
    outs.append(cat("mla_krope") if NK[2] else None)
    outs.append(cat("swa_k") if NK[3] else None)
    outs.append(cat("swa_v") if NK[3] else None)
    return tuple(outs)


def kernel(**inputs):
    return run(default_cfg(), inputs)
```
